# Optimizing a Trainium2 kernel written in Bass

```python
import math
import jax, jax.numpy as jnp
from jax import lax
import numpy as np

D_MODEL = 1024
BATCH = 1
SEQ = 16384
DEPTH = 1
DEC_BATCH = 128
DEC_SEQ = 8
PAST_LEN = 16384
PAGE_SIZE = 128

N_HEADS = 8
KV_HEADS = 2
HEAD_DIM = D_MODEL // 16
GROUP = N_HEADS // KV_HEADS
ATT_W = N_HEADS * HEAD_DIM
KV_W = KV_HEADS * HEAD_DIM
WINDOW = 128
BLOCK = 128
N_BUCKETS = 32
MAX_DISTANCE = 128
MAX_EXACT = N_BUCKETS // 2
GLA_HEADS = 4
GLA_DK = D_MODEL // 2
GLA_DV = D_MODEL
GLA_HK = GLA_DK // GLA_HEADS
GLA_HV = GLA_DV // GLA_HEADS
GLA_RANK = 16
GATE_NORMALIZER = 16.0
GLA_CHUNK = 64
SPLITS = (ATT_W, KV_W, KV_W, ATT_W, GLA_DK, GLA_DK, GLA_DV, GLA_DV, GLA_RANK, D_MODEL, D_MODEL)
N_IN = sum(SPLITS)
SPLIT_POINTS = tuple(int(s) for s in np.cumsum(SPLITS)[:-1])
DN_ALPHA = (2.0 * DEPTH) ** 0.25
DN_BETA = (8.0 * DEPTH) ** -0.25
EPS = 1e-5

kernel_name = "hybrid_swa_gla_gated_step"


def layer_norm(x, g, b):
    xf = x.astype(jnp.float32)
    mu = jnp.mean(xf, -1, keepdims=True)
    var = jnp.mean(jnp.square(xf - mu), -1, keepdims=True)
    return ((xf - mu) * lax.rsqrt(var + EPS) * g + b).astype(x.dtype)


def t5_bucket(dist):
    n = jnp.maximum(dist, 0)
    nf = jnp.maximum(n, 1).astype(jnp.float32)
    large = MAX_EXACT + (jnp.log(nf / MAX_EXACT) / math.log(MAX_DISTANCE / MAX_EXACT)
                         * (N_BUCKETS - MAX_EXACT)).astype(jnp.int32)
    large = jnp.minimum(large, N_BUCKETS - 1)
    return jnp.where(n < MAX_EXACT, n, large)


def rel_bias(rel_table, dist):
    b = rel_table[t5_bucket(dist)].astype(jnp.float32)
    return jnp.transpose(b, (2, 0, 1)).reshape(KV_HEADS, GROUP, dist.shape[0], dist.shape[1])


def sink_softmax(logits, mask, sink):
    logits = jnp.where(mask, logits, -1e30)
    m = jnp.maximum(jnp.max(logits, -1, keepdims=True), sink)
    p = jnp.exp(logits - m)
    return p / (jnp.sum(p, -1, keepdims=True) + jnp.exp(sink - m))


def swa_prompt(q, k, v, sink, rel_table):
    B, T = q.shape[0], q.shape[1]
    nb = T // BLOCK
    qb = q.reshape(B, nb, BLOCK, KV_HEADS, GROUP, HEAD_DIM)
    kb = k.reshape(B, nb, BLOCK, KV_HEADS, HEAD_DIM)
    vb = v.reshape(B, nb, BLOCK, KV_HEADS, HEAD_DIM)
    pad = ((0, 0), (1, 0), (0, 0), (0, 0), (0, 0))
    kband = jnp.concatenate([jnp.pad(kb, pad)[:, :-1], kb], axis=2)
    vband = jnp.concatenate([jnp.pad(vb, pad)[:, :-1], vb], axis=2)
    s = jnp.einsum('bnqhgd,bnkhd->bnhgqk', qb, kband).astype(jnp.float32) * (HEAD_DIM ** -0.5)
    dist = jnp.arange(BLOCK)[:, None] + BLOCK - jnp.arange(2 * BLOCK)[None, :]
    s = s + rel_bias(rel_table, dist)
    in_win = (dist >= 0) & (dist <= WINDOW)
    has_prev = (jnp.arange(nb)[:, None] > 0) | (jnp.arange(2 * BLOCK)[None, :] >= BLOCK)
    mask = (in_win[None] & has_prev[:, None, :])[None, :, None, None]
    p = sink_softmax(s, mask, sink.astype(jnp.float32).reshape(KV_HEADS, GROUP, 1, 1))
    o = jnp.einsum('bnhgqk,bnkhd->bnqhgd', p.astype(v.dtype), vband)
    return o.reshape(B, T, ATT_W)


def swa_sample(q, k_new, v_new, cache_k, cache_v, sink, rel_table):
    B, T = q.shape[0], q.shape[1]
    kall = jnp.concatenate([cache_k.astype(k_new.dtype), k_new], axis=1)
    vall = jnp.concatenate([cache_v.astype(v_new.dtype), v_new], axis=1)
    s = jnp.einsum('bqhgd,bkhd->bhgqk', q, kall).astype(jnp.float32) * (HEAD_DIM ** -0.5)
    dist = jnp.arange(T)[:, None] + WINDOW - jnp.arange(WINDOW + T)[None, :]
    s = s + rel_bias(rel_table, dist)
    mask = (dist >= 0) & (dist <= WINDOW)
    p = sink_softmax(s, mask, sink.astype(jnp.float32).reshape(KV_HEADS, GROUP, 1, 1))
    o = jnp.einsum('bhgqk,bkhd->bqhgd', p.astype(vall.dtype), vall).reshape(B, T, ATT_W)
    return o, kall[:, T:], vall[:, T:]


def gla(q, k, v, log_a, s0):
    B, T = q.shape[0], q.shape[1]
    c = math.gcd(GLA_CHUNK, T)
    nc = T // c

    def to_chunks(a):
        return jnp.moveaxis(a.reshape(B, nc, c, *a.shape[2:]), 1, 0).astype(jnp.float32)

    causal = jnp.tril(jnp.ones((c, c), dtype=bool))

    def step(S, xs):
        qc, kc, vc, gc = xs
        b = jnp.cumsum(gc, axis=1)
        b_last = b[:, -1]
        qt = qc * jnp.exp(b)
        kt = kc * jnp.exp(-b)
        kend = kc * jnp.exp(b_last[:, None] - b)
        o_inter = jnp.einsum('bthk,bhkv->bthv', qt, S)
        A = jnp.where(causal, jnp.einsum('bthk,bshk->bhts', qt, kt), 0.0)
        o_intra = jnp.einsum('bhts,bshv->bthv', A, vc)
        S = jnp.exp(b_last)[..., None] * S + jnp.einsum('bshk,bshv->bhkv', kend, vc)
        return S, o_inter + o_intra

    S, o = lax.scan(step, s0.astype(jnp.float32), (to_chunks(q), to_chunks(k), to_chunks(v), to_chunks(log_a)))
    o = jnp.moveaxis(o, 0, 1).reshape(B, T, GLA_HEADS, GLA_HV)
    return o, S


def hybrid_layer(x, cache_k, cache_v, s0, rel_table, w_in, w_gk_up, b_gk, attn_sink,
                 gla_norm_w, w_pa, w_pg, w_o, ln_g, ln_b, prompt):
    B, T, _ = x.shape
    z = jnp.einsum('btd,dn->btn', x, w_in)
    q, k, v, g_att, gq, gk, gv, g_gla, g_lr, r_att, r_gla = jnp.split(z, SPLIT_POINTS, axis=-1)
    q = q.reshape(B, T, KV_HEADS, GROUP, HEAD_DIM)
    k = k.reshape(B, T, KV_HEADS, HEAD_DIM)
    v = v.reshape(B, T, KV_HEADS, HEAD_DIM)
    if prompt:
        o_att = swa_prompt(q, k, v, attn_sink, rel_table)
        new_k, new_v = k[:, T - WINDOW:], v[:, T - WINDOW:]
    else:
        o_att, new_k, new_v = swa_sample(q, k, v, cache_k, cache_v, attn_sink, rel_table)
    log_a = jax.nn.log_sigmoid((jnp.einsum('btr,rk->btk', g_lr, w_gk_up) + b_gk).astype(jnp.float32)) / GATE_NORMALIZER
    o_gla, S = gla((gq * (GLA_HK ** -0.5)).reshape(B, T, GLA_HEADS, GLA_HK),
                   gk.reshape(B, T, GLA_HEADS, GLA_HK),
                   gv.reshape(B, T, GLA_HEADS, GLA_HV),
                   log_a.reshape(B, T, GLA_HEADS, GLA_HK), s0)
    o_gla = o_gla * lax.rsqrt(jnp.mean(jnp.square(o_gla), -1, keepdims=True) + EPS) * gla_norm_w
    o_gla = o_gla.reshape(B, T, GLA_DV).astype(x.dtype)
    h_att = jnp.einsum('bta,ad->btd', o_att * jax.nn.silu(g_att), w_pa)
    h_gla = jnp.einsum('btv,vd->btd', o_gla * jax.nn.silu(g_gla), w_pg)
    merged = jax.nn.sigmoid(r_att) * h_att + jax.nn.sigmoid(r_gla) * h_gla
    y = jnp.einsum('btd,de->bte', merged, w_o)
    x = layer_norm(DN_ALPHA * x + y, ln_g, ln_b)
    return x, new_k, new_v, S


def setup_inputs(seed: int = 0) -> dict:
    key = jax.random.key(seed)
    ks = jax.random.split(key, 20)
    f32 = jnp.float32
    nrm = lambda k, shape, s: jax.random.normal(k, shape, f32) * s
    return {
        "x_prompt": nrm(ks[0], (BATCH, SEQ, D_MODEL), 1.0),
        "x_sample": nrm(ks[1], (DEC_BATCH, DEC_SEQ, D_MODEL), 1.0),
        "cache_k": nrm(ks[2], (DEPTH, DEC_BATCH, WINDOW, KV_HEADS, HEAD_DIM), 1.0),
        "cache_v": nrm(ks[3], (DEPTH, DEC_BATCH, WINDOW, KV_HEADS, HEAD_DIM), 1.0),
        "state_gla": nrm(ks[4], (DEPTH, DEC_BATCH, GLA_HEADS, GLA_HK, GLA_HV), 0.3),
        "rel_bias_table": nrm(ks[5], (N_BUCKETS, N_HEADS), 0.5),
        "w_in": nrm(ks[6], (DEPTH, D_MODEL, N_IN), D_MODEL ** -0.5),
        "w_gk_up": nrm(ks[7], (DEPTH, GLA_RANK, GLA_DK), GLA_RANK ** -0.5),
        "b_gk": nrm(ks[8], (DEPTH, GLA_DK), 0.1),
        "attn_sink": nrm(ks[9], (DEPTH, N_HEADS), 0.5),
        "gla_norm_w": 1.0 + nrm(ks[10], (DEPTH, GLA_HV), 0.05),
        "w_pa": nrm(ks[11], (DEPTH, ATT_W, D_MODEL), ATT_W ** -0.5 * DN_BETA),
        "w_pg": nrm(ks[12], (DEPTH, GLA_DV, D_MODEL), GLA_DV ** -0.5 * DN_BETA),
        "w_o": nrm(ks[13], (DEPTH, D_MODEL, D_MODEL), D_MODEL ** -0.5 * DN_BETA),
        "ln_g": 1.0 + nrm(ks[14], (DEPTH, D_MODEL), 0.05),
        "ln_b": nrm(ks[15], (DEPTH, D_MODEL), 0.02),
    }


def reference(x_prompt, x_sample, cache_k, cache_v, state_gla, rel_bias_table, w_in, w_gk_up, b_gk,
              attn_sink, gla_norm_w, w_pa, w_pg, w_o, ln_g, ln_b):
    xp, xs = x_prompt, x_sample
    kp, vp, sp, ksm, vsm, ssm = [], [], [], [], [], []
    for l in range(DEPTH):
        s0 = jnp.zeros((xp.shape[0], GLA_HEADS, GLA_HK, GLA_HV), jnp.float32)
        xp, nk, nv, S = hybrid_layer(xp, None, None, s0, rel_bias_table, w_in[l], w_gk_up[l], b_gk[l],
                                     attn_sink[l], gla_norm_w[l], w_pa[l], w_pg[l], w_o[l], ln_g[l], ln_b[l], True)
        kp.append(nk); vp.append(nv); sp.append(S)
        xs, nk, nv, S = hybrid_layer(xs, cache_k[l], cache_v[l], state_gla[l], rel_bias_table, w_in[l], w_gk_up[l],
                                     b_gk[l], attn_sink[l], gla_norm_w[l], w_pa[l], w_pg[l], w_o[l], ln_g[l],
                                     ln_b[l], False)
        ksm.append(nk); vsm.append(nv); ssm.append(S)
    return (xp, xs, jnp.stack(kp), jnp.stack(vp), jnp.stack(sp), jnp.stack(ksm), jnp.stack(vsm), jnp.stack(ssm))
```

```python
import math
import os
from contextlib import ExitStack

import numpy as np
import ml_dtypes

import concourse.bass as bass
import concourse.mybir as mybir
from concourse.bass_utils import run_bass_kernel_spmd

F32 = mybir.dt.float32
BF16 = mybir.dt.bfloat16
AF = mybir.ActivationFunctionType
ALU = mybir.AluOpType

NCORES = 8
D = 1024
TPC = 2048
NTILE = TPC // 128
NSEQ = 16
N_IN = 6416
HK_SCALE = 128 ** -0.5
DN_ALPHA = 2.0 ** 0.25
EPS = 1e-5
NEG = -1e30

C_Q, C_K, C_V, C_GATT, C_GQ, C_GK, C_GV, C_GGLA, C_GLR, C_RATT, C_RGLA = (
    0, 512, 640, 768, 1280, 1792, 2304, 3328, 4352, 4368, 5392)
W1_COLS = 4368

CM_P, CM_S, BLKNEG, SEQM, RST_P, RST_S, IDENT = 0, 512, 1024, 1152, 1168, 1680, 2192
NCB = 2320

ENGS = ("pe", "act", "dve", "pool", "sp")
SEM_CHUNK = 500
DMA_SEM_LIMIT = 512


class Sched:
    def __init__(self, nc, stack):
        self.nc = nc
        self.eng = {"pe": nc.tensor, "act": nc.scalar, "dve": nc.vector, "pool": nc.gpsimd, "sp": nc.sync}
        self.stack = stack
        self.sems = {e: [] for e in ENGS}
        self.count = {e: 0 for e in ENGS}
        self.known = {e: {} for e in ENGS}
        self.last_w = {}
        self.readers = {}
        self.dma_ch = {}
        self.prog = {e: [] for e in ENGS}
        self.retired = []
        self.enabled = True
        self.stop_at = float(os.environ.get("KSTOP", "99"))

    def stage(self, n):
        self.enabled = n <= self.stop_at

    def _newsem(self, name):
        return self.stack.enter_context(self.nc.semaphore(name))

    def _eng_sem(self, e, idx):
        while len(self.sems[e]) <= idx:
            self.sems[e].append(self._newsem(f"s_{e}_{len(self.sems[e])}"))
        return self.sems[e][idx]

    def _wait(self, e, tok):
        _, key, sem, val = tok
        if self.known[e].get(key, 0) >= val:
            return
        self.known[e][key] = val
        eng = self.eng[e]
        self.prog[e].append(lambda: eng.wait_ge(sem, val))

    def _deps(self, e, reads, writes):
        toks = []
        for b in reads:
            t = self.last_w.get(b)
            if t is not None:
                toks.append(t)
        for b in writes:
            t = self.last_w.get(b)
            if t is not None:
                toks.append(t)
            toks.extend(self.readers.get(b, ()))
        for t in toks:
            self._wait(e, t)

    def _record(self, tok, reads, writes):
        for b in reads:
            self.readers.setdefault(b, []).append(tok)
        for b in writes:
            self.last_w[b] = tok
            self.readers[b] = []

    def op(self, e, fn, reads=(), writes=()):
        if not self.enabled:
            return None
        self._deps(e, reads, writes)
        n = self.count[e]
        idx, val = n // SEM_CHUNK, n % SEM_CHUNK + 1
        sem = self._eng_sem(e, idx)
        self.prog[e].append(lambda: fn().then_inc(sem, 1))
        self.count[e] = n + 1
        tok = ("eng", (e, idx), sem, val)
        self._record(tok, reads, writes)
        return tok

    def dma(self, e, ch, fn, reads=(), writes=(), n=1):
        if not self.enabled:
            return None
        if ch not in self.dma_ch:
            self.dma_ch[ch] = [self._newsem(f"d_{ch}"), 0]
        sem, cnt = self.dma_ch[ch]
        if cnt > 0:
            self._wait(e, ("dma", ("dma", ch, id(sem)), sem, cnt))
        if cnt + 16 * n > DMA_SEM_LIMIT:
            self.retired.append((ch, sem, cnt))
            sem, cnt = self._newsem(f"d_{ch}_{len(self.retired)}"), 0
            self.dma_ch[ch] = [sem, cnt]
        self._deps(e, reads, writes)

        def run():
            insts = fn()
            if not isinstance(insts, (list, tuple)):
                insts = [insts]
            assert len(insts) == n, (ch, len(insts), n)
            for i in insts:
                i.then_inc(sem, 16)
        self.prog[e].append(run)
        cnt += 16 * n
        self.dma_ch[ch][1] = cnt
        tok = ("dma", ("dma", ch, id(sem)), sem, cnt)
        self._record(tok, reads, writes)
        return tok

    def wait_all(self, e):
        for ch, (sem, cnt) in list(self.dma_ch.items()) + [(c, (s_, n_)) for c, s_, n_ in self.retired]:
            if cnt:
                self._wait(e, ("dma", ("dma", ch, id(sem)), sem, cnt))
        for e2 in ENGS:
            n = self.count[e2]
            if n:
                idx, val = (n - 1) // SEM_CHUNK, (n - 1) % SEM_CHUNK + 1
                self._wait(e, ("eng", (e2, idx), self.sems[e2][idx], val))

    def barrier(self):
        for e in ENGS:
            self.wait_all(e)

    def finalize(self):
        self.barrier()
        with self.nc.Block() as block:
            reg = {"pe": block.tensor, "act": block.scalar, "dve": block.vector, "pool": block.gpsimd,
                   "sp": block.sync}
            for e in ENGS:
                def body(_eng, _l=self.prog[e]):
                    for t in _l:
                        t()
                reg[e](body)


def build_nc():
    nc = bass.Bass("TRN2", target_bir_lowering=False)

    def din(name, shape, dt=F32):
        return nc.dram_tensor(name, list(shape), dt, kind="ExternalInput")

    def dout(name, shape):
        return nc.dram_tensor(name, list(shape), F32, kind="ExternalOutput")

    xT = din("xT", [D, TPC + 128])
    xtok = din("xtok", [TPC, D])
    xpvT = din("xpvT", [D, 7 * TPC])
    xsT = din("xsT", [D, 128])
    xstok = din("xstok", [128, D])
    ckT = din("ckT", [NSEQ, 128, 128])
    ck = din("ck", [NSEQ, 128, 128])
    cv = din("cv", [NSEQ, 128, 128])
    st0 = din("st0", [NSEQ, 4, 128, 256])
    table = din("table", [32, 8])
    w_in = din("w_in", [D, N_IN])
    w_up = din("w_up", [16, 512])
    b_gk = din("b_gk", [512])
    sink = din("sink", [8])
    gnw_d = din("gnw", [256])
    w_pa = din("w_pa", [512, D])
    w_pg = din("w_pg", [D, D])
    w_o = din("w_o", [D, D])
    ln_g = din("ln_g", [D])
    ln_b = din("ln_b", [D])
    cstb_d = din("cstb", [128, NCB], BF16)
    ohr_d = din("ohr", [32, 384])
    negr_d = din("negr", [8, 384])
    mj_d = din("mj", [128, 8])
    hneg_d = din("hneg", [128, 1])

    y_o = dout("y", [TPC, D])
    ys_o = dout("ys", [128, D])
    nkp_o = dout("nkp", [128, 128])
    nvp_o = dout("nvp", [128, 128])
    nsp_o = dout("nsp", [4, 128, 256])
    nks_o = dout("nks", [NSEQ, 128, 128])
    nvs_o = dout("nvs", [NSEQ, 128, 128])
    nss_o = dout("nss", [NSEQ, 4, 128, 256])

    rs_scr = nc.dram_tensor("rs_scr", [8, 384], F32)
    att_scr = nc.dram_tensor("att_scr", [NTILE + 1, 128, 512], BF16, kind="ExternalOutput")
    gla_scr = nc.dram_tensor("gla_scr", [NTILE, 128, 1024], BF16, kind="ExternalOutput")
    gla_scr_s = nc.dram_tensor("gla_scr_s", [1, 128, 1024], BF16, kind="ExternalOutput")

    with ExitStack() as st:
        S = Sched(nc, st)

        def sbt(stack, name, shape, dt):
            return stack.enter_context(nc.sbuf_tensor("sb_" + name, list(shape), dt))

        def pstride(t):
            return t[:].ap[0][0]

        NB = 7
        banks = [st.enter_context(nc.psum_tensor(f"bk{i}", [128, 512], F32)) for i in range(NB)]
        bankT = st.enter_context(nc.psum_tensor("bkT", [128, 1024], BF16))
        bctr = [0]

        reserved = set()

        def nextbank():
            while True:
                i = bctr[0] % NB
                bctr[0] += 1
                if f"bk{i}" not in reserved:
                    return banks[i], f"bk{i}"

        cstb = sbt(st, "cstb", [128, NCB], BF16)
        onesb = sbt(st, "onesb", [128, 128], BF16)
        zerob = sbt(st, "zerob", [128, 128], BF16)
        c32 = sbt(st, "c32", [128, 4], F32)
        S.dma("sp", "cst", lambda: nc.sync.dma_start(out=cstb[:], in_=cstb_d.ap()), writes=["cstb"])
        S.op("pool", lambda: nc.gpsimd.memset(onesb[:], 1.0), writes=["onesb"])
        S.op("pool", lambda: nc.gpsimd.memset(zerob[:], 0.0), writes=["zerob"])
        S.op("pool", lambda: nc.gpsimd.memset(c32[:, 0:1], 1.0), writes=["c32a"])
        S.op("pool", lambda: nc.gpsimd.memset(c32[:, 1:2], EPS), writes=["c32b"])
        ident = cstb[:, IDENT:IDENT + 128]

        def proj_fm(dst_bank, col0, W, wkey, wcol, xb, xkey, ntok, nchunks, bkey, msize=128):
            def fn():
                last = None
                for c in range(nchunks):
                    for dc in range(8):
                        last = nc.tensor.matmul(
                            dst_bank[0:msize, col0 + c * ntok: col0 + (c + 1) * ntok],
                            lhsT=W[:, dc, wcol + c * 128: wcol + c * 128 + msize],
                            rhs=xb[:, dc, 0:ntok], start=(dc == 0), stop=(dc == 7))
                return last
            S.op("pe", fn, reads=[wkey, xkey], writes=[bkey])

        with ExitStack() as s1:
            W1 = sbt(s1, "W1", [128, 8, W1_COLS], BF16)
            stg = sbt(s1, "stg", [128, 2, 1024], F32)
            xstg = sbt(s1, "xstg", [128, 2, 1024], F32)
            wupb = sbt(s1, "wupb", [16, 512], BF16)
            negb = sbt(s1, "negb", [128, 4], F32)
            gnw = sbt(s1, "gnw", [128, 2], F32)
            esk = sbt(s1, "esk", [128, 4], F32)
            mj = sbt(s1, "mj", [128, 8], F32)
            biasP = sbt(s1, "biasP", [128, 2, 512], BF16)
            biasQ = sbt(s1, "biasQ", [128, 2, 512], BF16)
            biasC = sbt(s1, "biasC", [128, 2, 512], BF16)
            biasN = sbt(s1, "biasN", [128, 2, 512], BF16)
            biasP0 = sbt(s1, "biasP0", [128, 2, 512], BF16)
            hneg = sbt(s1, "hneg", [128, 1], F32)
            s0 = ExitStack()
            hank = sbt(s0, "hank", [128, 2, 128], F32)
            tabl = sbt(s0, "tabl", [32, 8], F32)
            ohr = sbt(s0, "ohr", [32, 384], F32)
            negr = sbt(s0, "negr", [8, 384], F32)
            rr = sbt(s0, "rr", [8, 384], F32)

            wupf = sbt(s0, "wupf", [16, 512], F32)
            S.dma("sp", "p0", lambda: nc.sync.dma_start(out=wupf[:], in_=w_up.ap()), writes=["wupf"])
            S.op("dve", lambda: nc.vector.tensor_copy(out=wupb[:], in_=wupf[:]), reads=["wupf"], writes=["wupb"])

            def ld_small():
                with nc.allow_non_contiguous_dma(reason="tiny parameter vectors"):
                    a = nc.sync.dma_start(out=negb[:], in_=b_gk.ap().rearrange("(h k) -> k h", k=128))
                    b = nc.sync.dma_start(out=gnw[:], in_=gnw_d.ap().rearrange("(c v) -> v c", v=128))
                c = nc.sync.dma_start(out=esk[0:64, :], in_=bass.AP(sink, 0, [[0, 64], [1, 4]]))
                d = nc.sync.dma_start(out=esk[64:128, :], in_=bass.AP(sink, 4, [[0, 64], [1, 4]]))
                e = nc.sync.dma_start(out=mj[:], in_=mj_d.ap())
                f = nc.sync.dma_start(out=tabl[:], in_=table.ap())
                g = nc.sync.dma_start(out=ohr[:], in_=ohr_d.ap())
                h = nc.sync.dma_start(out=negr[:], in_=negr_d.ap())
                i = nc.sync.dma_start(out=hneg[:], in_=hneg_d.ap())
                return [a, b, c, d, e, f, g, h, i]
            S.dma("sp", "p1", ld_small, writes=["negb", "gnw", "esk", "mj", "tabl", "ohr", "negr", "hneg"], n=9)
            S.op("dve", lambda: nc.vector.tensor_scalar(out=negb[:], in0=negb[:], scalar1=-1.0, scalar2=None,
                                                        op0=ALU.mult), reads=["negb"], writes=["negb"])
            S.op("act", lambda: nc.scalar.activation(out=esk[:], in_=esk[:], func=AF.Exp), reads=["esk"],
                 writes=["esk"])

            t5b, t5k = nextbank()
            S.op("pe", lambda: nc.tensor.matmul(t5b[0:8, 0:384], lhsT=tabl[:, :], rhs=ohr[:, :], start=True,
                                                stop=True), reads=["tabl", "ohr"], writes=[t5k])
            S.op("dve", lambda: nc.vector.tensor_tensor(out=rr[:], in0=t5b[0:8, 0:384], in1=negr[:], op=ALU.add),
                 reads=[t5k, "negr"], writes=["rr"])
            S.dma("sp", "rs", lambda: nc.sync.dma_start(out=rs_scr.ap(), in_=rr[:]), reads=["rr"], writes=["rs_scr"])
            hps = pstride(hank)
            for hd in range(8):
                h, g = hd // 4, hd % 4
                for half, (base, dst) in enumerate(((0, biasP), (128, biasQ))):
                    slot = (hd * 2 + half) % 2
                    S.dma("sp", f"hk{slot}",
                          lambda hd=hd, base=base, slot=slot: nc.sync.dma_start(
                              out=hank[:, slot, :], in_=bass.AP(rs_scr, hd * 384 + base, [[1, 128], [1, 128]])),
                          reads=["rs_scr"], writes=[f"hank{slot}"])
                    S.op("dve",
                         lambda dst=dst, h=h, g=g, slot=slot: nc.vector.tensor_copy(
                             out=dst[:, h, g * 128:(g + 1) * 128],
                             in_=bass.AP(hank, slot * 128 + 127, [[hps, 128], [-1, 128]])),
                         reads=[f"hank{slot}"], writes=["bias" + ("P" if half == 0 else "Q")])
            bps = pstride(biasP)
            cps = pstride(cstb)
            for h in range(2):
                S.op("dve", lambda h=h: nc.vector.tensor_copy(
                    out=biasC[:, h, :].rearrange("p (b g t) -> p b g t", b=16, g=4),
                    in_=bass.AP(biasP, h * 512, [[bps, 128], [0, 16], [128, 4], [1, 8]])),
                    reads=["biasP"], writes=["biasC"])
                S.op("dve", lambda h=h: nc.vector.tensor_tensor(
                    out=biasN[:, h, :].rearrange("p (b g t) -> p b g t", b=16, g=4),
                    in0=bass.AP(biasQ, h * 512, [[bps, 128], [8, 16], [128, 4], [1, 8]]),
                    in1=bass.AP(cstb, BLKNEG, [[cps, 128], [8, 16], [0, 4], [1, 8]]), op=ALU.add),
                    reads=["biasQ", "cstb"], writes=["biasN"])
            S.op("dve", lambda: nc.vector.tensor_scalar(out=biasP0[:], in0=biasP[:], scalar1=hneg[:, 0:1],
                                                        scalar2=None, op0=ALU.add),
                 reads=["biasP", "hneg"], writes=["biasP0"])
            S.barrier()
            s0.close()

            S.stage(2)
            w_view = w_in.ap().rearrange("(dc p) n -> p dc n", p=128)
            wblocks = [("GK", C_GK, 512), ("GV", C_GV, 1024), ("GLR", C_GLR, 16), ("Q", C_Q, 512),
                       ("KV", C_K, 256), ("GATT", C_GATT, 512), ("GQ", C_GQ, 512), ("GGLA", C_GGLA, 1024)]
            wl = [0]

            def load_w1_item(name, c0, cw, dc):
                slot = wl[0] % 2
                wl[0] += 1
                S.dma("sp", f"w{slot}",
                      lambda slot=slot, dc=dc, c0=c0, cw=cw: nc.sync.dma_start(
                          out=stg[:, slot, 0:cw], in_=w_view[:, dc, c0:c0 + cw]),
                      writes=[f"stg{slot}"])
                ceng = ("pool", "dve", "act")[wl[0] % 3]

                def fcw(slot=slot, dc=dc, c0=c0, cw=cw, ceng=ceng):
                    o = W1[:, dc, c0:c0 + cw]
                    i = stg[:, slot, 0:cw]
                    if ceng == "pool":
                        return nc.gpsimd.tensor_copy(out=o, in_=i)
                    if ceng == "dve":
                        return nc.vector.tensor_copy(out=o, in_=i)
                    return nc.scalar.copy(out=o, in_=i)
                S.op(ceng, fcw, reads=[f"stg{slot}"], writes=[f"W_{name}"])


            w1_items = [(name, c0, cw, dc) for name, c0, cw in wblocks for dc in range(8)]
            n_first = 24
            for it in w1_items[:n_first]:
                load_w1_item(*it)
            w1_rest = list(w1_items[n_first:])

            nLl = sbt(s1, "nLl", [128, 64], F32)
            edl = sbt(s1, "edl", [128, 64], F32)
            S32 = sbt(s1, "S32", [128, 1028], F32)
            Sb = sbt(s1, "Sb", [128, 1024], BF16)

            S.stage(3)
            with ExitStack() as sA:
                xA = sbt(sA, "xA", [128, 2, 8, 512], BF16)
                ltA = sbt(sA, "ltA", [128, 2048], F32)
                LtA = sbt(sA, "LtA", [128, 2048], F32)
                kendTA = sbt(sA, "kendTA", [128, 2048], BF16)
                kendA = sbt(sA, "kendA", [128, 4, 512], BF16)
                gvA = sbt(sA, "gvA", [128, 4, 1024], BF16)
                glrbA = sbt(sA, "glrbA", [16, 512], BF16)
                xpv_view = xpvT.ap().rearrange("(dc p) t -> p dc t", p=128)
                NSTEP = 7 * TPC // 512
                xl = [0]
                ops1 = pstride(onesb)
                nps_ = pstride(nLl)

                def load_xA(sidx, par):
                    for dcp in range(4):
                        slot = xl[0] % 2
                        xl[0] += 1
                        S.dma("sp", f"x{slot}", lambda slot=slot, dcp=dcp: nc.sync.dma_start(
                            out=xstg[:, slot, :].rearrange("p (a t) -> p a t", a=2),
                            in_=xpv_view[:, dcp * 2:dcp * 2 + 2, sidx * 512:(sidx + 1) * 512]),
                            writes=[f"xstg{slot}"])
                        S.op("pool", lambda slot=slot, dcp=dcp: nc.gpsimd.tensor_copy(
                            out=xA[:, par, dcp * 2:dcp * 2 + 2, :],
                            in_=xstg[:, slot, :].rearrange("p (a t) -> p a t", a=2)),
                            reads=[f"xstg{slot}"], writes=[f"xA{par}"])

                def phaseA_step(sidx, par):
                    xb = xA[:, par]
                    xkey = f"xA{par}"
                    bk, bkk = nextbank()

                    def fglr():
                        last = None
                        for dc in range(8):
                            last = nc.tensor.matmul(bk[0:16, :], lhsT=W1[:, dc, C_GLR:C_GLR + 16], rhs=xb[:, dc, :],
                                                    start=(dc == 0), stop=(dc == 7))
                        return last
                    S.op("pe", fglr, reads=["W_GLR", xkey], writes=[bkk])
                    S.op("dve", lambda: nc.vector.tensor_copy(out=glrbA[:], in_=bk[0:16, :]), reads=[bkk],
                         writes=["glrbA"])
                    for h in range(4):
                        bkh, bkhk = nextbank()
                        S.op("pe", lambda bkh=bkh, h=h: nc.tensor.matmul(
                            bkh[:, :], lhsT=wupb[:, h * 128:(h + 1) * 128], rhs=glrbA[:, :], start=True, stop=True),
                            reads=["wupb", "glrbA"], writes=[bkhk])
                        S.op("act", lambda bkh=bkh, h=h: nc.scalar.activation(
                            out=ltA[:, h * 512:(h + 1) * 512], in_=bkh[:, :], func=AF.Exp, bias=negb[:, h:h + 1],
                            scale=-1.0), reads=[bkhk, "negb"], writes=["ltA"])
                    S.op("act", lambda: nc.scalar.activation(out=ltA[:], in_=ltA[:], func=AF.Ln, bias=c32[:, 0:1],
                                                             scale=1.0), reads=["ltA", "c32a"], writes=["ltA"])

                    def fscan():
                        last = None
                        for h in range(4):
                            last = nc.vector.tensor_tensor_scan(
                                out=LtA[:, h * 512:(h + 1) * 512], data0=bass.AP(onesb, 0, [[ops1, 128], [0, 512]]),
                                data1=ltA[:, h * 512:(h + 1) * 512], initial=0.0, op0=ALU.mult, op1=ALU.add)
                        return last
                    S.op("dve", fscan, reads=["ltA", "onesb"], writes=["LtA"])
                    lpsA = pstride(LtA)
                    S.op("dve", lambda: nc.vector.tensor_scalar(
                        out=nLl[:, 0:4], in0=bass.AP(LtA, 511, [[lpsA, 128], [512, 4]]), scalar1=-1.0 / 16.0,
                        scalar2=None, op0=ALU.mult), reads=["LtA"], writes=["nLl"])
                    S.op("dve", lambda: nc.vector.scalar_tensor_tensor(
                        out=ltA[:].rearrange("p (g c) -> p g c", g=4), in0=LtA[:].rearrange("p (g c) -> p g c", g=4),
                        scalar=1.0 / 16.0, in1=bass.AP(nLl, 0, [[nps_, 128], [1, 4], [0, 512]]),
                        op0=ALU.mult, op1=ALU.add), reads=["LtA", "nLl"], writes=["ltA"])
                    S.op("act", lambda: nc.scalar.activation(out=ltA[:], in_=ltA[:], func=AF.Exp), reads=["ltA"],
                         writes=["ltA"])
                    S.op("act", lambda: nc.scalar.activation(out=edl[:, 0:4], in_=nLl[:, 0:4], func=AF.Exp),
                         reads=["nLl"], writes=["edl"])
                    for h in range(4):
                        bkh, bkhk = nextbank()

                        def fgk(bkh=bkh, h=h):
                            last = None
                            for dc in range(8):
                                last = nc.tensor.matmul(bkh[:, :], lhsT=W1[:, dc, C_GK + h * 128: C_GK + (h + 1) * 128],
                                                        rhs=xb[:, dc, :], start=(dc == 0), stop=(dc == 7))
                            return last
                        S.op("pe", fgk, reads=["W_GK", xkey], writes=[bkhk])
                        S.op("dve", lambda bkh=bkh, h=h: nc.vector.tensor_tensor(
                            out=kendTA[:, h * 512:(h + 1) * 512], in0=bkh[:, :], in1=ltA[:, h * 512:(h + 1) * 512],
                            op=ALU.mult), reads=[bkhk, "ltA"], writes=["kendTA"])
                    for pr in range(2):
                        def ftr(pr=pr):
                            last = None
                            for ii in range(2):
                                i = pr * 2 + ii
                                for h in range(4):
                                    last = nc.tensor.transpose(
                                        bankT[:, ii * 512 + h * 128: ii * 512 + (h + 1) * 128],
                                        kendTA[:, h * 512 + i * 128: h * 512 + (i + 1) * 128], ident)
                            return last
                        S.op("pe", ftr, reads=["kendTA", "cstb"], writes=["bkT"])
                        S.op("act", lambda pr=pr: nc.scalar.copy(
                            out=kendA[:, pr * 2:pr * 2 + 2, :].rearrange("p a c -> p (a c)"), in_=bankT[:, 0:1024]),
                            reads=["bkT"], writes=["kendA"])
                    for i in range(4):
                        for nb in range(2):
                            bkv, bkvk = nextbank()

                            def fgv(bkv=bkv, nb=nb, i=i):
                                last = None
                                for dc in range(8):
                                    last = nc.tensor.matmul(bkv[:, :], lhsT=xb[:, dc, i * 128:(i + 1) * 128],
                                                            rhs=W1[:, dc, C_GV + nb * 512: C_GV + (nb + 1) * 512],
                                                            start=(dc == 0), stop=(dc == 7))
                                return last
                            S.op("pe", fgv, reads=["W_GV", xkey], writes=[bkvk])
                            S.op("act", lambda bkv=bkv, nb=nb, i=i: nc.scalar.copy(
                                out=gvA[:, i, nb * 512:(nb + 1) * 512], in_=bkv[:, :]), reads=[bkvk], writes=["gvA"])
                    b2 = [nextbank(), nextbank()]

                    def fds():
                        last = None
                        for h in range(4):
                            bk_ = b2[h // 2][0]
                            for i in range(4):
                                last = nc.tensor.matmul(bk_[:, (h % 2) * 256:(h % 2) * 256 + 256],
                                                        lhsT=kendA[:, i, h * 128:(h + 1) * 128],
                                                        rhs=gvA[:, i, h * 256:(h + 1) * 256], start=(i == 0),
                                                        stop=(i == 3))
                        return last
                    S.op("pe", fds, reads=["kendA", "gvA"], writes=[b2[0][1], b2[1][1]])

                    def fu():
                        last = None
                        for h in range(4):
                            bk_ = b2[h // 2][0]
                            last = nc.vector.scalar_tensor_tensor(
                                out=S32[:, h * 256:(h + 1) * 256], in0=S32[:, h * 256:(h + 1) * 256],
                                scalar=edl[:, h:h + 1], in1=bk_[:, (h % 2) * 256:(h % 2) * 256 + 256],
                                op0=ALU.mult, op1=ALU.add)
                        return last
                    S.op("dve", fu, reads=["S32", "edl", b2[0][1], b2[1][1]], writes=["S32"])

                S.op("pool", lambda: nc.gpsimd.memset(S32[:], 0.0), writes=["S32"])
                load_xA(0, 0)
                for sidx in range(NSTEP):
                    par = sidx % 2
                    if sidx + 1 < NSTEP:
                        load_xA(sidx + 1, 1 - par)
                    for _ in range(2):
                        if w1_rest:
                            load_w1_item(*w1_rest.pop(0))
                    phaseA_step(sidx, par)
                while w1_rest:
                    load_w1_item(*w1_rest.pop(0))
                S.op("pool", lambda: nc.gpsimd.tensor_copy(out=Sb[:], in_=S32[:, 0:1024]), reads=["S32"],
                     writes=["Sb"])
                S.barrier()

            xTb = sbt(s1, "xTb", [128, 2, 8, 128], BF16)
            qT = sbt(s1, "qT", [128, 2, 2, 512], BF16)
            gattS = sbt(s1, "gattS", [128, 2, 512], BF16)
            Kr = sbt(s1, "Kr", [128, 2, 128], BF16)
            Vr = sbt(s1, "Vr", [128, 2, 128], BF16)
            PT = sbt(s1, "PT", [128, 4, 512], BF16)
            lnd = sbt(s1, "lnd", [128, 512], F32)
            actatt = sbt(s1, "actatt", [128, 2, 512], BF16)
            glrb = sbt(s1, "glrb", [16, 128], BF16)
            lt = sbt(s1, "lt", [128, 512], F32)
            Lt = sbt(s1, "Lt", [128, 512], F32)
            ekd = sbt(s1, "ekd", [128, 512], F32)
            eb = sbt(s1, "eb", [128, 512], F32)
            enb = sbt(s1, "enb", [128, 512], F32)
            qt = sbt(s1, "qt", [128, 2, 512], BF16)
            kt = sbt(s1, "kt", [128, 2, 512], BF16)
            kendT = sbt(s1, "kendT", [128, 512], BF16)
            kend = sbt(s1, "kend", [128, 2, 512], BF16)
            gv = sbt(s1, "gv", [128, 2, 1024], BF16)
            gglaS = sbt(s1, "gglaS", [128, 2, 1024], BF16)
            ATm = sbt(s1, "ATm", [128, 512], BF16)
            osq = sbt(s1, "osq", [128, 1024], BF16)
            rstd = sbt(s1, "rstd", [128, 512], F32)
            rg2 = sbt(s1, "rg2", [128, 1024], BF16)
            actgla = sbt(s1, "actgla", [128, 2, 1024], BF16)
            kvtok = sbt(s1, "kvtok", [128, 256], F32)

            S.op("pool", lambda: nc.gpsimd.memset(qT[:].rearrange("p a h c -> p (a h c)"), 0.0),
                 writes=["qT0", "qT1"])

            def load_x(src_view, par, key):
                S.dma("sp", f"x{par}", lambda: nc.sync.dma_start(
                    out=xstg[:, par, :].rearrange("p (dc t) -> p dc t", dc=8), in_=src_view),
                    writes=[f"xstg{par}"])
                S.op("pool", lambda: nc.gpsimd.tensor_copy(
                    out=xTb[:, par].rearrange("p dc t -> p (dc t)"), in_=xstg[:, par, :]),
                    reads=[f"xstg{par}"], writes=[key])

            xT_view = xT.ap().rearrange("(dc p) t -> p dc t", p=128)
            xsT_view = xsT.ap().rearrange("(dc p) t -> p dc t", p=128)

            def decay_prep(par, xkey, sample, need_q):
                xb = xTb[:, par]
                bk, bkk = nextbank()
                proj_fm(bk, 0, W1, "W_GLR", C_GLR, xb, xkey, 128, 1, bkk, msize=16)
                S.op("dve", lambda: nc.vector.tensor_copy(out=glrb[:], in_=bk[0:16, 0:128]), reads=[bkk],
                     writes=["glrb"])
                bk2, bkk2 = nextbank()

                def fn():
                    last = None
                    for h in range(4):
                        last = nc.tensor.matmul(bk2[:, h * 128:(h + 1) * 128], lhsT=wupb[:, h * 128:(h + 1) * 128],
                                                rhs=glrb[:, :], start=True, stop=True)
                    return last
                S.op("pe", fn, reads=["wupb", "glrb"], writes=[bkk2])

                def fe():
                    last = None
                    for h in range(4):
                        last = nc.scalar.activation(out=lt[:, h * 128:(h + 1) * 128], in_=bk2[:, h * 128:(h + 1) * 128],
                                                    func=AF.Exp, bias=negb[:, h:h + 1], scale=-1.0)
                    return last
                S.op("act", fe, reads=[bkk2, "negb"], writes=["lt"])
                S.op("act", lambda: nc.scalar.activation(out=lt[:], in_=lt[:], func=AF.Ln, bias=c32[:, 0:1],
                                                         scale=1.0), reads=["lt", "c32a"], writes=["lt"])
                rst = RST_S if sample else RST_P
                S.op("dve", lambda: nc.vector.tensor_tensor_scan(
                    out=Lt[:], data0=cstb[:, rst:rst + 512], data1=lt[:], initial=0.0, op0=ALU.mult, op1=ALU.add),
                    reads=["lt", "cstb"], writes=["Lt"])
                ng = 64 if sample else 4
                cl = 512 // ng
                lps = pstride(Lt)
                S.op("dve", lambda: nc.vector.tensor_scalar(
                    out=nLl[:, 0:ng], in0=bass.AP(Lt, cl - 1, [[lps, 128], [cl, ng]]), scalar1=-1.0 / 16.0,
                    scalar2=None, op0=ALU.mult), reads=["Lt"], writes=["nLl"])
                nps = pstride(nLl)
                S.op("dve", lambda: nc.vector.scalar_tensor_tensor(
                    out=ekd[:].rearrange("p (g c) -> p g c", g=ng), in0=Lt[:].rearrange("p (g c) -> p g c", g=ng),
                    scalar=1.0 / 16.0, in1=bass.AP(nLl, 0, [[nps, 128], [1, ng], [0, cl]]),
                    op0=ALU.mult, op1=ALU.add), reads=["Lt", "nLl"], writes=["ekd"])
                S.op("act", lambda: nc.scalar.activation(out=ekd[:], in_=ekd[:], func=AF.Exp), reads=["ekd"],
                     writes=["ekd"])
                S.op("act", lambda: nc.scalar.activation(out=edl[:, 0:ng], in_=nLl[:, 0:ng], func=AF.Exp),
                     reads=["nLl"], writes=["edl"])
                if need_q:
                    S.op("act", lambda: nc.scalar.activation(out=eb[:], in_=Lt[:], func=AF.Exp, scale=-1.0 / 16.0),
                         reads=["Lt"], writes=["eb"])
                    S.op("act", lambda: nc.scalar.activation(out=enb[:], in_=Lt[:], func=AF.Exp, scale=1.0 / 16.0),
                         reads=["Lt"], writes=["enb"])

            def gk_gv(par, xkey, need_q):
                xb = xTb[:, par]
                bk, bkk = nextbank()
                proj_fm(bk, 0, W1, "W_GK", C_GK, xb, xkey, 128, 4, bkk)
                S.op("dve", lambda: nc.vector.tensor_tensor(out=kendT[:], in0=bk[:, :], in1=ekd[:], op=ALU.mult),
                     reads=[bkk, "ekd"], writes=["kendT"])
                if need_q:
                    S.op("dve", lambda: nc.vector.tensor_tensor(out=kt[:, par, :], in0=bk[:, :], in1=enb[:],
                                                                op=ALU.mult),
                         reads=[bkk, "enb"], writes=[f"kt{par}"])

                def ftr():
                    last = None
                    for h in range(4):
                        last = nc.tensor.transpose(bankT[:, h * 128:(h + 1) * 128], kendT[:, h * 128:(h + 1) * 128],
                                                   ident)
                    return last
                S.op("pe", ftr, reads=["kendT", "cstb"], writes=["bkT"])
                S.op("act", lambda: nc.scalar.copy(out=kend[:, par, :], in_=bankT[:, 0:512]), reads=["bkT"],
                     writes=[f"kend{par}"])
                for nb in range(2):
                    bkv, bkvk = nextbank()

                    def fgv(bkv=bkv, nb=nb):
                        last = None
                        for dc in range(8):
                            last = nc.tensor.matmul(bkv[:, :], lhsT=xb[:, dc, :],
                                                    rhs=W1[:, dc, C_GV + nb * 512: C_GV + (nb + 1) * 512],
                                                    start=(dc == 0), stop=(dc == 7))
                        return last
                    S.op("pe", fgv, reads=["W_GV", xkey], writes=[bkvk])
                    S.op("act", lambda bkv=bkv, nb=nb: nc.scalar.copy(out=gv[:, par, nb * 512:(nb + 1) * 512],
                                                                      in_=bkv[:, :]),
                         reads=[bkvk], writes=[f"gv{par}"])

            def state_update(par, st32, stkey, kend_ap, kendkey, edl_col0, edl_step, out32=None, outkey=None):
                if out32 is None:
                    out32, outkey = st32, stkey
                b2 = [nextbank(), nextbank()]

                def fn():
                    last = None
                    for h in range(4):
                        bk_ = b2[h // 2][0]
                        last = nc.tensor.matmul(bk_[:, (h % 2) * 256:(h % 2) * 256 + 256],
                                                lhsT=kend_ap[:, h * 128:(h + 1) * 128],
                                                rhs=gv[:, par, h * 256:(h + 1) * 256], start=True, stop=True)
                    return last
                S.op("pe", fn, reads=[kendkey, f"gv{par}"], writes=[b2[0][1], b2[1][1]])

                def fu():
                    last = None
                    for h in range(4):
                        bk_ = b2[h // 2][0]
                        c = edl_col0 + h * edl_step
                        last = nc.vector.scalar_tensor_tensor(
                            out=out32[:, h * 256:(h + 1) * 256], in0=st32[:, h * 256:(h + 1) * 256],
                            scalar=edl[:, c:c + 1], in1=bk_[:, (h % 2) * 256:(h % 2) * 256 + 256],
                            op0=ALU.mult, op1=ALU.add)
                    return last
                S.op("dve", fu, reads=[stkey, "edl", b2[0][1], b2[1][1]], writes=[outkey])

            def attn_proj(par, xkey, ntv_key):
                xb = xTb[:, par]
                bk, bkk = nextbank()
                proj_fm(bk, 0, W1, "W_Q", C_Q, xb, xkey, 128, 4, bkk)
                def fq():
                    nc.scalar.activation(out=qT[0:64, par, 0, :], in_=bk[0:64, :], func=AF.Copy, scale=0.125)
                    return nc.scalar.activation(out=qT[64:128, par, 1, :], in_=bk[64:128, :], func=AF.Copy, scale=0.125)
                S.op("act", fq, reads=[bkk], writes=[f"qT{par}"])
                bk2, bkk2 = nextbank()
                proj_fm(bk2, 0, W1, "W_GATT", C_GATT, xb, xkey, 128, 4, bkk2)
                S.op("act", lambda: nc.scalar.activation(out=gattS[:, par, :], in_=bk2[:, :], func=AF.Silu),
                     reads=[bkk2], writes=[f"gattS{par}"])

            def attn_finish(par, bo, bok, bd, bdk, scr_idx, perm=False):
                def fl():
                    last = None
                    for g in range(4):
                        src = bd[:, g * 128:(g + 1) * 128]
                        dstv = lnd[:, g * 128:(g + 1) * 128]
                        if perm:
                            src = bd[:, :].rearrange("p (b g t) -> p g b t", b=16, g=4)[:, g, :, :]
                            dstv = dstv.rearrange("p (b t) -> p b t", b=16)
                        last = nc.scalar.activation(out=dstv, in_=src, func=AF.Ln, bias=esk[:, g:g + 1], scale=1.0)
                    return last
                S.op("act", fl, reads=[bdk, "esk"], writes=["lnd"])
                S.op("act", lambda: nc.scalar.activation(out=lnd[:], in_=lnd[:], func=AF.Exp, scale=-1.0),
                     reads=["lnd"], writes=["lnd"])
                S.op("pool", lambda: nc.gpsimd.tensor_tensor(out=lnd[:], in0=lnd[:], in1=gattS[:, par, :],
                                                             op=ALU.mult),
                     reads=["lnd", f"gattS{par}"], writes=["lnd"])
                def fm():
                    if not perm:
                        return nc.vector.tensor_tensor(out=actatt[:, par, :], in0=bo[:, :], in1=lnd[:], op=ALU.mult)
                    last = None
                    for g in range(4):
                        last = nc.vector.tensor_tensor(
                            out=actatt[:, par, g * 128:(g + 1) * 128].rearrange("p (b t) -> p b t", b=16),
                            in0=bo[:, :].rearrange("p (b g t) -> p g b t", b=16, g=4)[:, g, :, :],
                            in1=lnd[:, g * 128:(g + 1) * 128].rearrange("p (b t) -> p b t", b=16), op=ALU.mult)
                    return last
                S.op("dve", fm, reads=[bok, "lnd"], writes=[f"actatt{par}"])
                if not os.environ.get("KSKIP_SA"):
                  S.dma("sp", f"sa{par}", lambda: nc.sync.dma_start(out=att_scr.ap()[scr_idx], in_=actatt[:, par, :]),
                      reads=[f"actatt{par}"], writes=[f"att_scr{scr_idx}"])

            def gla_proj(par, xkey):
                xb = xTb[:, par]
                bk, bkk = nextbank()
                proj_fm(bk, 0, W1, "W_GQ", C_GQ, xb, xkey, 128, 4, bkk)
                S.op("dve", lambda: nc.vector.scalar_tensor_tensor(out=qt[:, par, :], in0=bk[:, :], scalar=HK_SCALE,
                                                                   in1=eb[:], op0=ALU.mult, op1=ALU.mult),
                     reads=[bkk, "eb"], writes=[f"qt{par}"])
                for half in range(2):
                    bk2, bkk2 = nextbank()
                    proj_fm(bk2, 0, W1, "W_GGLA", C_GGLA + half * 512, xb, xkey, 128, 4, bkk2)
                    S.op("act", lambda bk2=bk2, half=half: nc.scalar.activation(
                        out=gglaS[:, par, half * 512:(half + 1) * 512], in_=bk2[:, :], func=AF.Silu),
                        reads=[bkk2], writes=[f"gglaS{par}"])

                def fw():
                    last = None
                    v = gglaS[:, par, :].rearrange("p (h c t) -> p h c t", h=4, c=2)
                    for c in range(2):
                        last = nc.gpsimd.tensor_scalar(out=v[:, :, c, :], in0=v[:, :, c, :], scalar1=gnw[:, c:c + 1],
                                                       scalar2=None, op0=ALU.mult)
                    return last
                S.op("pool", fw, reads=[f"gglaS{par}", "gnw"], writes=[f"gglaS{par}"])

            def gla_AT(par, cm):
                bk, bkk = nextbank()

                def fn():
                    last = None
                    for h in range(4):
                        last = nc.tensor.matmul(bk[:, h * 128:(h + 1) * 128], lhsT=kt[:, par, h * 128:(h + 1) * 128],
                                                rhs=qt[:, par, h * 128:(h + 1) * 128], start=True, stop=True)
                    return last
                S.op("pe", fn, reads=[f"kt{par}", f"qt{par}"], writes=[bkk])
                S.op("dve", lambda: nc.vector.tensor_tensor(out=ATm[:], in0=bk[:, :], in1=cstb[:, cm:cm + 512],
                                                            op=ALU.mult), reads=[bkk, "cstb"], writes=["ATm"])

            def gla_finish(par, bo2, scr_idx):
                for i in range(2):
                    S.op("act", lambda i=i: nc.scalar.activation(out=osq[:, i * 512:(i + 1) * 512], in_=bo2[i][0][:, :],
                                                                 func=AF.Square),
                         reads=[bo2[i][1]], writes=["osq"])
                if scr_idx == NTILE: S.stage(6.1)
                bs, bsk = nextbank()
                ops_ = pstride(osq)

                def fs():
                    last = None
                    for c in range(2):
                        last = nc.tensor.matmul(bs[:, :], lhsT=onesb[:, :],
                                                rhs=bass.AP(osq, c * 128, [[ops_, 128], [256, 4], [1, 128]]),
                                                start=(c == 0), stop=(c == 1))
                    return last
                S.op("pe", fs, reads=["osq", "onesb"], writes=[bsk])
                if scr_idx == NTILE: S.stage(6.2)
                S.op("act", lambda: nc.scalar.activation(out=rstd[:], in_=bs[:, :], func=AF.Ln, bias=c32[:, 1:2],
                                                         scale=1.0 / 256.0), reads=[bsk, "c32b"], writes=["rstd"])
                S.op("act", lambda: nc.scalar.activation(out=rstd[:], in_=rstd[:], func=AF.Exp, scale=-0.5),
                     reads=["rstd"], writes=["rstd"])

                if scr_idx == NTILE: S.stage(6.3)

                def fr():
                    last = None
                    gvw = gglaS[:, par, :].rearrange("p (h c t) -> p h c t", h=4, c=2)
                    rv = rg2[:].rearrange("p (h c t) -> p h c t", h=4, c=2)
                    for c in range(2):
                        last = nc.gpsimd.tensor_tensor(out=rv[:, :, c, :], in0=gvw[:, :, c, :],
                                                       in1=rstd[:].rearrange("p (h t) -> p h t", h=4), op=ALU.mult)
                    return last
                S.op("pool", fr, reads=[f"gglaS{par}", "rstd"], writes=["rg2"])
                if scr_idx == NTILE: S.stage(6.4)
                for i in range(2):
                    S.op("dve", lambda i=i: nc.vector.tensor_tensor(
                        out=actgla[:, par, i * 512:(i + 1) * 512], in0=bo2[i][0][:, :],
                        in1=rg2[:, i * 512:(i + 1) * 512], op=ALU.mult),
                        reads=[bo2[i][1], "rg2"], writes=[f"actgla{par}"])
                if scr_idx == NTILE: S.stage(6.5)
                S.dma(("pool" if os.environ.get("KPOOLQ") else "sp"), (f"sa{par}" if os.environ.get("KCH") else f"sg{par}"), lambda: [
                    (nc.gpsimd if os.environ.get("KPOOLQ") else nc.sync).dma_start(out=(gla_scr_s.ap()[0] if scr_idx == NTILE else gla_scr.ap()[scr_idx])[:, i * 512:(i + 1) * 512],
                                      in_=(gv if os.environ.get("KSRC") else actgla)[:, par, i * 512:(i + 1) * 512]) for i in range(2)],
                      reads=([] if os.environ.get("KNODEP") else [f"actgla{par}"]), writes=[f"gla_scr{scr_idx}"], n=2)

            S.stage(5)
            with ExitStack() as s2:
                KcT = sbt(s2, "KcT", [128, 16, 128], BF16)
                Vc = sbt(s2, "Vc", [128, 16, 128], BF16)
                S0f = sbt(s2, "S0f", [128, 2, 1028], F32)
                S0b = sbt(s2, "S0b", [128, 2, 1024], BF16)
                kendm = sbt(s2, "kendm", [128, 2, 512], BF16)
                Snew = sbt(s2, "Snew", [128, 2, 1024], F32)

                s_par = 0
                load_x(xsT_view, s_par, "xTb0")
                s_xkey = "xTb0"
                s_xb = xTb[:, s_par]
                if os.environ.get("KSKIP_CACHE"):
                    S.enabled = False
                S.dma("sp", "w0", lambda: nc.sync.dma_start(
                    out=stg[:].rearrange("p a (b j) -> p (a b) j", j=128), in_=ckT.ap().rearrange("b p j -> p b j")),
                    writes=["stg0", "stg1"])
                S.op("pool", lambda: nc.gpsimd.tensor_copy(out=KcT[:].rearrange("p b j -> p (b j)"),
                                                           in_=stg[:].rearrange("p a c -> p (a c)")),
                     reads=["stg0", "stg1"], writes=["KcT"])
                S.dma("sp", "w0", lambda: nc.sync.dma_start(
                    out=stg[:].rearrange("p a (b j) -> p (a b) j", j=128), in_=cv.ap().rearrange("b j c -> j b c")),
                    writes=["stg0", "stg1"])
                S.op("pool", lambda: nc.gpsimd.tensor_copy(out=Vc[:].rearrange("p b j -> p (b j)"),
                                                           in_=stg[:].rearrange("p a c -> p (a c)")),
                     reads=["stg0", "stg1"], writes=["Vc"])
                if os.environ.get("KSKIP_CACHE"):
                    S.enabled = True
                if not os.environ.get("KSKIP_CP"):
                  S.dma("sp", "cpk", lambda: nc.sync.dma_start(out=nks_o.ap()[:, 0:120, :], in_=ck.ap()[:, 8:128, :]),
                      writes=["nks_a"])
                if not os.environ.get("KSKIP_CP"):
                  S.dma("sp", "cpv", lambda: nc.sync.dma_start(out=nvs_o.ap()[:, 0:120, :], in_=cv.ap()[:, 8:128, :]),
                      writes=["nvs_a"])

                S.stage(5.2)
                attn_proj(s_par, s_xkey, None)
                s_bk, s_bkk = nextbank()
                proj_fm(s_bk, 0, W1, "W_KV", C_K, s_xb, s_xkey, 128, 1, s_bkk)
                S.op("act", lambda: nc.scalar.copy(out=Kr[:, 0, :], in_=s_bk[:, 0:128]), reads=[s_bkk], writes=["Kr0"])
                s_bk2, s_bkk2 = nextbank()

                def fkv():
                    last = None
                    for dc in range(8):
                        last = nc.tensor.matmul(s_bk2[:, 0:256], lhsT=s_xb[:, dc, :], rhs=W1[:, dc, C_K:C_K + 256],
                                                start=(dc == 0), stop=(dc == 7))
                    return last
                S.op("pe", fkv, reads=["W_KV", s_xkey], writes=[s_bkk2])
                S.op("act", lambda: nc.scalar.copy(out=kvtok[:], in_=s_bk2[:, 0:256]), reads=[s_bkk2], writes=["kvtok"])
                S.op("dve", lambda: nc.vector.tensor_copy(out=Vr[:, 0, :], in_=s_bk2[:, 128:256]), reads=[s_bkk2],
                     writes=["Vr0"])

                def st_newkv():
                    l = []
                    for b in range(NSEQ):
                        l.append(nc.sync.dma_start(out=nks_o.ap()[b, 120:128, :], in_=kvtok[b * 8:(b + 1) * 8, 0:128]))
                        l.append(nc.sync.dma_start(out=nvs_o.ap()[b, 120:128, :], in_=kvtok[b * 8:(b + 1) * 8, 128:256]))
                    return l
                S.dma("sp", "nkv", st_newkv, reads=["kvtok"], writes=["nks_b"], n=2 * NSEQ)

                S.stage(5.3)
                scb = []
                for h in range(2):
                    hp = slice(h * 64, (h + 1) * 64)
                    bkn, bknk = nextbank()

                    def fsn(bkn=bkn, h=h, hp=hp):
                        nc.tensor.matmul(bkn[:, :], lhsT=ident, rhs=biasN[:, h, :], start=True, stop=False)
                        return nc.tensor.matmul(bkn[:, :], lhsT=Kr[:, 0, :],
                                                rhs=qT[:, s_par, h, :].rearrange("p (g b t) -> p b g t", g=4, b=16),
                                                start=False, stop=True)
                    S.op("pe", fsn, reads=["cstb", "biasN", "Kr0", f"qT{s_par}"], writes=[bknk])
                    S.op("act", lambda bkn=bkn, h=h: nc.scalar.activation(out=PT[:, h * 2 + 1, :], in_=bkn[:, :],
                                                                          func=AF.Exp),
                         reads=[bknk], writes=[f"PT{h * 2 + 1}"])
                    bkc, bkck = nextbank()

                    def fsc(bkc=bkc, h=h, hp=hp):
                        last = nc.tensor.matmul(bkc[:, :], lhsT=ident, rhs=biasC[:, h, :], start=True, stop=False)
                        qv = qT[:, s_par, h, :].rearrange("p (g b t) -> p g b t", g=4, b=16)
                        for b in range(NSEQ):
                            last = nc.tensor.matmul(bkc[:, b * 32:(b + 1) * 32], lhsT=KcT[:, b, :], rhs=qv[:, :, b, :],
                                                    start=False, stop=(b == NSEQ - 1))
                        return last
                    S.op("pe", fsc, reads=["cstb", "biasC", "KcT", f"qT{s_par}"], writes=[bkck])
                    S.op("act", lambda bkc=bkc, h=h: nc.scalar.activation(out=PT[:, h * 2, :], in_=bkc[:, :],
                                                                          func=AF.Exp),
                         reads=[bkck], writes=[f"PT{h * 2}"])
                s_bo, s_bok = nextbank()
                s_bd, s_bdk = nextbank()

                def fpv_s(dst, use_v):
                    last = None
                    for h in range(2):
                        hp = slice(h * 64, (h + 1) * 64)
                        lw = Vr[:, 0, hp] if use_v else onesb[:, 0:64]
                        last = nc.tensor.matmul(dst[hp, :], lhsT=lw, rhs=PT[:, h * 2 + 1, :], start=True, stop=False)
                        for b in range(NSEQ):
                            lw = Vc[:, b, hp] if use_v else onesb[:, 0:64]
                            last = nc.tensor.matmul(dst[hp, b * 32:(b + 1) * 32], lhsT=lw,
                                                    rhs=PT[:, h * 2, b * 32:(b + 1) * 32], start=False,
                                                    stop=(b == NSEQ - 1))
                    return last
                S.op("pe", lambda: fpv_s(s_bo, True), reads=["Vr0", "Vc", "PT0", "PT1", "PT2", "PT3"], writes=[s_bok])
                S.op("pe", lambda: fpv_s(s_bd, False), reads=["onesb", "PT0", "PT1", "PT2", "PT3"], writes=[s_bdk])
                attn_finish(s_par, s_bo, s_bok, s_bd, s_bdk, NTILE, perm=True)

                S.stage(5.4)
                decay_prep(s_par, s_xkey, True, True)
                gk_gv(s_par, s_xkey, True)
                gla_proj(s_par, s_xkey)
                gla_AT(s_par, CM_S)
                S.stage(5.5)
                s_bo2 = [nextbank(), nextbank()]
                reserved.update((s_bo2[0][1], s_bo2[1][1]))

                def fzero():
                    last = None
                    for i in range(2):
                        last = nc.tensor.matmul(s_bo2[i][0][:, :], lhsT=zerob[:, :], rhs=cstb[:, CM_P:CM_P + 512],
                                                start=True, stop=False)
                    for h in range(4):
                        for c in range(2):
                            blk = (h * 2 + c) % 4
                            last = nc.tensor.matmul(s_bo2[h // 2][0][:, blk * 128:(blk + 1) * 128],
                                                    lhsT=gv[:, s_par, h * 256 + c * 128: h * 256 + (c + 1) * 128],
                                                    rhs=ATm[:, h * 128:(h + 1) * 128], start=False, stop=False)
                    return last
                S.op("pe", fzero, reads=["zerob", "cstb", f"gv{s_par}", "ATm"], writes=[s_bo2[0][1], s_bo2[1][1]])
                sps = pstride(cstb)
                for b in range(NSEQ):
                    sl = b % 2
                    S.dma("sp", f"s0{sl}", lambda b=b, sl=sl: nc.sync.dma_start(
                        out=S0f[:, sl, 0:1024].rearrange("p (h v) -> p h v", h=4),
                        in_=st0.ap()[b].rearrange("h k v -> k h v")), writes=[f"S0f{sl}"])
                    S.op("act", lambda sl=sl: nc.scalar.copy(out=S0b[:, sl, :], in_=S0f[:, sl, 0:1024]),
                         reads=[f"S0f{sl}"], writes=[f"S0b{sl}"])

                    def fin(b=b, sl=sl):
                        last = None
                        for h in range(4):
                            for c in range(2):
                                blk = (h * 2 + c) % 4
                                last = nc.tensor.matmul(
                                    s_bo2[h // 2][0][:, blk * 128 + b * 8: blk * 128 + (b + 1) * 8],
                                    lhsT=S0b[:, sl, h * 256 + c * 128: h * 256 + (c + 1) * 128],
                                    rhs=qt[:, s_par, h * 128 + b * 8: h * 128 + (b + 1) * 8], start=False,
                                    stop=(b == NSEQ - 1 and h % 2 == 1 and c == 1))
                        return last
                    S.op("pe", fin, reads=[f"S0b{sl}", f"qt{s_par}"], writes=[s_bo2[0][1], s_bo2[1][1]])
                    S.op("dve", lambda b=b, sl=sl: nc.vector.tensor_scalar(
                        out=kendm[:, sl, :], in0=kend[:, s_par, :], scalar1=cstb[:, SEQM + b:SEQM + b + 1], scalar2=None,
                        op0=ALU.mult), reads=[f"kend{s_par}", "cstb"], writes=[f"kendm{sl}"])
                    state_update(s_par, S0f[:, sl, :], f"S0f{sl}", kendm[:, sl, :], f"kendm{sl}", b, 16,
                                 out32=Snew[:, sl, :], outkey=f"Snew{sl}")
                    S.dma("sp", f"so{sl}", lambda b=b, sl=sl: nc.sync.dma_start(
                        out=nss_o.ap()[b].rearrange("h k v -> k h v"),
                        in_=Snew[:, sl, :].rearrange("p (h v) -> p h v", h=4)),
                        reads=[f"Snew{sl}"], writes=[f"nss{b}"])
                S.stage(5.6)
                reserved.clear()
                gla_finish(s_par, s_bo2, NTILE)

                S.barrier()

            S.stage(7)
            load_x(xT_view[:, :, 0:128], 1, "xTb1")

            def kv_proj(par, xkey, slot, last_tile):
                xb = xTb[:, par]
                bk, bkk = nextbank()
                proj_fm(bk, 0, W1, "W_KV", C_K, xb, xkey, 128, 1, bkk)
                S.op("act", lambda: nc.scalar.copy(out=Kr[:, slot, :], in_=bk[:, 0:128]), reads=[bkk],
                     writes=[f"Kr{slot}"])
                bk2, bkk2 = nextbank()

                def fkv():
                    last = None
                    for dc in range(8):
                        last = nc.tensor.matmul(bk2[:, 0:256], lhsT=xb[:, dc, :], rhs=W1[:, dc, C_K:C_K + 256],
                                                start=(dc == 0), stop=(dc == 7))
                    return last
                S.op("pe", fkv, reads=["W_KV", xkey], writes=[bkk2])
                S.op("dve", lambda: nc.vector.tensor_copy(out=Vr[:, slot, :], in_=bk2[:, 128:256]), reads=[bkk2],
                     writes=[f"Vr{slot}"])
                if last_tile and not os.environ.get("KSKIP_NKVP"):
                    S.op("dve", lambda: nc.vector.tensor_copy(out=kvtok[:], in_=bk2[:, 0:256]), reads=[bkk2],
                         writes=["kvtok"])
                    def st_pkv():
                        l = []
                        for b in range(16):
                            l.append(nc.sync.dma_start(out=nkp_o.ap()[b * 8:(b + 1) * 8, :],
                                                       in_=kvtok[b * 8:(b + 1) * 8, 0:128]))
                            l.append(nc.sync.dma_start(out=nvp_o.ap()[b * 8:(b + 1) * 8, :],
                                                       in_=kvtok[b * 8:(b + 1) * 8, 128:256]))
                        return l
                    if not os.environ.get("KSKIP_NKVP2"):
                        S.dma("sp", "nkvp", st_pkv, reads=["kvtok"], writes=["nkp"], n=32)

            kv_proj(1, "xTb1", 0, False)
            load_x(xT_view[:, :, 128:256], 0, "xTb0")
            for j in range(NTILE):
                par = j % 2
                xkey = f"xTb{par}"
                sp_, sc_ = j % 2, (j + 1) % 2
                if j + 1 < NTILE:
                    load_x(xT_view[:, :, 128 + (j + 1) * 128: 256 + (j + 1) * 128], 1 - par, f"xTb{1 - par}")
                if j == 0: S.stage(7.1)
                if j >= 1: S.stage(7.9 + j * 0.002)
                attn_proj(par, xkey, None)
                kv_proj(par, xkey, sc_, j == NTILE - 1)
                if j == 0: S.stage(7.2)
                for h in range(2):
                    hp = slice(h * 64, (h + 1) * 64)
                    for half, (slot, bt) in enumerate(((sp_, biasP0 if j == 0 else biasP), (sc_, biasQ))):
                        bks, bksk = nextbank()

                        def fsc(bks=bks, h=h, hp=hp, slot=slot, bt=bt, par=par):
                            nc.tensor.matmul(bks[:, :], lhsT=ident, rhs=bt[:, h, :], start=True, stop=False)
                            return nc.tensor.matmul(bks[:, :], lhsT=Kr[:, slot, :], rhs=qT[:, par, h, :], start=False,
                                                    stop=True)
                        S.op("pe", fsc, reads=["cstb", "biasP", "biasP0", "biasQ", f"Kr{slot}", f"qT{par}"], writes=[bksk])
                        S.op("act", lambda bks=bks, h=h, half=half: nc.scalar.activation(
                            out=PT[:, h * 2 + half, :], in_=bks[:, :], func=AF.Exp),
                            reads=[bksk], writes=[f"PT{h * 2 + half}"])
                if j == 0: S.stage(7.3)
                bo, bok = nextbank()
                bd, bdk = nextbank()

                def fpv(dst, use_v, sp_=sp_, sc_=sc_):
                    last = None
                    for h in range(2):
                        hp = slice(h * 64, (h + 1) * 64)
                        for half, slot in enumerate((sp_, sc_)):
                            lw = Vr[:, slot, hp] if use_v else onesb[:, 0:64]
                            last = nc.tensor.matmul(dst[hp, :], lhsT=lw, rhs=PT[:, h * 2 + half, :],
                                                    start=(half == 0), stop=(half == 1))
                    return last
                S.op("pe", lambda bo=bo, fpv=fpv: fpv(bo, True),
                     reads=[f"Vr{sp_}", f"Vr{sc_}", "PT0", "PT1", "PT2", "PT3"], writes=[bok])
                S.op("pe", lambda bd=bd, fpv=fpv: fpv(bd, False), reads=["onesb", "PT0", "PT1", "PT2", "PT3"],
                     writes=[bdk])
                if j == 0: S.stage(7.4)
                attn_finish(par, bo, bok, bd, bdk, j)

                if j == 0: S.stage(7.5)
                decay_prep(par, xkey, False, True)
                gk_gv(par, xkey, True)
                gla_proj(par, xkey)
                gla_AT(par, CM_P)
                if j == 0: S.stage(7.6)
                bo2 = [nextbank(), nextbank()]

                def fo(bo2=bo2, par=par):
                    last = None
                    for h in range(4):
                        for c in range(2):
                            blk = (h * 2 + c) % 4
                            dst = bo2[h // 2][0][:, blk * 128:(blk + 1) * 128]
                            nc.tensor.matmul(dst, lhsT=Sb[:, h * 256 + c * 128: h * 256 + (c + 1) * 128],
                                             rhs=qt[:, par, h * 128:(h + 1) * 128], start=True, stop=False)
                            last = nc.tensor.matmul(dst, lhsT=gv[:, par, h * 256 + c * 128: h * 256 + (c + 1) * 128],
                                                    rhs=ATm[:, h * 128:(h + 1) * 128], start=False, stop=True)
                    return last
                S.op("pe", fo, reads=["Sb", f"qt{par}", f"gv{par}", "ATm"], writes=[bo2[0][1], bo2[1][1]])
                if j == 0: S.stage(7.7)
                state_update(par, S32, "S32", kend[:, par, :], f"kend{par}", 0, 1)
                S.op("act", lambda: nc.scalar.copy(out=Sb[:], in_=S32[:, 0:1024]), reads=["S32"],
                     writes=["Sb"])
                if j == 0: S.stage(7.8)
                gla_finish(par, bo2, j)
            S.stage(7.95)
            S.dma("sp", "nsp", lambda: nc.sync.dma_start(
                out=nsp_o.ap().rearrange("h k v -> k h v"), in_=S32[:, 0:1024].rearrange("p (h v) -> p h v", h=4)),
                reads=["S32"], writes=["nsp"])
            S.barrier()

        S.stage(8)
        with ExitStack() as s3:
            Wr = sbt(s3, "Wr", [128, 8, 2048], BF16)
            Wpa = sbt(s3, "Wpa", [128, 4, 1024], BF16)
            Wpg = sbt(s3, "Wpg", [128, 8, 1024], BF16)
            Wo = sbt(s3, "Wo", [128, 8, 1024], BF16)
            stg2 = sbt(s3, "stg2", [128, 2, 1024], F32)
            lng = sbt(s3, "lng", [128, 1024], F32)
            lnb = sbt(s3, "lnb", [128, 1024], F32)
            x2 = sbt(s3, "x2", [128, 2, 8, 512], BF16)
            att2 = sbt(s3, "att2", [128, 2, 4, 512], BF16)
            gla2 = sbt(s3, "gla2", [128, 2, 4, 1024], BF16)
            sa = sbt(s3, "sa", [128, 2, 512], F32)
            sg = sbt(s3, "sg", [128, 2, 512], F32)
            merged = sbt(s3, "merged", [128, 2, 8, 512], BF16)
            xt2 = sbt(s3, "xt2", [128, 2, 1024], F32)
            bnst = sbt(s3, "bnst", [128, 2, 6], F32)
            bnag = sbt(s3, "bnag", [128, 8], F32)

            S.dma("sp", "lnp", lambda: [nc.sync.dma_start(out=lng[:], in_=bass.AP(ln_g, 0, [[0, 128], [1, 1024]])),
                                        nc.sync.dma_start(out=lnb[:], in_=bass.AP(ln_b, 0, [[0, 128], [1, 1024]]))],
                  writes=["lng", "lnb"], n=2)
            wl2 = [0]

            def load_w(src_view, ndc, ncols, dst, dkey, col0=0):
                for dc in range(ndc):
                    for cb in range(0, ncols, 1024):
                        cw = min(1024, ncols - cb)
                        slot = wl2[0] % 2
                        wl2[0] += 1
                        S.dma("sp", f"v{slot}", lambda slot=slot, dc=dc, cb=cb, cw=cw: nc.sync.dma_start(
                            out=stg2[:, slot, 0:cw], in_=src_view[:, dc, col0 + cb: col0 + cb + cw]),
                            writes=[f"stg2{slot}"])
                        eng = ("pool", "dve", "act")[wl2[0] % 3]

                        def fc(slot=slot, dc=dc, cb=cb, cw=cw, eng=eng):
                            o = dst[:, dc, cb:cb + cw]
                            i = stg2[:, slot, 0:cw]
                            if eng == "pool":
                                return nc.gpsimd.tensor_copy(out=o, in_=i)
                            if eng == "dve":
                                return nc.vector.tensor_copy(out=o, in_=i)
                            return nc.scalar.copy(out=o, in_=i)
                        S.op(eng, fc, reads=[f"stg2{slot}"], writes=[dkey])
            w_view = w_in.ap().rearrange("(dc p) n -> p dc n", p=128)
            load_w(w_view, 8, 2048, Wr, "Wr", col0=C_RATT)
            load_w(w_pa.ap().rearrange("(g p) n -> p g n", p=128), 4, 1024, Wpa, "Wpa")
            load_w(w_pg.ap().rearrange("(c p) n -> p c n", p=128), 8, 1024, Wpg, "Wpg")
            load_w(w_o.ap().rearrange("(c p) n -> p c n", p=128), 8, 1024, Wo, "Wo")

            xT_view = xT.ap().rearrange("(dc p) t -> p dc t", p=128)
            xsT_view = xsT.ap().rearrange("(dc p) t -> p dc t", p=128)
            groups = [(s, 4) for s in range(4)] + [(4, 1)]

            def load_group(gi):
                s, nt = groups[gi]
                par = gi % 2
                T = nt * 128
                for dcp in range(4):
                    slot = wl2[0] % 2
                    wl2[0] += 1
                    src = (xT_view[:, dcp * 2:dcp * 2 + 2, 128 + s * 512: 128 + s * 512 + T] if nt == 4
                           else xsT_view[:, dcp * 2:dcp * 2 + 2, :])
                    S.dma("sp", f"v{slot}", lambda slot=slot, src=src, T=T: nc.sync.dma_start(
                        out=stg2[:, slot, 0:2 * T].rearrange("p (a t) -> p a t", a=2), in_=src),
                        writes=[f"stg2{slot}"])
                    S.op("pool", lambda slot=slot, dcp=dcp, T=T, par=par: nc.gpsimd.tensor_copy(
                        out=x2[:, par, dcp * 2:dcp * 2 + 2, 0:T],
                        in_=stg2[:, slot, 0:2 * T].rearrange("p (a t) -> p a t", a=2)),
                        reads=[f"stg2{slot}"], writes=[f"x2{par}"])
                t0 = s * 4
                S.dma("sp", f"la{par}", lambda: [
                    nc.sync.dma_start(out=att2[:, par, 0:nt, :], in_=att_scr.ap()[t0:t0 + nt].rearrange("n p c -> p n c")),
                    nc.sync.dma_start(out=gla2[:, par, 0:nt, :], in_=(gla_scr_s.ap() if t0 == NTILE else gla_scr.ap()[t0:t0 + nt]).rearrange("n p c -> p n c"))],
                    reads=[f"att_scr{t0 + i}" for i in range(nt)] + [f"gla_scr{t0 + i}" for i in range(nt)],
                    writes=[f"att2{par}", f"gla2{par}"], n=2)

            a2s = pstride(att2)
            g2s = pstride(gla2)
            load_group(0)
            xtl = [0]
            for gi, (s, nt) in enumerate(groups):
                par = gi % 2
                T = nt * 128
                if gi + 1 < len(groups):
                    load_group(gi + 1)
                for dc in range(8):
                    dsl = slice(dc * 128, (dc + 1) * 128)
                    bha, bhak = nextbank()

                    def fha(bha=bha, dsl=dsl, par=par, nt=nt, T=T):
                        last = None
                        for g in range(4):
                            rhs = bass.AP(att2, par * 2048 + g * 128, [[a2s, 128], [512, nt], [1, 128]])
                            last = nc.tensor.matmul(bha[:, 0:T],
                                                    lhsT=Wpa[:, g, dsl], rhs=rhs, start=(g == 0), stop=(g == 3))
                        return last
                    S.op("pe", fha, reads=["Wpa", f"att2{par}"], writes=[bhak])
                    bhg, bhgk = nextbank()

                    def fhg(bhg=bhg, dsl=dsl, par=par, nt=nt, T=T):
                        last = None
                        for c in range(8):
                            rhs = bass.AP(gla2, par * 4096 + c * 128, [[g2s, 128], [1024, nt], [1, 128]])
                            last = nc.tensor.matmul(bhg[:, 0:T],
                                                    lhsT=Wpg[:, c, dsl], rhs=rhs, start=(c == 0), stop=(c == 7))
                        return last
                    S.op("pe", fhg, reads=["Wpg", f"gla2{par}"], writes=[bhgk])
                    gates = []
                    for which, dstt in ((0, sa), (1, sg)):
                        bkr, bkrk = nextbank()

                        def fr(bkr=bkr, which=which, dc=dc, par=par, T=T):
                            last = None
                            for d2 in range(8):
                                last = nc.tensor.matmul(
                                    bkr[:, 0:T], lhsT=Wr[:, d2, which * 1024 + dc * 128: which * 1024 + (dc + 1) * 128],
                                    rhs=x2[:, par, d2, 0:T], start=(d2 == 0), stop=(d2 == 7))
                            return last
                        S.op("pe", fr, reads=["Wr", f"x2{par}"], writes=[bkrk])
                        dpar = dc % 2
                        S.op("act", lambda bkr=bkr, dstt=dstt, dpar=dpar, T=T: nc.scalar.activation(
                            out=dstt[:, dpar, 0:T], in_=bkr[:, 0:T], func=AF.Sigmoid),
                            reads=[bkrk], writes=[f"{'sa' if which == 0 else 'sg'}{dpar}"])
                    dpar = dc % 2
                    S.op("dve", lambda bha=bha, dpar=dpar, T=T: nc.vector.tensor_tensor(
                        out=sa[:, dpar, 0:T], in0=bha[:, 0:T], in1=sa[:, dpar, 0:T], op=ALU.mult),
                        reads=[bhak, f"sa{dpar}"], writes=[f"sa{dpar}"])
                    S.op("dve", lambda bhg=bhg, dpar=dpar, T=T: nc.vector.tensor_tensor(
                        out=sg[:, dpar, 0:T], in0=bhg[:, 0:T], in1=sg[:, dpar, 0:T], op=ALU.mult),
                        reads=[bhgk, f"sg{dpar}"], writes=[f"sg{dpar}"])
                    S.op("pool", lambda dpar=dpar, dc=dc, par=par, T=T: nc.gpsimd.tensor_tensor(
                        out=merged[:, par, dc, 0:T], in0=sa[:, dpar, 0:T], in1=sg[:, dpar, 0:T], op=ALU.add),
                        reads=[f"sa{dpar}", f"sg{dpar}"], writes=[f"merged{par}"])
                for i in range(nt):
                    xp = xtl[0] % 2
                    xtl[0] += 1
                    src = xtok.ap()[(s * 4 + i) * 128:(s * 4 + i + 1) * 128, :] if nt == 4 else xstok.ap()
                    dsto = y_o.ap()[(s * 4 + i) * 128:(s * 4 + i + 1) * 128, :] if nt == 4 else ys_o.ap()
                    S.dma("sp", f"xt{xp}", lambda xp=xp, src=src: nc.sync.dma_start(out=xt2[:, xp, :], in_=src),
                          writes=[f"xt2{xp}"])
                    by = [nextbank(), nextbank()]

                    def fy(by=by, par=par, i=i):
                        last = None
                        for nb in range(2):
                            for dc in range(8):
                                last = nc.tensor.matmul(by[nb][0][:, :], lhsT=merged[:, par, dc, i * 128:(i + 1) * 128],
                                                        rhs=Wo[:, dc, nb * 512:(nb + 1) * 512], start=(dc == 0),
                                                        stop=(dc == 7))
                        return last
                    S.op("pe", fy, reads=["Wo", f"merged{par}"], writes=[by[0][1], by[1][1]])
                    for nb in range(2):
                        S.op("dve", lambda xp=xp, nb=nb, by=by: nc.vector.scalar_tensor_tensor(
                            out=xt2[:, xp, nb * 512:(nb + 1) * 512], in0=xt2[:, xp, nb * 512:(nb + 1) * 512],
                            scalar=DN_ALPHA, in1=by[nb][0][:, :], op0=ALU.mult, op1=ALU.add),
                            reads=[f"xt2{xp}", by[nb][1]], writes=[f"xt2{xp}"])

                    def fbn(xp=xp):
                        nc.vector.bn_stats(out=bnst[:, 0, :], in_=xt2[:, xp, 0:512])
                        return nc.vector.bn_stats(out=bnst[:, 1, :], in_=xt2[:, xp, 512:1024])
                    S.op("dve", fbn, reads=[f"xt2{xp}"], writes=["bnst"])
                    S.op("dve", lambda: nc.vector.bn_aggr(out=bnag[:, 0:2], in_=bnst[:].rearrange("p a b -> p (a b)")),
                         reads=["bnst"], writes=["bnag"])
                    S.op("act", lambda: nc.scalar.activation(out=bnag[:, 2:3], in_=bnag[:, 1:2], func=AF.Ln,
                                                             bias=c32[:, 1:2], scale=1.0),
                         reads=["bnag", "c32b"], writes=["bnag"])
                    S.op("act", lambda: nc.scalar.activation(out=bnag[:, 2:3], in_=bnag[:, 2:3], func=AF.Exp,
                                                             scale=-0.5), reads=["bnag"], writes=["bnag"])
                    S.op("dve", lambda: nc.vector.scalar_tensor_tensor(
                        out=bnag[:, 3:4], in0=bnag[:, 0:1], scalar=-1.0, in1=bnag[:, 2:3], op0=ALU.mult,
                        op1=ALU.mult), reads=["bnag"], writes=["bnag"])
                    S.op("act", lambda xp=xp: nc.scalar.activation(out=xt2[:, xp, :], in_=xt2[:, xp, :],
                                                                   func=AF.Identity, bias=bnag[:, 3:4],
                                                                   scale=bnag[:, 2:3]),
                         reads=[f"xt2{xp}", "bnag"], writes=[f"xt2{xp}"])
                    S.op("pool", lambda xp=xp: nc.gpsimd.tensor_tensor(out=xt2[:, xp, :], in0=xt2[:, xp, :],
                                                                       in1=lng[:], op=ALU.mult),
                         reads=[f"xt2{xp}", "lng"], writes=[f"xt2{xp}"])
                    S.op("pool", lambda xp=xp: nc.gpsimd.tensor_tensor(out=xt2[:, xp, :], in0=xt2[:, xp, :],
                                                                       in1=lnb[:], op=ALU.add),
                         reads=[f"xt2{xp}", "lnb"], writes=[f"xt2{xp}"])
                    S.dma("sp", f"yo{xp}", lambda xp=xp, dsto=dsto: nc.sync.dma_start(out=dsto, in_=xt2[:, xp, :]),
                          reads=[f"xt2{xp}"], writes=[f"yout{xp}"])
            S.barrier()
        S.finalize()
    return nc


def _t5_bucket(dist):
    n = np.maximum(dist, 0)
    nf = np.maximum(n, 1).astype(np.float32)
    large = 16 + (np.log(nf / np.float32(16)) / np.float32(math.log(128 / 16)) * np.float32(16)).astype(np.int32)
    large = np.minimum(large, 31)
    return np.where(n < 16, n, large)


def _constants():
    cb = np.zeros((128, NCB), np.float32)
    p = np.arange(128)[:, None]
    t = np.arange(128)[None, :]
    cmp_ = (p <= t).astype(np.float32)
    cb[:, CM_P:CM_P + 512] = np.tile(cmp_, (1, 4))
    same = (p // 8 == t // 8)
    cms = (same & (p <= t)).astype(np.float32)
    cb[:, CM_S:CM_S + 512] = np.tile(cms, (1, 4))
    cb[:, BLKNEG:BLKNEG + 128] = np.where(same, 0.0, NEG)
    cb[:, SEQM:SEQM + 16] = (p // 8 == np.arange(16)[None, :]).astype(np.float32)
    c512 = np.arange(512)[None, :]
    cb[:, RST_P:RST_P + 512] = np.broadcast_to((c512 % 128 != 0).astype(np.float32), (128, 512))
    cb[:, RST_S:RST_S + 512] = np.broadcast_to((c512 % 8 != 0).astype(np.float32), (128, 512))
    cb[:, IDENT:IDENT + 128] = np.eye(128, dtype=np.float32)
    cb = cb.astype(ml_dtypes.bfloat16)
    n = np.arange(384)
    dist = 255 - n
    valid = (dist >= 0) & (dist <= 128) & (n <= 382)
    bucket = _t5_bucket(dist)
    ohr = np.zeros((32, 384), np.float32)
    ohr[bucket[valid], n[valid]] = 1.0
    negr = np.broadcast_to(np.where(valid, 0.0, NEG).astype(np.float32), (8, 384)).copy()
    return cb, ohr, negr


_NC_CACHE = {}


def make_in_maps(x_prompt, x_sample, cache_k, cache_v, state_gla, rel_bias_table, w_in, w_gk_up, b_gk, attn_sink,
                 gla_norm_w, w_pa, w_pg, w_o, ln_g, ln_b):
    f = lambda a: np.ascontiguousarray(np.asarray(a, dtype=np.float32))
    x_prompt, x_sample, cache_k, cache_v, state_gla = map(f, (x_prompt, x_sample, cache_k, cache_v, state_gla))
    w_in0 = f(w_in)[0]
    pq = np.array([(h * 4 + g) * 64 + d for g in range(4) for h in range(2) for d in range(64)])
    perm = np.arange(N_IN)
    perm[C_Q:C_Q + 512] = C_Q + pq
    perm[C_GATT:C_GATT + 512] = C_GATT + pq
    w_in_p = np.ascontiguousarray(w_in0[:, perm])
    w_pa_p = np.ascontiguousarray(f(w_pa)[0][pq, :])
    cb, ohr, negr = _constants()

    xp = x_prompt[0]
    xpT = np.ascontiguousarray(xp.T)
    in_maps = []
    for c in range(NCORES):
        xT_c = np.zeros((D, TPC + 128), np.float32)
        xT_c[:, 128:] = xpT[:, c * TPC:(c + 1) * TPC]
        if c > 0:
            xT_c[:, 0:128] = xpT[:, c * TPC - 128:c * TPC]
        xs_c = x_sample[c * NSEQ:(c + 1) * NSEQ].reshape(128, D)
        ck_c = cache_k[0, c * NSEQ:(c + 1) * NSEQ].reshape(NSEQ, 128, 128)
        cv_c = cache_v[0, c * NSEQ:(c + 1) * NSEQ].reshape(NSEQ, 128, 128)
        mj = np.zeros((128, 8), np.float32)
        xpv = np.zeros((D, 7 * TPC), np.float32)
        if c > 0:
            xpv[:, (7 - c) * TPC:] = xpT[:, 0:c * TPC]
        negr_c = negr.copy()
        hneg = np.full((128, 1), NEG if c == 0 else 0.0, np.float32)
        if c == 0:
            pass
        in_maps.append({
            "xT": xT_c, "xpvT": xpv, "xtok": np.ascontiguousarray(xp[c * TPC:(c + 1) * TPC]),
            "xsT": np.ascontiguousarray(xs_c.T), "xstok": np.ascontiguousarray(xs_c),
            "ckT": np.ascontiguousarray(ck_c.transpose(0, 2, 1)), "ck": np.ascontiguousarray(ck_c),
            "cv": np.ascontiguousarray(cv_c), "st0": np.ascontiguousarray(state_gla[0, c * NSEQ:(c + 1) * NSEQ]),
            "table": f(rel_bias_table), "w_in": w_in_p, "w_up": f(w_gk_up)[0], "b_gk": f(b_gk)[0],
            "sink": f(attn_sink)[0], "gnw": f(gla_norm_w)[0], "w_pa": w_pa_p, "w_pg": f(w_pg)[0], "w_o": f(w_o)[0],
            "ln_g": f(ln_g)[0], "ln_b": f(ln_b)[0], "cstb": cb, "ohr": ohr, "negr": negr_c, "mj": mj, "hneg": hneg,
        })
    return in_maps


def kernel(**inputs):
    in_maps = make_in_maps(**inputs)
    if "nc" not in _NC_CACHE:
        _NC_CACHE["nc"] = build_nc()
    res = run_bass_kernel_spmd(_NC_CACHE["nc"], in_maps, core_ids=list(range(NCORES)))
    R = res.results
    y_prompt = np.concatenate([R[c]["y"] for c in range(NCORES)], 0).reshape(1, 16384, D)
    y_sample = np.concatenate([R[c]["ys"] for c in range(NCORES)], 0).reshape(128, 8, D)
    nkp = R[NCORES - 1]["nkp"].reshape(1, 1, 128, 2, 64)
    nvp = R[NCORES - 1]["nvp"].reshape(1, 1, 128, 2, 64)
    nsp = R[NCORES - 1]["nsp"].reshape(1, 1, 4, 128, 256)
    nks = np.concatenate([R[c]["nks"] for c in range(NCORES)], 0).reshape(1, 128, 128, 2, 64)
    nvs = np.concatenate([R[c]["nvs"] for c in range(NCORES)], 0).reshape(1, 128, 128, 2, 64)
    nss = np.concatenate([R[c]["nss"] for c in range(NCORES)], 0).reshape(1, 128, 4, 128, 256)
    return tuple(np.asarray(a, dtype=np.float32) for a in (y_prompt, y_sample, nkp, nvp, nsp, nks, nvs, nss))
```

```python
import math
import os
from contextlib import ExitStack

import numpy as np
import ml_dtypes

import concourse.bass as bass
import concourse.mybir as mybir
from concourse.bass_utils import run_bass_kernel_spmd

F32 = mybir.dt.float32
BF16 = mybir.dt.bfloat16
AF = mybir.ActivationFunctionType
ALU = mybir.AluOpType

NCORES = 8
D = 1024
TPC = 2048
NTILE = TPC // 128
NSEQ = 16
N_IN = 6416
HK_SCALE = 128 ** -0.5
DN_ALPHA = 2.0 ** 0.25
EPS = 1e-5
NEG = -1e30

C_Q, C_K, C_V, C_GATT, C_GQ, C_GK, C_GV, C_GGLA, C_GLR, C_RATT, C_RGLA = (
    0, 512, 640, 768, 1280, 1792, 2304, 3328, 4352, 4368, 5392)
W1_COLS = 4368

CM_P, CM_S, BLKNEG, SEQM, RST_P, RST_S, IDENT = 0, 512, 1024, 1152, 1168, 1680, 2192
NCB = 2320

ENGS = ("pe", "act", "dve", "pool", "sp")
SEM_CHUNK = 500
DMA_SEM_LIMIT = 512


class Sched:
    def __init__(self, nc, stack):
        self.nc = nc
        self.eng = {"pe": nc.tensor, "act": nc.scalar, "dve": nc.vector, "pool": nc.gpsimd, "sp": nc.sync}
        self.stack = stack
        self.sems = {e: [] for e in ENGS}
        self.count = {e: 0 for e in ENGS}
        self.known = {e: {} for e in ENGS}
        self.last_w = {}
        self.readers = {}
        self.dma_ch = {}
        self.prog = {e: [] for e in ENGS}
        self.retired = []
        self.enabled = True
        self.stop_at = float(os.environ.get("KSTOP", "99"))

    def stage(self, n):
        self.enabled = n <= self.stop_at

    def _newsem(self, name):
        return self.stack.enter_context(self.nc.semaphore(name))

    def _eng_sem(self, e, idx):
        while len(self.sems[e]) <= idx:
            self.sems[e].append(self._newsem(f"s_{e}_{len(self.sems[e])}"))
        return self.sems[e][idx]

    def _wait(self, e, tok):
        _, key, sem, val = tok
        if self.known[e].get(key, 0) >= val:
            return
        self.known[e][key] = val
        eng = self.eng[e]
        self.prog[e].append(lambda: eng.wait_ge(sem, val))

    def _deps(self, e, reads, writes):
        toks = []
        for b in reads:
            t = self.last_w.get(b)
            if t is not None:
                toks.append(t)
        for b in writes:
            t = self.last_w.get(b)
            if t is not None:
                toks.append(t)
            toks.extend(self.readers.get(b, ()))
        for t in toks:
            self._wait(e, t)

    def _record(self, tok, reads, writes):
        for b in reads:
            self.readers.setdefault(b, []).append(tok)
        for b in writes:
            self.last_w[b] = tok
            self.readers[b] = []

    def op(self, e, fn, reads=(), writes=()):
        if not self.enabled:
            return None
        self._deps(e, reads, writes)
        n = self.count[e]
        idx, val = n // SEM_CHUNK, n % SEM_CHUNK + 1
        sem = self._eng_sem(e, idx)
        self.prog[e].append(lambda: fn().then_inc(sem, 1))
        self.count[e] = n + 1
        tok = ("eng", (e, idx), sem, val)
        self._record(tok, reads, writes)
        return tok

    def dma(self, e, ch, fn, reads=(), writes=(), n=1):
        if not self.enabled:
            return None
        if ch not in self.dma_ch:
            self.dma_ch[ch] = [self._newsem(f"d_{ch}"), 0]
        sem, cnt = self.dma_ch[ch]
        if cnt > 0:
            self._wait(e, ("dma", ("dma", ch, id(sem)), sem, cnt))
        if cnt + 16 * n > DMA_SEM_LIMIT:
            self.retired.append((ch, sem, cnt))
            sem, cnt = self._newsem(f"d_{ch}_{len(self.retired)}"), 0
            self.dma_ch[ch] = [sem, cnt]
        self._deps(e, reads, writes)

        def run():
            insts = fn()
            if not isinstance(insts, (list, tuple)):
                insts = [insts]
            assert len(insts) == n, (ch, len(insts), n)
            for i in insts:
                i.then_inc(sem, 16)
        self.prog[e].append(run)
        cnt += 16 * n
        self.dma_ch[ch][1] = cnt
        tok = ("dma", ("dma", ch, id(sem)), sem, cnt)
        self._record(tok, reads, writes)
        return tok

    def wait_all(self, e):
        for ch, (sem, cnt) in list(self.dma_ch.items()) + [(c, (s_, n_)) for c, s_, n_ in self.retired]:
            if cnt:
                self._wait(e, ("dma", ("dma", ch, id(sem)), sem, cnt))
        for e2 in ENGS:
            n = self.count[e2]
            if n:
                idx, val = (n - 1) // SEM_CHUNK, (n - 1) % SEM_CHUNK + 1
                self._wait(e, ("eng", (e2, idx), self.sems[e2][idx], val))

    def barrier(self):
        for e in ENGS:
            self.wait_all(e)

    def finalize(self):
        self.barrier()
        with self.nc.Block() as block:
            reg = {"pe": block.tensor, "act": block.scalar, "dve": block.vector, "pool": block.gpsimd,
                   "sp": block.sync}
            for e in ENGS:
                def body(_eng, _l=self.prog[e]):
                    for t in _l:
                        t()
                reg[e](body)


def build_nc():
    nc = bass.Bass("TRN2", target_bir_lowering=False)

    def din(name, shape, dt=F32):
        return nc.dram_tensor(name, list(shape), dt, kind="ExternalInput")

    def dout(name, shape):
        return nc.dram_tensor(name, list(shape), F32, kind="ExternalOutput")

    xT = din("xT", [D, TPC + 128])
    xtok = din("xtok", [TPC, D])
    xpvT = din("xpvT", [D, 7 * TPC])
    xsT = din("xsT", [D, 128])
    xstok = din("xstok", [128, D])
    ckT = din("ckT", [NSEQ, 128, 128])
    ck = din("ck", [NSEQ, 128, 128])
    cv = din("cv", [NSEQ, 128, 128])
    st0 = din("st0", [NSEQ, 4, 128, 256])
    table = din("table", [32, 8])
    w_in = din("w_in", [D, N_IN])
    w_up = din("w_up", [16, 512])
    b_gk = din("b_gk", [512])
    sink = din("sink", [8])
    gnw_d = din("gnw", [256])
    w_pa = din("w_pa", [512, D])
    w_pg = din("w_pg", [D, D])
    w_o = din("w_o", [D, D])
    ln_g = din("ln_g", [D])
    ln_b = din("ln_b", [D])
    cstb_d = din("cstb", [128, NCB], BF16)
    ohr_d = din("ohr", [32, 384])
    negr_d = din("negr", [8, 384])
    mj_d = din("mj", [128, 8])
    hneg_d = din("hneg", [128, 1])

    y_o = dout("y", [TPC, D])
    ys_o = dout("ys", [128, D])
    nkp_o = dout("nkp", [128, 128])
    nvp_o = dout("nvp", [128, 128])
    nsp_o = dout("nsp", [4, 128, 256])
    nks_o = dout("nks", [NSEQ, 128, 128])
    nvs_o = dout("nvs", [NSEQ, 128, 128])
    nss_o = dout("nss", [NSEQ, 4, 128, 256])

    rs_scr = nc.dram_tensor("rs_scr", [8, 384], F32)
    att_scr = nc.dram_tensor("att_scr", [NTILE + 1, 128, 512], BF16, kind="ExternalOutput")
    gla_scr = nc.dram_tensor("gla_scr", [NTILE, 128, 1024], BF16, kind="ExternalOutput")
    gla_scr_s = nc.dram_tensor("gla_scr_s", [1, 128, 1024], BF16, kind="ExternalOutput")

    with ExitStack() as st:
        S = Sched(nc, st)

        def sbt(stack, name, shape, dt):
            return stack.enter_context(nc.sbuf_tensor("sb_" + name, list(shape), dt))

        def pstride(t):
            return t[:].ap[0][0]

        NB = 7
        banks = [st.enter_context(nc.psum_tensor(f"bk{i}", [128, 512], F32)) for i in range(NB)]
        bankT = st.enter_context(nc.psum_tensor("bkT", [128, 1024], BF16))
        bctr = [0]

        reserved = set()

        def nextbank():
            while True:
                i = bctr[0] % NB
                bctr[0] += 1
                if f"bk{i}" not in reserved:
                    return banks[i], f"bk{i}"

        cstb = sbt(st, "cstb", [128, NCB], BF16)
        onesb = sbt(st, "onesb", [128, 128], BF16)
        zerob = sbt(st, "zerob", [128, 128], BF16)
        c32 = sbt(st, "c32", [128, 4], F32)
        S.dma("sp", "cst", lambda: nc.sync.dma_start(out=cstb[:], in_=cstb_d.ap()), writes=["cstb"])
        S.op("pool", lambda: nc.gpsimd.memset(onesb[:], 1.0), writes=["onesb"])
        S.op("pool", lambda: nc.gpsimd.memset(zerob[:], 0.0), writes=["zerob"])
        S.op("pool", lambda: nc.gpsimd.memset(c32[:, 0:1], 1.0), writes=["c32a"])
        S.op("pool", lambda: nc.gpsimd.memset(c32[:, 1:2], EPS), writes=["c32b"])
        ident = cstb[:, IDENT:IDENT + 128]

        def proj_fm(dst_bank, col0, W, wkey, wcol, xb, xkey, ntok, nchunks, bkey, msize=128):
            def fn():
                last = None
                for c in range(nchunks):
                    for dc in range(8):
                        last = nc.tensor.matmul(
                            dst_bank[0:msize, col0 + c * ntok: col0 + (c + 1) * ntok],
                            lhsT=W[:, dc, wcol + c * 128: wcol + c * 128 + msize],
                            rhs=xb[:, dc, 0:ntok], start=(dc == 0), stop=(dc == 7))
                return last
            S.op("pe", fn, reads=[wkey, xkey], writes=[bkey])

        with ExitStack() as s1:
            W1 = sbt(s1, "W1", [128, 8, W1_COLS], BF16)
            stg = sbt(s1, "stg", [128, 2, 1024], F32)
            xstg = sbt(s1, "xstg", [128, 2, 1024], F32)
            wupb = sbt(s1, "wupb", [16, 512], BF16)
            negb = sbt(s1, "negb", [128, 4], F32)
            gnw = sbt(s1, "gnw", [128, 2], F32)
            esk = sbt(s1, "esk", [128, 4], F32)
            mj = sbt(s1, "mj", [128, 8], F32)
            biasP = sbt(s1, "biasP", [128, 2, 512], BF16)
            biasQ = sbt(s1, "biasQ", [128, 2, 512], BF16)
            biasC = sbt(s1, "biasC", [128, 2, 512], BF16)
            biasN = sbt(s1, "biasN", [128, 2, 512], BF16)
            biasP0 = sbt(s1, "biasP0", [128, 2, 512], BF16)
            hneg = sbt(s1, "hneg", [128, 1], F32)
            s0 = ExitStack()
            hank = sbt(s0, "hank", [128, 2, 128], F32)
            tabl = sbt(s0, "tabl", [32, 8], F32)
            ohr = sbt(s0, "ohr", [32, 384], F32)
            negr = sbt(s0, "negr", [8, 384], F32)
            rr = sbt(s0, "rr", [8, 384], F32)

            wupf = sbt(s0, "wupf", [16, 512], F32)
            S.dma("sp", "p0", lambda: nc.sync.dma_start(out=wupf[:], in_=w_up.ap()), writes=["wupf"])
            S.op("dve", lambda: nc.vector.tensor_copy(out=wupb[:], in_=wupf[:]), reads=["wupf"], writes=["wupb"])

            def ld_small():
                with nc.allow_non_contiguous_dma(reason="tiny parameter vectors"):
                    a = nc.sync.dma_start(out=negb[:], in_=b_gk.ap().rearrange("(h k) -> k h", k=128))
                    b = nc.sync.dma_start(out=gnw[:], in_=gnw_d.ap().rearrange("(c v) -> v c", v=128))
                c = nc.sync.dma_start(out=esk[0:64, :], in_=bass.AP(sink, 0, [[0, 64], [1, 4]]))
                d = nc.sync.dma_start(out=esk[64:128, :], in_=bass.AP(sink, 4, [[0, 64], [1, 4]]))
                e = nc.sync.dma_start(out=mj[:], in_=mj_d.ap())
                f = nc.sync.dma_start(out=tabl[:], in_=table.ap())
                g = nc.sync.dma_start(out=ohr[:], in_=ohr_d.ap())
                h = nc.sync.dma_start(out=negr[:], in_=negr_d.ap())
                i = nc.sync.dma_start(out=hneg[:], in_=hneg_d.ap())
                return [a, b, c, d, e, f, g, h, i]
            S.dma("sp", "p1", ld_small, writes=["negb", "gnw", "esk", "mj", "tabl", "ohr", "negr", "hneg"], n=9)
            S.op("dve", lambda: nc.vector.tensor_scalar(out=negb[:], in0=negb[:], scalar1=-1.0, scalar2=None,
                                                        op0=ALU.mult), reads=["negb"], writes=["negb"])
            S.op("act", lambda: nc.scalar.activation(out=esk[:], in_=esk[:], func=AF.Exp), reads=["esk"],
                 writes=["esk"])

            t5b, t5k = nextbank()
            S.op("pe", lambda: nc.tensor.matmul(t5b[0:8, 0:384], lhsT=tabl[:, :], rhs=ohr[:, :], start=True,
                                                stop=True), reads=["tabl", "ohr"], writes=[t5k])
            S.op("dve", lambda: nc.vector.tensor_tensor(out=rr[:], in0=t5b[0:8, 0:384], in1=negr[:], op=ALU.add),
                 reads=[t5k, "negr"], writes=["rr"])
            S.dma("sp", "rs", lambda: nc.sync.dma_start(out=rs_scr.ap(), in_=rr[:]), reads=["rr"], writes=["rs_scr"])
            hps = pstride(hank)
            for hd in range(8):
                h, g = hd // 4, hd % 4
                for half, (base, dst) in enumerate(((0, biasP), (128, biasQ))):
                    slot = (hd * 2 + half) % 2
                    S.dma("sp", f"hk{slot}",
                          lambda hd=hd, base=base, slot=slot: nc.sync.dma_start(
                              out=hank[:, slot, :], in_=bass.AP(rs_scr, hd * 384 + base, [[1, 128], [1, 128]])),
                          reads=["rs_scr"], writes=[f"hank{slot}"])
                    S.op("dve",
                         lambda dst=dst, h=h, g=g, slot=slot: nc.vector.tensor_copy(
                             out=dst[:, h, g * 128:(g + 1) * 128],
                             in_=bass.AP(hank, slot * 128 + 127, [[hps, 128], [-1, 128]])),
                         reads=[f"hank{slot}"], writes=["bias" + ("P" if half == 0 else "Q")])
            bps = pstride(biasP)
            cps = pstride(cstb)
            for h in range(2):
                S.op("dve", lambda h=h: nc.vector.tensor_copy(
                    out=biasC[:, h, :].rearrange("p (b g t) -> p b g t", b=16, g=4),
                    in_=bass.AP(biasP, h * 512, [[bps, 128], [0, 16], [128, 4], [1, 8]])),
                    reads=["biasP"], writes=["biasC"])
                S.op("dve", lambda h=h: nc.vector.tensor_tensor(
                    out=biasN[:, h, :].rearrange("p (b g t) -> p b g t", b=16, g=4),
                    in0=bass.AP(biasQ, h * 512, [[bps, 128], [8, 16], [128, 4], [1, 8]]),
                    in1=bass.AP(cstb, BLKNEG, [[cps, 128], [8, 16], [0, 4], [1, 8]]), op=ALU.add),
                    reads=["biasQ", "cstb"], writes=["biasN"])
            S.op("dve", lambda: nc.vector.tensor_scalar(out=biasP0[:], in0=biasP[:], scalar1=hneg[:, 0:1],
                                                        scalar2=None, op0=ALU.add),
                 reads=["biasP", "hneg"], writes=["biasP0"])
            S.barrier()
            s0.close()

            S.stage(2)
            w_view = w_in.ap().rearrange("(dc p) n -> p dc n", p=128)
            wblocks = [("GK", C_GK, 512), ("GV", C_GV, 1024), ("GLR", C_GLR, 16), ("Q", C_Q, 512),
                       ("KV", C_K, 256), ("GATT", C_GATT, 512), ("GQ", C_GQ, 512), ("GGLA", C_GGLA, 1024)]
            wl = [0]
            for name, c0, cw in wblocks:
                for dc in range(8):
                    slot = wl[0] % 2
                    wl[0] += 1
                    S.dma("sp", f"w{slot}",
                          lambda slot=slot, dc=dc, c0=c0, cw=cw: nc.sync.dma_start(
                              out=stg[:, slot, 0:cw], in_=w_view[:, dc, c0:c0 + cw]),
                          writes=[f"stg{slot}"])
                    ceng = ("pool", "dve", "act")[wl[0] % 3]

                    def fcw(slot=slot, dc=dc, c0=c0, cw=cw, ceng=ceng):
                        o = W1[:, dc, c0:c0 + cw]
                        i = stg[:, slot, 0:cw]
                        if ceng == "pool":
                            return nc.gpsimd.tensor_copy(out=o, in_=i)
                        if ceng == "dve":
                            return nc.vector.tensor_copy(out=o, in_=i)
                        return nc.scalar.copy(out=o, in_=i)
                    S.op(ceng, fcw, reads=[f"stg{slot}"], writes=[f"W_{name}"])

            nLl = sbt(s1, "nLl", [128, 64], F32)
            edl = sbt(s1, "edl", [128, 64], F32)
            S32 = sbt(s1, "S32", [128, 1028], F32)
            Sb = sbt(s1, "Sb", [128, 1024], BF16)

            S.stage(3)
            with ExitStack() as sA:
                xA = sbt(sA, "xA", [128, 2, 8, 512], BF16)
                ltA = sbt(sA, "ltA", [128, 2048], F32)
                LtA = sbt(sA, "LtA", [128, 2048], F32)
                kendTA = sbt(sA, "kendTA", [128, 2048], BF16)
                kendA = sbt(sA, "kendA", [128, 4, 512], BF16)
                gvA = sbt(sA, "gvA", [128, 4, 1024], BF16)
                glrbA = sbt(sA, "glrbA", [16, 512], BF16)
                xpv_view = xpvT.ap().rearrange("(dc p) t -> p dc t", p=128)
                NSTEP = 7 * TPC // 512
                xl = [0]
                ops1 = pstride(onesb)
                nps_ = pstride(nLl)

                def load_xA(sidx, par):
                    for dcp in range(4):
                        slot = xl[0] % 2
                        xl[0] += 1
                        S.dma("sp", f"x{slot}", lambda slot=slot, dcp=dcp: nc.sync.dma_start(
                            out=xstg[:, slot, :].rearrange("p (a t) -> p a t", a=2),
                            in_=xpv_view[:, dcp * 2:dcp * 2 + 2, sidx * 512:(sidx + 1) * 512]),
                            writes=[f"xstg{slot}"])
                        S.op("pool", lambda slot=slot, dcp=dcp: nc.gpsimd.tensor_copy(
                            out=xA[:, par, dcp * 2:dcp * 2 + 2, :],
                            in_=xstg[:, slot, :].rearrange("p (a t) -> p a t", a=2)),
                            reads=[f"xstg{slot}"], writes=[f"xA{par}"])

                def phaseA_step(sidx, par):
                    xb = xA[:, par]
                    xkey = f"xA{par}"
                    bk, bkk = nextbank()

                    def fglr():
                        last = None
                        for dc in range(8):
                            last = nc.tensor.matmul(bk[0:16, :], lhsT=W1[:, dc, C_GLR:C_GLR + 16], rhs=xb[:, dc, :],
                                                    start=(dc == 0), stop=(dc == 7))
                        return last
                    S.op("pe", fglr, reads=["W_GLR", xkey], writes=[bkk])
                    S.op("dve", lambda: nc.vector.tensor_copy(out=glrbA[:], in_=bk[0:16, :]), reads=[bkk],
                         writes=["glrbA"])
                    for h in range(4):
                        bkh, bkhk = nextbank()
                        S.op("pe", lambda bkh=bkh, h=h: nc.tensor.matmul(
                            bkh[:, :], lhsT=wupb[:, h * 128:(h + 1) * 128], rhs=glrbA[:, :], start=True, stop=True),
                            reads=["wupb", "glrbA"], writes=[bkhk])
                        S.op("act", lambda bkh=bkh, h=h: nc.scalar.activation(
                            out=ltA[:, h * 512:(h + 1) * 512], in_=bkh[:, :], func=AF.Exp, bias=negb[:, h:h + 1],
                            scale=-1.0), reads=[bkhk, "negb"], writes=["ltA"])
                    S.op("act", lambda: nc.scalar.activation(out=ltA[:], in_=ltA[:], func=AF.Ln, bias=c32[:, 0:1],
                                                             scale=1.0), reads=["ltA", "c32a"], writes=["ltA"])

                    for i in range(4):
                        for nb in range(2):
                            bkv, bkvk = nextbank()

                            def fgv(bkv=bkv, nb=nb, i=i):
                                last = None
                                for dc in range(8):
                                    last = nc.tensor.matmul(bkv[:, :], lhsT=xb[:, dc, i * 128:(i + 1) * 128],
                                                            rhs=W1[:, dc, C_GV + nb * 512: C_GV + (nb + 1) * 512],
                                                            start=(dc == 0), stop=(dc == 7))
                                return last
                            S.op("pe", fgv, reads=["W_GV", xkey], writes=[bkvk])
                            S.op("act", lambda bkv=bkv, nb=nb, i=i: nc.scalar.copy(
                                out=gvA[:, i, nb * 512:(nb + 1) * 512], in_=bkv[:, :]), reads=[bkvk], writes=["gvA"])
                    def fscan():
                        last = None
                        for h in range(4):
                            last = nc.vector.tensor_tensor_scan(
                                out=LtA[:, h * 512:(h + 1) * 512], data0=bass.AP(onesb, 0, [[ops1, 128], [0, 512]]),
                                data1=ltA[:, h * 512:(h + 1) * 512], initial=0.0, op0=ALU.mult, op1=ALU.add)
                        return last
                    S.op("dve", fscan, reads=["ltA", "onesb"], writes=["LtA"])
                    lpsA = pstride(LtA)
                    S.op("dve", lambda: nc.vector.tensor_scalar(
                        out=nLl[:, 0:4], in0=bass.AP(LtA, 511, [[lpsA, 128], [512, 4]]), scalar1=-1.0 / 16.0,
                        scalar2=None, op0=ALU.mult), reads=["LtA"], writes=["nLl"])
                    S.op("dve", lambda: nc.vector.scalar_tensor_tensor(
                        out=ltA[:].rearrange("p (g c) -> p g c", g=4), in0=LtA[:].rearrange("p (g c) -> p g c", g=4),
                        scalar=1.0 / 16.0, in1=bass.AP(nLl, 0, [[nps_, 128], [1, 4], [0, 512]]),
                        op0=ALU.mult, op1=ALU.add), reads=["LtA", "nLl"], writes=["ltA"])
                    S.op("act", lambda: nc.scalar.activation(out=ltA[:], in_=ltA[:], func=AF.Exp), reads=["ltA"],
                         writes=["ltA"])
                    S.op("act", lambda: nc.scalar.activation(out=edl[:, 0:4], in_=nLl[:, 0:4], func=AF.Exp),
                         reads=["nLl"], writes=["edl"])
                    for h in range(4):
                        bkh, bkhk = nextbank()

                        def fgk(bkh=bkh, h=h):
                            last = None
                            for dc in range(8):
                                last = nc.tensor.matmul(bkh[:, :], lhsT=W1[:, dc, C_GK + h * 128: C_GK + (h + 1) * 128],
                                                        rhs=xb[:, dc, :], start=(dc == 0), stop=(dc == 7))
                            return last
                        S.op("pe", fgk, reads=["W_GK", xkey], writes=[bkhk])
                        S.op("dve", lambda bkh=bkh, h=h: nc.vector.tensor_tensor(
                            out=kendTA[:, h * 512:(h + 1) * 512], in0=bkh[:, :], in1=ltA[:, h * 512:(h + 1) * 512],
                            op=ALU.mult), reads=[bkhk, "ltA"], writes=["kendTA"])
                    for pr in range(2):
                        def ftr(pr=pr):
                            last = None
                            for ii in range(2):
                                i = pr * 2 + ii
                                for h in range(4):
                                    last = nc.tensor.transpose(
                                        bankT[:, ii * 512 + h * 128: ii * 512 + (h + 1) * 128],
                                        kendTA[:, h * 512 + i * 128: h * 512 + (i + 1) * 128], ident)
                            return last
                        S.op("pe", ftr, reads=["kendTA", "cstb"], writes=["bkT"])
                        S.op("act", lambda pr=pr: nc.scalar.copy(
                            out=kendA[:, pr * 2:pr * 2 + 2, :].rearrange("p a c -> p (a c)"), in_=bankT[:, 0:1024]),
                            reads=["bkT"], writes=["kendA"])
                    b2 = [nextbank(), nextbank()]

                    def fds():
                        last = None
                        for h in range(4):
                            bk_ = b2[h // 2][0]
                            for i in range(4):
                                last = nc.tensor.matmul(bk_[:, (h % 2) * 256:(h % 2) * 256 + 256],
                                                        lhsT=kendA[:, i, h * 128:(h + 1) * 128],
                                                        rhs=gvA[:, i, h * 256:(h + 1) * 256], start=(i == 0),
                                                        stop=(i == 3))
                        return last
                    S.op("pe", fds, reads=["kendA", "gvA"], writes=[b2[0][1], b2[1][1]])

                    def fu():
                        last = None
                        for h in range(4):
                            bk_ = b2[h // 2][0]
                            last = nc.vector.scalar_tensor_tensor(
                                out=S32[:, h * 256:(h + 1) * 256], in0=S32[:, h * 256:(h + 1) * 256],
                                scalar=edl[:, h:h + 1], in1=bk_[:, (h % 2) * 256:(h % 2) * 256 + 256],
                                op0=ALU.mult, op1=ALU.add)
                        return last
                    S.op("dve", fu, reads=["S32", "edl", b2[0][1], b2[1][1]], writes=["S32"])

                S.op("pool", lambda: nc.gpsimd.memset(S32[:], 0.0), writes=["S32"])
                load_xA(0, 0)
                for sidx in range(NSTEP):
                    par = sidx % 2
                    if sidx + 1 < NSTEP:
                        load_xA(sidx + 1, 1 - par)
                    phaseA_step(sidx, par)
                S.op("pool", lambda: nc.gpsimd.tensor_copy(out=Sb[:], in_=S32[:, 0:1024]), reads=["S32"],
                     writes=["Sb"])
                S.barrier()

            xTb = sbt(s1, "xTb", [128, 2, 8, 128], BF16)
            qT = sbt(s1, "qT", [128, 2, 2, 512], BF16)
            gattS = sbt(s1, "gattS", [128, 2, 512], BF16)
            Kr = sbt(s1, "Kr", [128, 2, 128], BF16)
            Vr = sbt(s1, "Vr", [128, 2, 128], BF16)
            PT = sbt(s1, "PT", [128, 4, 512], BF16)
            lnd = sbt(s1, "lnd", [128, 512], F32)
            actatt = sbt(s1, "actatt", [128, 2, 512], BF16)
            glrb = sbt(s1, "glrb", [16, 128], BF16)
            lt = sbt(s1, "lt", [128, 512], F32)
            Lt = sbt(s1, "Lt", [128, 512], F32)
            ekd = sbt(s1, "ekd", [128, 512], F32)
            eb = sbt(s1, "eb", [128, 512], F32)
            enb = sbt(s1, "enb", [128, 512], F32)
            qt = sbt(s1, "qt", [128, 2, 512], BF16)
            kt = sbt(s1, "kt", [128, 2, 512], BF16)
            kendT = sbt(s1, "kendT", [128, 512], BF16)
            kend = sbt(s1, "kend", [128, 2, 512], BF16)
            gv = sbt(s1, "gv", [128, 2, 1024], BF16)
            gglaS = sbt(s1, "gglaS", [128, 2, 1024], BF16)
            ATm = sbt(s1, "ATm", [128, 512], BF16)
            osq = sbt(s1, "osq", [128, 1024], BF16)
            rstd = sbt(s1, "rstd", [128, 512], F32)
            rg2 = sbt(s1, "rg2", [128, 1024], BF16)
            actgla = sbt(s1, "actgla", [128, 2, 1024], BF16)
            kvtok = sbt(s1, "kvtok", [128, 256], F32)

            S.op("pool", lambda: nc.gpsimd.memset(qT[:].rearrange("p a h c -> p (a h c)"), 0.0),
                 writes=["qT0", "qT1"])

            def load_x(src_view, par, key):
                S.dma("sp", f"x{par}", lambda: nc.sync.dma_start(
                    out=xstg[:, par, :].rearrange("p (dc t) -> p dc t", dc=8), in_=src_view),
                    writes=[f"xstg{par}"])
                S.op("pool", lambda: nc.gpsimd.tensor_copy(
                    out=xTb[:, par].rearrange("p dc t -> p (dc t)"), in_=xstg[:, par, :]),
                    reads=[f"xstg{par}"], writes=[key])

            xT_view = xT.ap().rearrange("(dc p) t -> p dc t", p=128)
            xsT_view = xsT.ap().rearrange("(dc p) t -> p dc t", p=128)

            def decay_prep(par, xkey, sample, need_q):
                xb = xTb[:, par]
                bk, bkk = nextbank()
                proj_fm(bk, 0, W1, "W_GLR", C_GLR, xb, xkey, 128, 1, bkk, msize=16)
                S.op("dve", lambda: nc.vector.tensor_copy(out=glrb[:], in_=bk[0:16, 0:128]), reads=[bkk],
                     writes=["glrb"])
                bk2, bkk2 = nextbank()

                def fn():
                    last = None
                    for h in range(4):
                        last = nc.tensor.matmul(bk2[:, h * 128:(h + 1) * 128], lhsT=wupb[:, h * 128:(h + 1) * 128],
                                                rhs=glrb[:, :], start=True, stop=True)
                    return last
                S.op("pe", fn, reads=["wupb", "glrb"], writes=[bkk2])

                def fe():
                    last = None
                    for h in range(4):
                        last = nc.scalar.activation(out=lt[:, h * 128:(h + 1) * 128], in_=bk2[:, h * 128:(h + 1) * 128],
                                                    func=AF.Exp, bias=negb[:, h:h + 1], scale=-1.0)
                    return last
                S.op("act", fe, reads=[bkk2, "negb"], writes=["lt"])
                S.op("act", lambda: nc.scalar.activation(out=lt[:], in_=lt[:], func=AF.Ln, bias=c32[:, 0:1],
                                                         scale=1.0), reads=["lt", "c32a"], writes=["lt"])
                rst = RST_S if sample else RST_P
                S.op("dve", lambda: nc.vector.tensor_tensor_scan(
                    out=Lt[:], data0=cstb[:, rst:rst + 512], data1=lt[:], initial=0.0, op0=ALU.mult, op1=ALU.add),
                    reads=["lt", "cstb"], writes=["Lt"])
                ng = 64 if sample else 4
                cl = 512 // ng
                lps = pstride(Lt)
                S.op("dve", lambda: nc.vector.tensor_scalar(
                    out=nLl[:, 0:ng], in0=bass.AP(Lt, cl - 1, [[lps, 128], [cl, ng]]), scalar1=-1.0 / 16.0,
                    scalar2=None, op0=ALU.mult), reads=["Lt"], writes=["nLl"])
                nps = pstride(nLl)
                S.op("dve", lambda: nc.vector.scalar_tensor_tensor(
                    out=ekd[:].rearrange("p (g c) -> p g c", g=ng), in0=Lt[:].rearrange("p (g c) -> p g c", g=ng),
                    scalar=1.0 / 16.0, in1=bass.AP(nLl, 0, [[nps, 128], [1, ng], [0, cl]]),
                    op0=ALU.mult, op1=ALU.add), reads=["Lt", "nLl"], writes=["ekd"])
                S.op("act", lambda: nc.scalar.activation(out=ekd[:], in_=ekd[:], func=AF.Exp), reads=["ekd"],
                     writes=["ekd"])
                S.op("act", lambda: nc.scalar.activation(out=edl[:, 0:ng], in_=nLl[:, 0:ng], func=AF.Exp),
                     reads=["nLl"], writes=["edl"])
                if need_q:
                    S.op("act", lambda: nc.scalar.activation(out=eb[:], in_=Lt[:], func=AF.Exp, scale=-1.0 / 16.0),
                         reads=["Lt"], writes=["eb"])
                    S.op("act", lambda: nc.scalar.activation(out=enb[:], in_=Lt[:], func=AF.Exp, scale=1.0 / 16.0),
                         reads=["Lt"], writes=["enb"])

            def gk_gv(par, xkey, need_q):
                xb = xTb[:, par]
                bk, bkk = nextbank()
                proj_fm(bk, 0, W1, "W_GK", C_GK, xb, xkey, 128, 4, bkk)
                S.op("dve", lambda: nc.vector.tensor_tensor(out=kendT[:], in0=bk[:, :], in1=ekd[:], op=ALU.mult),
                     reads=[bkk, "ekd"], writes=["kendT"])
                if need_q:
                    S.op("dve", lambda: nc.vector.tensor_tensor(out=kt[:, par, :], in0=bk[:, :], in1=enb[:],
                                                                op=ALU.mult),
                         reads=[bkk, "enb"], writes=[f"kt{par}"])

                def ftr():
                    last = None
                    for h in range(4):
                        last = nc.tensor.transpose(bankT[:, h * 128:(h + 1) * 128], kendT[:, h * 128:(h + 1) * 128],
                                                   ident)
                    return last
                S.op("pe", ftr, reads=["kendT", "cstb"], writes=["bkT"])
                S.op("act", lambda: nc.scalar.copy(out=kend[:, par, :], in_=bankT[:, 0:512]), reads=["bkT"],
                     writes=[f"kend{par}"])
                for nb in range(2):
                    bkv, bkvk = nextbank()

                    def fgv(bkv=bkv, nb=nb):
                        last = None
                        for dc in range(8):
                            last = nc.tensor.matmul(bkv[:, :], lhsT=xb[:, dc, :],
                                                    rhs=W1[:, dc, C_GV + nb * 512: C_GV + (nb + 1) * 512],
                                                    start=(dc == 0), stop=(dc == 7))
                        return last
                    S.op("pe", fgv, reads=["W_GV", xkey], writes=[bkvk])
                    S.op("act", lambda bkv=bkv, nb=nb: nc.scalar.copy(out=gv[:, par, nb * 512:(nb + 1) * 512],
                                                                      in_=bkv[:, :]),
                         reads=[bkvk], writes=[f"gv{par}"])

            def state_update(par, st32, stkey, kend_ap, kendkey, edl_col0, edl_step, out32=None, outkey=None):
                if out32 is None:
                    out32, outkey = st32, stkey
                b2 = [nextbank(), nextbank()]

                def fn():
                    last = None
                    for h in range(4):
                        bk_ = b2[h // 2][0]
                        last = nc.tensor.matmul(bk_[:, (h % 2) * 256:(h % 2) * 256 + 256],
                                                lhsT=kend_ap[:, h * 128:(h + 1) * 128],
                                                rhs=gv[:, par, h * 256:(h + 1) * 256], start=True, stop=True)
                    return last
                S.op("pe", fn, reads=[kendkey, f"gv{par}"], writes=[b2[0][1], b2[1][1]])

                def fu():
                    last = None
                    for h in range(4):
                        bk_ = b2[h // 2][0]
                        c = edl_col0 + h * edl_step
                        last = nc.vector.scalar_tensor_tensor(
                            out=out32[:, h * 256:(h + 1) * 256], in0=st32[:, h * 256:(h + 1) * 256],
                            scalar=edl[:, c:c + 1], in1=bk_[:, (h % 2) * 256:(h % 2) * 256 + 256],
                            op0=ALU.mult, op1=ALU.add)
                    return last
                S.op("dve", fu, reads=[stkey, "edl", b2[0][1], b2[1][1]], writes=[outkey])

            def attn_proj(par, xkey, ntv_key):
                xb = xTb[:, par]
                bk, bkk = nextbank()
                proj_fm(bk, 0, W1, "W_Q", C_Q, xb, xkey, 128, 4, bkk)
                def fq():
                    nc.scalar.activation(out=qT[0:64, par, 0, :], in_=bk[0:64, :], func=AF.Copy, scale=0.125)
                    return nc.scalar.activation(out=qT[64:128, par, 1, :], in_=bk[64:128, :], func=AF.Copy, scale=0.125)
                S.op("act", fq, reads=[bkk], writes=[f"qT{par}"])
                bk2, bkk2 = nextbank()
                proj_fm(bk2, 0, W1, "W_GATT", C_GATT, xb, xkey, 128, 4, bkk2)
                S.op("act", lambda: nc.scalar.activation(out=gattS[:, par, :], in_=bk2[:, :], func=AF.Silu),
                     reads=[bkk2], writes=[f"gattS{par}"])

            def attn_finish(par, bo, bok, bd, bdk, scr_idx, perm=False):
                def fl():
                    last = None
                    for g in range(4):
                        src = bd[:, g * 128:(g + 1) * 128]
                        dstv = lnd[:, g * 128:(g + 1) * 128]
                        if perm:
                            src = bd[:, :].rearrange("p (b g t) -> p g b t", b=16, g=4)[:, g, :, :]
                            dstv = dstv.rearrange("p (b t) -> p b t", b=16)
                        last = nc.scalar.activation(out=dstv, in_=src, func=AF.Ln, bias=esk[:, g:g + 1], scale=1.0)
                    return last
                S.op("act", fl, reads=[bdk, "esk"], writes=["lnd"])
                S.op("act", lambda: nc.scalar.activation(out=lnd[:], in_=lnd[:], func=AF.Exp, scale=-1.0),
                     reads=["lnd"], writes=["lnd"])
                S.op("pool", lambda: nc.gpsimd.tensor_tensor(out=lnd[:], in0=lnd[:], in1=gattS[:, par, :],
                                                             op=ALU.mult),
                     reads=["lnd", f"gattS{par}"], writes=["lnd"])
                def fm():
                    if not perm:
                        return nc.vector.tensor_tensor(out=actatt[:, par, :], in0=bo[:, :], in1=lnd[:], op=ALU.mult)
                    last = None
                    for g in range(4):
                        last = nc.vector.tensor_tensor(
                            out=actatt[:, par, g * 128:(g + 1) * 128].rearrange("p (b t) -> p b t", b=16),
                            in0=bo[:, :].rearrange("p (b g t) -> p g b t", b=16, g=4)[:, g, :, :],
                            in1=lnd[:, g * 128:(g + 1) * 128].rearrange("p (b t) -> p b t", b=16), op=ALU.mult)
                    return last
                S.op("dve", fm, reads=[bok, "lnd"], writes=[f"actatt{par}"])
                if not os.environ.get("KSKIP_SA"):
                  S.dma("sp", f"sa{par}", lambda: nc.sync.dma_start(out=att_scr.ap()[scr_idx], in_=actatt[:, par, :]),
                      reads=[f"actatt{par}"], writes=[f"att_scr{scr_idx}"])

            def gla_proj(par, xkey):
                xb = xTb[:, par]
                bk, bkk = nextbank()
                proj_fm(bk, 0, W1, "W_GQ", C_GQ, xb, xkey, 128, 4, bkk)
                S.op("dve", lambda: nc.vector.scalar_tensor_tensor(out=qt[:, par, :], in0=bk[:, :], scalar=HK_SCALE,
                                                                   in1=eb[:], op0=ALU.mult, op1=ALU.mult),
                     reads=[bkk, "eb"], writes=[f"qt{par}"])
                for half in range(2):
                    bk2, bkk2 = nextbank()
                    proj_fm(bk2, 0, W1, "W_GGLA", C_GGLA + half * 512, xb, xkey, 128, 4, bkk2)
                    S.op("act", lambda bk2=bk2, half=half: nc.scalar.activation(
                        out=gglaS[:, par, half * 512:(half + 1) * 512], in_=bk2[:, :], func=AF.Silu),
                        reads=[bkk2], writes=[f"gglaS{par}"])

                def fw():
                    last = None
                    v = gglaS[:, par, :].rearrange("p (h c t) -> p h c t", h=4, c=2)
                    for c in range(2):
                        last = nc.gpsimd.tensor_scalar(out=v[:, :, c, :], in0=v[:, :, c, :], scalar1=gnw[:, c:c + 1],
                                                       scalar2=None, op0=ALU.mult)
                    return last
                S.op("pool", fw, reads=[f"gglaS{par}", "gnw"], writes=[f"gglaS{par}"])

            def gla_AT(par, cm):
                bk, bkk = nextbank()

                def fn():
                    last = None
                    for h in range(4):
                        last = nc.tensor.matmul(bk[:, h * 128:(h + 1) * 128], lhsT=kt[:, par, h * 128:(h + 1) * 128],
                                                rhs=qt[:, par, h * 128:(h + 1) * 128], start=True, stop=True)
                    return last
                S.op("pe", fn, reads=[f"kt{par}", f"qt{par}"], writes=[bkk])
                S.op("dve", lambda: nc.vector.tensor_tensor(out=ATm[:], in0=bk[:, :], in1=cstb[:, cm:cm + 512],
                                                            op=ALU.mult), reads=[bkk, "cstb"], writes=["ATm"])

            def gla_finish(par, bo2, scr_idx):
                for i in range(2):
                    S.op("act", lambda i=i: nc.scalar.activation(out=osq[:, i * 512:(i + 1) * 512], in_=bo2[i][0][:, :],
                                                                 func=AF.Square),
                         reads=[bo2[i][1]], writes=["osq"])
                if scr_idx == NTILE: S.stage(6.1)
                bs, bsk = nextbank()
                ops_ = pstride(osq)

                def fs():
                    last = None
                    for c in range(2):
                        last = nc.tensor.matmul(bs[:, :], lhsT=onesb[:, :],
                                                rhs=bass.AP(osq, c * 128, [[ops_, 128], [256, 4], [1, 128]]),
                                                start=(c == 0), stop=(c == 1))
                    return last
                S.op("pe", fs, reads=["osq", "onesb"], writes=[bsk])
                if scr_idx == NTILE: S.stage(6.2)
                S.op("act", lambda: nc.scalar.activation(out=rstd[:], in_=bs[:, :], func=AF.Ln, bias=c32[:, 1:2],
                                                         scale=1.0 / 256.0), reads=[bsk, "c32b"], writes=["rstd"])
                S.op("act", lambda: nc.scalar.activation(out=rstd[:], in_=rstd[:], func=AF.Exp, scale=-0.5),
                     reads=["rstd"], writes=["rstd"])

                if scr_idx == NTILE: S.stage(6.3)

                def fr():
                    last = None
                    gvw = gglaS[:, par, :].rearrange("p (h c t) -> p h c t", h=4, c=2)
                    rv = rg2[:].rearrange("p (h c t) -> p h c t", h=4, c=2)
                    for c in range(2):
                        last = nc.gpsimd.tensor_tensor(out=rv[:, :, c, :], in0=gvw[:, :, c, :],
                                                       in1=rstd[:].rearrange("p (h t) -> p h t", h=4), op=ALU.mult)
                    return last
                S.op("pool", fr, reads=[f"gglaS{par}", "rstd"], writes=["rg2"])
                if scr_idx == NTILE: S.stage(6.4)
                for i in range(2):
                    S.op("dve", lambda i=i: nc.vector.tensor_tensor(
                        out=actgla[:, par, i * 512:(i + 1) * 512], in0=bo2[i][0][:, :],
                        in1=rg2[:, i * 512:(i + 1) * 512], op=ALU.mult),
                        reads=[bo2[i][1], "rg2"], writes=[f"actgla{par}"])
                if scr_idx == NTILE: S.stage(6.5)
                S.dma(("pool" if os.environ.get("KPOOLQ") else "sp"), (f"sa{par}" if os.environ.get("KCH") else f"sg{par}"), lambda: [
                    (nc.gpsimd if os.environ.get("KPOOLQ") else nc.sync).dma_start(out=(gla_scr_s.ap()[0] if scr_idx == NTILE else gla_scr.ap()[scr_idx])[:, i * 512:(i + 1) * 512],
                                      in_=(gv if os.environ.get("KSRC") else actgla)[:, par, i * 512:(i + 1) * 512]) for i in range(2)],
                      reads=([] if os.environ.get("KNODEP") else [f"actgla{par}"]), writes=[f"gla_scr{scr_idx}"], n=2)

            S.stage(5)
            with ExitStack() as s2:
                KcT = sbt(s2, "KcT", [128, 16, 128], BF16)
                Vc = sbt(s2, "Vc", [128, 16, 128], BF16)
                S0f = sbt(s2, "S0f", [128, 2, 1028], F32)
                S0b = sbt(s2, "S0b", [128, 2, 1024], BF16)
                kendm = sbt(s2, "kendm", [128, 2, 512], BF16)
                Snew = sbt(s2, "Snew", [128, 2, 1024], F32)

                s_par = 0
                load_x(xsT_view, s_par, "xTb0")
                s_xkey = "xTb0"
                s_xb = xTb[:, s_par]
                if os.environ.get("KSKIP_CACHE"):
                    S.enabled = False
                S.dma("sp", "w0", lambda: nc.sync.dma_start(
                    out=stg[:].rearrange("p a (b j) -> p (a b) j", j=128), in_=ckT.ap().rearrange("b p j -> p b j")),
                    writes=["stg0", "stg1"])
                S.op("pool", lambda: nc.gpsimd.tensor_copy(out=KcT[:].rearrange("p b j -> p (b j)"),
                                                           in_=stg[:].rearrange("p a c -> p (a c)")),
                     reads=["stg0", "stg1"], writes=["KcT"])
                S.dma("sp", "w0", lambda: nc.sync.dma_start(
                    out=stg[:].rearrange("p a (b j) -> p (a b) j", j=128), in_=cv.ap().rearrange("b j c -> j b c")),
                    writes=["stg0", "stg1"])
                S.op("pool", lambda: nc.gpsimd.tensor_copy(out=Vc[:].rearrange("p b j -> p (b j)"),
                                                           in_=stg[:].rearrange("p a c -> p (a c)")),
                     reads=["stg0", "stg1"], writes=["Vc"])
                if os.environ.get("KSKIP_CACHE"):
                    S.enabled = True
                if not os.environ.get("KSKIP_CP"):
                  S.dma("sp", "cpk", lambda: nc.sync.dma_start(out=nks_o.ap()[:, 0:120, :], in_=ck.ap()[:, 8:128, :]),
                      writes=["nks_a"])
                if not os.environ.get("KSKIP_CP"):
                  S.dma("sp", "cpv", lambda: nc.sync.dma_start(out=nvs_o.ap()[:, 0:120, :], in_=cv.ap()[:, 8:128, :]),
                      writes=["nvs_a"])

                S.stage(5.2)
                attn_proj(s_par, s_xkey, None)
                s_bk, s_bkk = nextbank()
                proj_fm(s_bk, 0, W1, "W_KV", C_K, s_xb, s_xkey, 128, 1, s_bkk)
                S.op("act", lambda: nc.scalar.copy(out=Kr[:, 0, :], in_=s_bk[:, 0:128]), reads=[s_bkk], writes=["Kr0"])
                s_bk2, s_bkk2 = nextbank()

                def fkv():
                    last = None
                    for dc in range(8):
                        last = nc.tensor.matmul(s_bk2[:, 0:256], lhsT=s_xb[:, dc, :], rhs=W1[:, dc, C_K:C_K + 256],
                                                start=(dc == 0), stop=(dc == 7))
                    return last
                S.op("pe", fkv, reads=["W_KV", s_xkey], writes=[s_bkk2])
                S.op("act", lambda: nc.scalar.copy(out=kvtok[:], in_=s_bk2[:, 0:256]), reads=[s_bkk2], writes=["kvtok"])
                S.op("dve", lambda: nc.vector.tensor_copy(out=Vr[:, 0, :], in_=s_bk2[:, 128:256]), reads=[s_bkk2],
                     writes=["Vr0"])

                def st_newkv():
                    l = []
                    for b in range(NSEQ):
                        l.append(nc.sync.dma_start(out=nks_o.ap()[b, 120:128, :], in_=kvtok[b * 8:(b + 1) * 8, 0:128]))
                        l.append(nc.sync.dma_start(out=nvs_o.ap()[b, 120:128, :], in_=kvtok[b * 8:(b + 1) * 8, 128:256]))
                    return l
                S.dma("sp", "nkv", st_newkv, reads=["kvtok"], writes=["nks_b"], n=2 * NSEQ)

                S.stage(5.3)
                scb = []
                for h in range(2):
                    hp = slice(h * 64, (h + 1) * 64)
                    bkn, bknk = nextbank()

                    def fsn(bkn=bkn, h=h, hp=hp):
                        nc.tensor.matmul(bkn[:, :], lhsT=ident, rhs=biasN[:, h, :], start=True, stop=False)
                        return nc.tensor.matmul(bkn[:, :], lhsT=Kr[:, 0, :],
                                                rhs=qT[:, s_par, h, :].rearrange("p (g b t) -> p b g t", g=4, b=16),
                                                start=False, stop=True)
                    S.op("pe", fsn, reads=["cstb", "biasN", "Kr0", f"qT{s_par}"], writes=[bknk])
                    S.op("act", lambda bkn=bkn, h=h: nc.scalar.activation(out=PT[:, h * 2 + 1, :], in_=bkn[:, :],
                                                                          func=AF.Exp),
                         reads=[bknk], writes=[f"PT{h * 2 + 1}"])
                    bkc, bkck = nextbank()

                    def fsc(bkc=bkc, h=h, hp=hp):
                        last = nc.tensor.matmul(bkc[:, :], lhsT=ident, rhs=biasC[:, h, :], start=True, stop=False)
                        qv = qT[:, s_par, h, :].rearrange("p (g b t) -> p g b t", g=4, b=16)
                        for b in range(NSEQ):
                            last = nc.tensor.matmul(bkc[:, b * 32:(b + 1) * 32], lhsT=KcT[:, b, :], rhs=qv[:, :, b, :],
                                                    start=False, stop=(b == NSEQ - 1))
                        return last
                    S.op("pe", fsc, reads=["cstb", "biasC", "KcT", f"qT{s_par}"], writes=[bkck])
                    S.op("act", lambda bkc=bkc, h=h: nc.scalar.activation(out=PT[:, h * 2, :], in_=bkc[:, :],
                                                                          func=AF.Exp),
                         reads=[bkck], writes=[f"PT{h * 2}"])
                s_bo, s_bok = nextbank()
                s_bd, s_bdk = nextbank()

                def fpv_s(dst, use_v):
                    last = None
                    for h in range(2):
                        hp = slice(h * 64, (h + 1) * 64)
                        lw = Vr[:, 0, hp] if use_v else onesb[:, 0:64]
                        last = nc.tensor.matmul(dst[hp, :], lhsT=lw, rhs=PT[:, h * 2 + 1, :], start=True, stop=False)
                        for b in range(NSEQ):
                            lw = Vc[:, b, hp] if use_v else onesb[:, 0:64]
                            last = nc.tensor.matmul(dst[hp, b * 32:(b + 1) * 32], lhsT=lw,
                                                    rhs=PT[:, h * 2, b * 32:(b + 1) * 32], start=False,
                                                    stop=(b == NSEQ - 1))
                    return last
                S.op("pe", lambda: fpv_s(s_bo, True), reads=["Vr0", "Vc", "PT0", "PT1", "PT2", "PT3"], writes=[s_bok])
                S.op("pe", lambda: fpv_s(s_bd, False), reads=["onesb", "PT0", "PT1", "PT2", "PT3"], writes=[s_bdk])
                attn_finish(s_par, s_bo, s_bok, s_bd, s_bdk, NTILE, perm=True)

                S.stage(5.4)
                decay_prep(s_par, s_xkey, True, True)
                gk_gv(s_par, s_xkey, True)
                gla_proj(s_par, s_xkey)
                gla_AT(s_par, CM_S)
                S.stage(5.5)
                s_bo2 = [nextbank(), nextbank()]
                reserved.update((s_bo2[0][1], s_bo2[1][1]))

                def fzero():
                    last = None
                    for i in range(2):
                        last = nc.tensor.matmul(s_bo2[i][0][:, :], lhsT=zerob[:, :], rhs=cstb[:, CM_P:CM_P + 512],
                                                start=True, stop=False)
                    for h in range(4):
                        for c in range(2):
                            blk = (h * 2 + c) % 4
                            last = nc.tensor.matmul(s_bo2[h // 2][0][:, blk * 128:(blk + 1) * 128],
                                                    lhsT=gv[:, s_par, h * 256 + c * 128: h * 256 + (c + 1) * 128],
                                                    rhs=ATm[:, h * 128:(h + 1) * 128], start=False, stop=False)
                    return last
                S.op("pe", fzero, reads=["zerob", "cstb", f"gv{s_par}", "ATm"], writes=[s_bo2[0][1], s_bo2[1][1]])
                sps = pstride(cstb)
                for b in range(NSEQ):
                    sl = b % 2
                    S.dma("sp", f"s0{sl}", lambda b=b, sl=sl: nc.sync.dma_start(
                        out=S0f[:, sl, 0:1024].rearrange("p (h v) -> p h v", h=4),
                        in_=st0.ap()[b].rearrange("h k v -> k h v")), writes=[f"S0f{sl}"])
                    S.op("act", lambda sl=sl: nc.scalar.copy(out=S0b[:, sl, :], in_=S0f[:, sl, 0:1024]),
                         reads=[f"S0f{sl}"], writes=[f"S0b{sl}"])

                    def fin(b=b, sl=sl):
                        last = None
                        for h in range(4):
                            for c in range(2):
                                blk = (h * 2 + c) % 4
                                last = nc.tensor.matmul(
                                    s_bo2[h // 2][0][:, blk * 128 + b * 8: blk * 128 + (b + 1) * 8],
                                    lhsT=S0b[:, sl, h * 256 + c * 128: h * 256 + (c + 1) * 128],
                                    rhs=qt[:, s_par, h * 128 + b * 8: h * 128 + (b + 1) * 8], start=False,
                                    stop=(b == NSEQ - 1 and h % 2 == 1 and c == 1))
                        return last
                    S.op("pe", fin, reads=[f"S0b{sl}", f"qt{s_par}"], writes=[s_bo2[0][1], s_bo2[1][1]])
                    S.op("dve", lambda b=b, sl=sl: nc.vector.tensor_scalar(
                        out=kendm[:, sl, :], in0=kend[:, s_par, :], scalar1=cstb[:, SEQM + b:SEQM + b + 1], scalar2=None,
                        op0=ALU.mult), reads=[f"kend{s_par}", "cstb"], writes=[f"kendm{sl}"])
                    state_update(s_par, S0f[:, sl, :], f"S0f{sl}", kendm[:, sl, :], f"kendm{sl}", b, 16,
                                 out32=Snew[:, sl, :], outkey=f"Snew{sl}")
                    S.dma("sp", f"so{sl}", lambda b=b, sl=sl: nc.sync.dma_start(
                        out=nss_o.ap()[b].rearrange("h k v -> k h v"),
                        in_=Snew[:, sl, :].rearrange("p (h v) -> p h v", h=4)),
                        reads=[f"Snew{sl}"], writes=[f"nss{b}"])
                S.stage(5.6)
                reserved.clear()
                gla_finish(s_par, s_bo2, NTILE)

                S.barrier()

            S.stage(7)
            load_x(xT_view[:, :, 0:128], 1, "xTb1")

            def kv_proj(par, xkey, slot, last_tile):
                xb = xTb[:, par]
                bk, bkk = nextbank()
                proj_fm(bk, 0, W1, "W_KV", C_K, xb, xkey, 128, 1, bkk)
                S.op("act", lambda: nc.scalar.copy(out=Kr[:, slot, :], in_=bk[:, 0:128]), reads=[bkk],
                     writes=[f"Kr{slot}"])
                bk2, bkk2 = nextbank()

                def fkv():
                    last = None
                    for dc in range(8):
                        last = nc.tensor.matmul(bk2[:, 0:256], lhsT=xb[:, dc, :], rhs=W1[:, dc, C_K:C_K + 256],
                                                start=(dc == 0), stop=(dc == 7))
                    return last
                S.op("pe", fkv, reads=["W_KV", xkey], writes=[bkk2])
                S.op("dve", lambda: nc.vector.tensor_copy(out=Vr[:, slot, :], in_=bk2[:, 128:256]), reads=[bkk2],
                     writes=[f"Vr{slot}"])
                if last_tile and not os.environ.get("KSKIP_NKVP"):
                    S.op("dve", lambda: nc.vector.tensor_copy(out=kvtok[:], in_=bk2[:, 0:256]), reads=[bkk2],
                         writes=["kvtok"])
                    def st_pkv():
                        l = []
                        for b in range(16):
                            l.append(nc.sync.dma_start(out=nkp_o.ap()[b * 8:(b + 1) * 8, :],
                                                       in_=kvtok[b * 8:(b + 1) * 8, 0:128]))
                            l.append(nc.sync.dma_start(out=nvp_o.ap()[b * 8:(b + 1) * 8, :],
                                                       in_=kvtok[b * 8:(b + 1) * 8, 128:256]))
                        return l
                    if not os.environ.get("KSKIP_NKVP2"):
                        S.dma("sp", "nkvp", st_pkv, reads=["kvtok"], writes=["nkp"], n=32)

            kv_proj(1, "xTb1", 0, False)
            load_x(xT_view[:, :, 128:256], 0, "xTb0")
            for j in range(NTILE):
                par = j % 2
                xkey = f"xTb{par}"
                sp_, sc_ = j % 2, (j + 1) % 2
                if j + 1 < NTILE:
                    load_x(xT_view[:, :, 128 + (j + 1) * 128: 256 + (j + 1) * 128], 1 - par, f"xTb{1 - par}")
                if j == 0: S.stage(7.1)
                if j >= 1: S.stage(7.9 + j * 0.002)
                attn_proj(par, xkey, None)
                kv_proj(par, xkey, sc_, j == NTILE - 1)
                if j == 0: S.stage(7.2)
                for h in range(2):
                    hp = slice(h * 64, (h + 1) * 64)
                    for half, (slot, bt) in enumerate(((sp_, biasP0 if j == 0 else biasP), (sc_, biasQ))):
                        bks, bksk = nextbank()

                        def fsc(bks=bks, h=h, hp=hp, slot=slot, bt=bt, par=par):
                            nc.tensor.matmul(bks[:, :], lhsT=ident, rhs=bt[:, h, :], start=True, stop=False)
                            return nc.tensor.matmul(bks[:, :], lhsT=Kr[:, slot, :], rhs=qT[:, par, h, :], start=False,
                                                    stop=True)
                        S.op("pe", fsc, reads=["cstb", "biasP", "biasP0", "biasQ", f"Kr{slot}", f"qT{par}"], writes=[bksk])
                        S.op("act", lambda bks=bks, h=h, half=half: nc.scalar.activation(
                            out=PT[:, h * 2 + half, :], in_=bks[:, :], func=AF.Exp),
                            reads=[bksk], writes=[f"PT{h * 2 + half}"])
                if j == 0: S.stage(7.3)
                bo, bok = nextbank()
                bd, bdk = nextbank()

                def fpv(dst, use_v, sp_=sp_, sc_=sc_):
                    last = None
                    for h in range(2):
                        hp = slice(h * 64, (h + 1) * 64)
                        for half, slot in enumerate((sp_, sc_)):
                            lw = Vr[:, slot, hp] if use_v else onesb[:, 0:64]
                            last = nc.tensor.matmul(dst[hp, :], lhsT=lw, rhs=PT[:, h * 2 + half, :],
                                                    start=(half == 0), stop=(half == 1))
                    return last
                S.op("pe", lambda bo=bo, fpv=fpv: fpv(bo, True),
                     reads=[f"Vr{sp_}", f"Vr{sc_}", "PT0", "PT1", "PT2", "PT3"], writes=[bok])
                S.op("pe", lambda bd=bd, fpv=fpv: fpv(bd, False), reads=["onesb", "PT0", "PT1", "PT2", "PT3"],
                     writes=[bdk])
                if j == 0: S.stage(7.4)
                attn_finish(par, bo, bok, bd, bdk, j)

                if j == 0: S.stage(7.5)
                decay_prep(par, xkey, False, True)
                gk_gv(par, xkey, True)
                gla_proj(par, xkey)
                gla_AT(par, CM_P)
                if j == 0: S.stage(7.6)
                bo2 = [nextbank(), nextbank()]

                def fo(bo2=bo2, par=par):
                    last = None
                    for h in range(4):
                        for c in range(2):
                            blk = (h * 2 + c) % 4
                            dst = bo2[h // 2][0][:, blk * 128:(blk + 1) * 128]
                            nc.tensor.matmul(dst, lhsT=Sb[:, h * 256 + c * 128: h * 256 + (c + 1) * 128],
                                             rhs=qt[:, par, h * 128:(h + 1) * 128], start=True, stop=False)
                            last = nc.tensor.matmul(dst, lhsT=gv[:, par, h * 256 + c * 128: h * 256 + (c + 1) * 128],
                                                    rhs=ATm[:, h * 128:(h + 1) * 128], start=False, stop=True)
                    return last
                S.op("pe", fo, reads=["Sb", f"qt{par}", f"gv{par}", "ATm"], writes=[bo2[0][1], bo2[1][1]])
                if j == 0: S.stage(7.7)
                state_update(par, S32, "S32", kend[:, par, :], f"kend{par}", 0, 1)
                S.op("act", lambda: nc.scalar.copy(out=Sb[:], in_=S32[:, 0:1024]), reads=["S32"],
                     writes=["Sb"])
                if j == 0: S.stage(7.8)
                gla_finish(par, bo2, j)
            S.stage(7.95)
            S.dma("sp", "nsp", lambda: nc.sync.dma_start(
                out=nsp_o.ap().rearrange("h k v -> k h v"), in_=S32[:, 0:1024].rearrange("p (h v) -> p h v", h=4)),
                reads=["S32"], writes=["nsp"])
            S.barrier()

        S.stage(8)
        with ExitStack() as s3:
            Wr = sbt(s3, "Wr", [128, 8, 2048], BF16)
            Wpa = sbt(s3, "Wpa", [128, 4, 1024], BF16)
            Wpg = sbt(s3, "Wpg", [128, 8, 1024], BF16)
            Wo = sbt(s3, "Wo", [128, 8, 1024], BF16)
            stg2 = sbt(s3, "stg2", [128, 2, 1024], F32)
            lng = sbt(s3, "lng", [128, 1024], F32)
            lnb = sbt(s3, "lnb", [128, 1024], F32)
            x2 = sbt(s3, "x2", [128, 2, 8, 512], BF16)
            att2 = sbt(s3, "att2", [128, 2, 4, 512], BF16)
            gla2 = sbt(s3, "gla2", [128, 2, 4, 1024], BF16)
            sa = sbt(s3, "sa", [128, 2, 512], F32)
            sg = sbt(s3, "sg", [128, 2, 512], F32)
            merged = sbt(s3, "merged", [128, 2, 8, 512], BF16)
            xt2 = sbt(s3, "xt2", [128, 2, 1024], F32)
            bnst = sbt(s3, "bnst", [128, 2, 6], F32)
            bnag = sbt(s3, "bnag", [128, 8], F32)

            S.dma("sp", "lnp", lambda: [nc.sync.dma_start(out=lng[:], in_=bass.AP(ln_g, 0, [[0, 128], [1, 1024]])),
                                        nc.sync.dma_start(out=lnb[:], in_=bass.AP(ln_b, 0, [[0, 128], [1, 1024]]))],
                  writes=["lng", "lnb"], n=2)
            wl2 = [0]

            def load_w(src_view, ndc, ncols, dst, dkey, col0=0):
                for dc in range(ndc):
                    for cb in range(0, ncols, 1024):
                        cw = min(1024, ncols - cb)
                        slot = wl2[0] % 2
                        wl2[0] += 1
                        S.dma("sp", f"v{slot}", lambda slot=slot, dc=dc, cb=cb, cw=cw: nc.sync.dma_start(
                            out=stg2[:, slot, 0:cw], in_=src_view[:, dc, col0 + cb: col0 + cb + cw]),
                            writes=[f"stg2{slot}"])
                        eng = ("pool", "dve", "act")[wl2[0] % 3]

                        def fc(slot=slot, dc=dc, cb=cb, cw=cw, eng=eng):
                            o = dst[:, dc, cb:cb + cw]
                            i = stg2[:, slot, 0:cw]
                            if eng == "pool":
                                return nc.gpsimd.tensor_copy(out=o, in_=i)
                            if eng == "dve":
                                return nc.vector.tensor_copy(out=o, in_=i)
                            return nc.scalar.copy(out=o, in_=i)
                        S.op(eng, fc, reads=[f"stg2{slot}"], writes=[dkey])
            w_view = w_in.ap().rearrange("(dc p) n -> p dc n", p=128)
            load_w(w_view, 8, 2048, Wr, "Wr", col0=C_RATT)
            load_w(w_pa.ap().rearrange("(g p) n -> p g n", p=128), 4, 1024, Wpa, "Wpa")
            load_w(w_pg.ap().rearrange("(c p) n -> p c n", p=128), 8, 1024, Wpg, "Wpg")
            load_w(w_o.ap().rearrange("(c p) n -> p c n", p=128), 8, 1024, Wo, "Wo")

            xT_view = xT.ap().rearrange("(dc p) t -> p dc t", p=128)
            xsT_view = xsT.ap().rearrange("(dc p) t -> p dc t", p=128)
            groups = [(s, 4) for s in range(4)] + [(4, 1)]

            def load_group(gi):
                s, nt = groups[gi]
                par = gi % 2
                T = nt * 128
                for dcp in range(4):
                    slot = wl2[0] % 2
                    wl2[0] += 1
                    src = (xT_view[:, dcp * 2:dcp * 2 + 2, 128 + s * 512: 128 + s * 512 + T] if nt == 4
                           else xsT_view[:, dcp * 2:dcp * 2 + 2, :])
                    S.dma("sp", f"v{slot}", lambda slot=slot, src=src, T=T: nc.sync.dma_start(
                        out=stg2[:, slot, 0:2 * T].rearrange("p (a t) -> p a t", a=2), in_=src),
                        writes=[f"stg2{slot}"])
                    S.op("pool", lambda slot=slot, dcp=dcp, T=T, par=par: nc.gpsimd.tensor_copy(
                        out=x2[:, par, dcp * 2:dcp * 2 + 2, 0:T],
                        in_=stg2[:, slot, 0:2 * T].rearrange("p (a t) -> p a t", a=2)),
                        reads=[f"stg2{slot}"], writes=[f"x2{par}"])
                t0 = s * 4
                S.dma("sp", f"la{par}", lambda: [
                    nc.sync.dma_start(out=att2[:, par, 0:nt, :], in_=att_scr.ap()[t0:t0 + nt].rearrange("n p c -> p n c")),
                    nc.sync.dma_start(out=gla2[:, par, 0:nt, :], in_=(gla_scr_s.ap() if t0 == NTILE else gla_scr.ap()[t0:t0 + nt]).rearrange("n p c -> p n c"))],
                    reads=[f"att_scr{t0 + i}" for i in range(nt)] + [f"gla_scr{t0 + i}" for i in range(nt)],
                    writes=[f"att2{par}", f"gla2{par}"], n=2)

            a2s = pstride(att2)
            g2s = pstride(gla2)
            load_group(0)
            xtl = [0]
            for gi, (s, nt) in enumerate(groups):
                par = gi % 2
                T = nt * 128
                if gi + 1 < len(groups):
                    load_group(gi + 1)
                for dc in range(8):
                    dsl = slice(dc * 128, (dc + 1) * 128)
                    bha, bhak = nextbank()

                    def fha(bha=bha, dsl=dsl, par=par, nt=nt, T=T):
                        last = None
                        for g in range(4):
                            rhs = bass.AP(att2, par * 2048 + g * 128, [[a2s, 128], [512, nt], [1, 128]])
                            last = nc.tensor.matmul(bha[:, 0:T],
                                                    lhsT=Wpa[:, g, dsl], rhs=rhs, start=(g == 0), stop=(g == 3))
                        return last
                    S.op("pe", fha, reads=["Wpa", f"att2{par}"], writes=[bhak])
                    bhg, bhgk = nextbank()

                    def fhg(bhg=bhg, dsl=dsl, par=par, nt=nt, T=T):
                        last = None
                        for c in range(8):
                            rhs = bass.AP(gla2, par * 4096 + c * 128, [[g2s, 128], [1024, nt], [1, 128]])
                            last = nc.tensor.matmul(bhg[:, 0:T],
                                                    lhsT=Wpg[:, c, dsl], rhs=rhs, start=(c == 0), stop=(c == 7))
                        return last
                    S.op("pe", fhg, reads=["Wpg", f"gla2{par}"], writes=[bhgk])
                    gates = []
                    for which, dstt in ((0, sa), (1, sg)):
                        bkr, bkrk = nextbank()

                        def fr(bkr=bkr, which=which, dc=dc, par=par, T=T):
                            last = None
                            for d2 in range(8):
                                last = nc.tensor.matmul(
                                    bkr[:, 0:T], lhsT=Wr[:, d2, which * 1024 + dc * 128: which * 1024 + (dc + 1) * 128],
                                    rhs=x2[:, par, d2, 0:T], start=(d2 == 0), stop=(d2 == 7))
                            return last
                        S.op("pe", fr, reads=["Wr", f"x2{par}"], writes=[bkrk])
                        dpar = dc % 2
                        S.op("act", lambda bkr=bkr, dstt=dstt, dpar=dpar, T=T: nc.scalar.activation(
                            out=dstt[:, dpar, 0:T], in_=bkr[:, 0:T], func=AF.Sigmoid),
                            reads=[bkrk], writes=[f"{'sa' if which == 0 else 'sg'}{dpar}"])
                    dpar = dc % 2
                    S.op("dve", lambda bha=bha, dpar=dpar, T=T: nc.vector.tensor_tensor(
                        out=sa[:, dpar, 0:T], in0=bha[:, 0:T], in1=sa[:, dpar, 0:T], op=ALU.mult),
                        reads=[bhak, f"sa{dpar}"], writes=[f"sa{dpar}"])
                    S.op("dve", lambda bhg=bhg, dpar=dpar, T=T: nc.vector.tensor_tensor(
                        out=sg[:, dpar, 0:T], in0=bhg[:, 0:T], in1=sg[:, dpar, 0:T], op=ALU.mult),
                        reads=[bhgk, f"sg{dpar}"], writes=[f"sg{dpar}"])
                    S.op("pool", lambda dpar=dpar, dc=dc, par=par, T=T: nc.gpsimd.tensor_tensor(
                        out=merged[:, par, dc, 0:T], in0=sa[:, dpar, 0:T], in1=sg[:, dpar, 0:T], op=ALU.add),
                        reads=[f"sa{dpar}", f"sg{dpar}"], writes=[f"merged{par}"])
                for i in range(nt):
                    xp = xtl[0] % 2
                    xtl[0] += 1
                    src = xtok.ap()[(s * 4 + i) * 128:(s * 4 + i + 1) * 128, :] if nt == 4 else xstok.ap()
                    dsto = y_o.ap()[(s * 4 + i) * 128:(s * 4 + i + 1) * 128, :] if nt == 4 else ys_o.ap()
                    S.dma("sp", f"xt{xp}", lambda xp=xp, src=src: nc.sync.dma_start(out=xt2[:, xp, :], in_=src),
                          writes=[f"xt2{xp}"])
                    by = [nextbank(), nextbank()]

                    def fy(by=by, par=par, i=i):
                        last = None
                        for nb in range(2):
                            for dc in range(8):
                                last = nc.tensor.matmul(by[nb][0][:, :], lhsT=merged[:, par, dc, i * 128:(i + 1) * 128],
                                                        rhs=Wo[:, dc, nb * 512:(nb + 1) * 512], start=(dc == 0),
                                                        stop=(dc == 7))
                        return last
                    S.op("pe", fy, reads=["Wo", f"merged{par}"], writes=[by[0][1], by[1][1]])
                    for nb in range(2):
                        S.op("dve", lambda xp=xp, nb=nb, by=by: nc.vector.scalar_tensor_tensor(
                            out=xt2[:, xp, nb * 512:(nb + 1) * 512], in0=xt2[:, xp, nb * 512:(nb + 1) * 512],
                            scalar=DN_ALPHA, in1=by[nb][0][:, :], op0=ALU.mult, op1=ALU.add),
                            reads=[f"xt2{xp}", by[nb][1]], writes=[f"xt2{xp}"])

                    def fbn(xp=xp):
                        nc.vector.bn_stats(out=bnst[:, 0, :], in_=xt2[:, xp, 0:512])
                        return nc.vector.bn_stats(out=bnst[:, 1, :], in_=xt2[:, xp, 512:1024])
                    S.op("dve", fbn, reads=[f"xt2{xp}"], writes=["bnst"])
                    S.op("dve", lambda: nc.vector.bn_aggr(out=bnag[:, 0:2], in_=bnst[:].rearrange("p a b -> p (a b)")),
                         reads=["bnst"], writes=["bnag"])
                    S.op("act", lambda: nc.scalar.activation(out=bnag[:, 2:3], in_=bnag[:, 1:2], func=AF.Ln,
                                                             bias=c32[:, 1:2], scale=1.0),
                         reads=["bnag", "c32b"], writes=["bnag"])
                    S.op("act", lambda: nc.scalar.activation(out=bnag[:, 2:3], in_=bnag[:, 2:3], func=AF.Exp,
                                                             scale=-0.5), reads=["bnag"], writes=["bnag"])
                    S.op("dve", lambda: nc.vector.scalar_tensor_tensor(
                        out=bnag[:, 3:4], in0=bnag[:, 0:1], scalar=-1.0, in1=bnag[:, 2:3], op0=ALU.mult,
                        op1=ALU.mult), reads=["bnag"], writes=["bnag"])
                    S.op("act", lambda xp=xp: nc.scalar.activation(out=xt2[:, xp, :], in_=xt2[:, xp, :],
                                                                   func=AF.Identity, bias=bnag[:, 3:4],
                                                                   scale=bnag[:, 2:3]),
                         reads=[f"xt2{xp}", "bnag"], writes=[f"xt2{xp}"])
                    S.op("pool", lambda xp=xp: nc.gpsimd.tensor_tensor(out=xt2[:, xp, :], in0=xt2[:, xp, :],
                                                                       in1=lng[:], op=ALU.mult),
                         reads=[f"xt2{xp}", "lng"], writes=[f"xt2{xp}"])
                    S.op("pool", lambda xp=xp: nc.gpsimd.tensor_tensor(out=xt2[:, xp, :], in0=xt2[:, xp, :],
                                                                       in1=lnb[:], op=ALU.add),
                         reads=[f"xt2{xp}", "lnb"], writes=[f"xt2{xp}"])
                    S.dma("sp", f"yo{xp}", lambda xp=xp, dsto=dsto: nc.sync.dma_start(out=dsto, in_=xt2[:, xp, :]),
                          reads=[f"xt2{xp}"], writes=[f"yout{xp}"])
            S.barrier()
        S.finalize()
    return nc


def _t5_bucket(dist):
    n = np.maximum(dist, 0)
    nf = np.maximum(n, 1).astype(np.float32)
    large = 16 + (np.log(nf / np.float32(16)) / np.float32(math.log(128 / 16)) * np.float32(16)).astype(np.int32)
    large = np.minimum(large, 31)
    return np.where(n < 16, n, large)


def _constants():
    cb = np.zeros((128, NCB), np.float32)
    p = np.arange(128)[:, None]
    t = np.arange(128)[None, :]
    cmp_ = (p <= t).astype(np.float32)
    cb[:, CM_P:CM_P + 512] = np.tile(cmp_, (1, 4))
    same = (p // 8 == t // 8)
    cms = (same & (p <= t)).astype(np.float32)
    cb[:, CM_S:CM_S + 512] = np.tile(cms, (1, 4))
    cb[:, BLKNEG:BLKNEG + 128] = np.where(same, 0.0, NEG)
    cb[:, SEQM:SEQM + 16] = (p // 8 == np.arange(16)[None, :]).astype(np.float32)
    c512 = np.arange(512)[None, :]
    cb[:, RST_P:RST_P + 512] = np.broadcast_to((c512 % 128 != 0).astype(np.float32), (128, 512))
    cb[:, RST_S:RST_S + 512] = np.broadcast_to((c512 % 8 != 0).astype(np.float32), (128, 512))
    cb[:, IDENT:IDENT + 128] = np.eye(128, dtype=np.float32)
    cb = cb.astype(ml_dtypes.bfloat16)
    n = np.arange(384)
    dist = 255 - n
    valid = (dist >= 0) & (dist <= 128) & (n <= 382)
    bucket = _t5_bucket(dist)
    ohr = np.zeros((32, 384), np.float32)
    ohr[bucket[valid], n[valid]] = 1.0
    negr = np.broadcast_to(np.where(valid, 0.0, NEG).astype(np.float32), (8, 384)).copy()
    return cb, ohr, negr


_NC_CACHE = {}


def make_in_maps(x_prompt, x_sample, cache_k, cache_v, state_gla, rel_bias_table, w_in, w_gk_up, b_gk, attn_sink,
                 gla_norm_w, w_pa, w_pg, w_o, ln_g, ln_b):
    f = lambda a: np.ascontiguousarray(np.asarray(a, dtype=np.float32))
    x_prompt, x_sample, cache_k, cache_v, state_gla = map(f, (x_prompt, x_sample, cache_k, cache_v, state_gla))
    w_in0 = f(w_in)[0]
    pq = np.array([(h * 4 + g) * 64 + d for g in range(4) for h in range(2) for d in range(64)])
    perm = np.arange(N_IN)
    perm[C_Q:C_Q + 512] = C_Q + pq
    perm[C_GATT:C_GATT + 512] = C_GATT + pq
    w_in_p = np.ascontiguousarray(w_in0[:, perm])
    w_pa_p = np.ascontiguousarray(f(w_pa)[0][pq, :])
    cb, ohr, negr = _constants()

    xp = x_prompt[0]
    xpT = np.ascontiguousarray(xp.T)
    in_maps = []
    for c in range(NCORES):
        xT_c = np.zeros((D, TPC + 128), np.float32)
        xT_c[:, 128:] = xpT[:, c * TPC:(c + 1) * TPC]
        if c > 0:
            xT_c[:, 0:128] = xpT[:, c * TPC - 128:c * TPC]
        xs_c = x_sample[c * NSEQ:(c + 1) * NSEQ].reshape(128, D)
        ck_c = cache_k[0, c * NSEQ:(c + 1) * NSEQ].reshape(NSEQ, 128, 128)
        cv_c = cache_v[0, c * NSEQ:(c + 1) * NSEQ].reshape(NSEQ, 128, 128)
        mj = np.zeros((128, 8), np.float32)
        xpv = np.zeros((D, 7 * TPC), np.float32)
        if c > 0:
            xpv[:, (7 - c) * TPC:] = xpT[:, 0:c * TPC]
        negr_c = negr.copy()
        hneg = np.full((128, 1), NEG if c == 0 else 0.0, np.float32)
        if c == 0:
            pass
        in_maps.append({
            "xT": xT_c, "xpvT": xpv, "xtok": np.ascontiguousarray(xp[c * TPC:(c + 1) * TPC]),
            "xsT": np.ascontiguousarray(xs_c.T), "xstok": np.ascontiguousarray(xs_c),
            "ckT": np.ascontiguousarray(ck_c.transpose(0, 2, 1)), "ck": np.ascontiguousarray(ck_c),
            "cv": np.ascontiguousarray(cv_c), "st0": np.ascontiguousarray(state_gla[0, c * NSEQ:(c + 1) * NSEQ]),
            "table": f(rel_bias_table), "w_in": w_in_p, "w_up": f(w_gk_up)[0], "b_gk": f(b_gk)[0],
            "sink": f(attn_sink)[0], "gnw": f(gla_norm_w)[0], "w_pa": w_pa_p, "w_pg": f(w_pg)[0], "w_o": f(w_o)[0],
            "ln_g": f(ln_g)[0], "ln_b": f(ln_b)[0], "cstb": cb, "ohr": ohr, "negr": negr_c, "mj": mj, "hneg": hneg,
        })
    return in_maps


def kernel(**inputs):
    in_maps = make_in_maps(**inputs)
    if "nc" not in _NC_CACHE:
        _NC_CACHE["nc"] = build_nc()
    res = run_bass_kernel_spmd(_NC_CACHE["nc"], in_maps, core_ids=list(range(NCORES)))
    R = res.results
    y_prompt = np.concatenate([R[c]["y"] for c in range(NCORES)], 0).reshape(1, 16384, D)
    y_sample = np.concatenate([R[c]["ys"] for c in range(NCORES)], 0).reshape(128, 8, D)
    nkp = R[NCORES - 1]["nkp"].reshape(1, 1, 128, 2, 64)
    nvp = R[NCORES - 1]["nvp"].reshape(1, 1, 128, 2, 64)
    nsp = R[NCORES - 1]["nsp"].reshape(1, 1, 4, 128, 256)
    nks = np.concatenate([R[c]["nks"] for c in range(NCORES)], 0).reshape(1, 128, 128, 2, 64)
    nvs = np.concatenate([R[c]["nvs"] for c in range(NCORES)], 0).reshape(1, 128, 128, 2, 64)
    nss = np.concatenate([R[c]["nss"] for c in range(NCORES)], 0).reshape(1, 128, 4, 128, 256)
    return tuple(np.asarray(a, dtype=np.float32) for a in (y_prompt, y_sample, nkp, nvp, nsp, nks, nvs, nss))
```

```python
import math
import os
from contextlib import ExitStack

import numpy as np
import ml_dtypes

import concourse.bass as bass
import concourse.mybir as mybir
from concourse.bass_utils import run_bass_kernel_spmd

F32 = mybir.dt.float32
BF16 = mybir.dt.bfloat16
AF = mybir.ActivationFunctionType
ALU = mybir.AluOpType

NCORES = 8
D = 1024
TPC = 2048
NTILE = TPC // 128
NSEQ = 16
N_IN = 6416
HK_SCALE = 128 ** -0.5
DN_ALPHA = 2.0 ** 0.25
EPS = 1e-5
NEG = -1e30

C_Q, C_K, C_V, C_GATT, C_GQ, C_GK, C_GV, C_GGLA, C_GLR, C_RATT, C_RGLA = (
    0, 512, 640, 768, 1280, 1792, 2304, 3328, 4352, 4368, 5392)
W1_COLS = 4368

CM_P, CM_S, BLKNEG, SEQM, RST_P, RST_S, IDENT = 0, 512, 1024, 1152, 1168, 1680, 2192
NCB = 2320

ENGS = ("pe", "act", "dve", "pool", "sp")
SEM_CHUNK = 500
DMA_SEM_LIMIT = 512


class Sched:
    def __init__(self, nc, stack):
        self.nc = nc
        self.eng = {"pe": nc.tensor, "act": nc.scalar, "dve": nc.vector, "pool": nc.gpsimd, "sp": nc.sync}
        self.stack = stack
        self.sems = {e: [] for e in ENGS}
        self.count = {e: 0 for e in ENGS}
        self.known = {e: {} for e in ENGS}
        self.last_w = {}
        self.readers = {}
        self.dma_ch = {}
        self.prog = {e: [] for e in ENGS}
        self.retired = []
        self.enabled = True
        self.stop_at = float(os.environ.get("KSTOP", "99"))

    def stage(self, n):
        self.enabled = n <= self.stop_at

    def _newsem(self, name):
        return self.stack.enter_context(self.nc.semaphore(name))

    def _eng_sem(self, e, idx):
        while len(self.sems[e]) <= idx:
            self.sems[e].append(self._newsem(f"s_{e}_{len(self.sems[e])}"))
        return self.sems[e][idx]

    def _wait(self, e, tok):
        _, key, sem, val = tok
        if self.known[e].get(key, 0) >= val:
            return
        self.known[e][key] = val
        eng = self.eng[e]
        self.prog[e].append(lambda: eng.wait_ge(sem, val))

    def _deps(self, e, reads, writes):
        toks = []
        for b in reads:
            t = self.last_w.get(b)
            if t is not None:
                toks.append(t)
        for b in writes:
            t = self.last_w.get(b)
            if t is not None:
                toks.append(t)
            toks.extend(self.readers.get(b, ()))
        for t in toks:
            self._wait(e, t)

    def _record(self, tok, reads, writes):
        for b in reads:
            self.readers.setdefault(b, []).append(tok)
        for b in writes:
            self.last_w[b] = tok
            self.readers[b] = []

    def op(self, e, fn, reads=(), writes=()):
        if not self.enabled:
            return None
        self._deps(e, reads, writes)
        n = self.count[e]
        idx, val = n // SEM_CHUNK, n % SEM_CHUNK + 1
        sem = self._eng_sem(e, idx)
        self.prog[e].append(lambda: fn().then_inc(sem, 1))
        self.count[e] = n + 1
        tok = ("eng", (e, idx), sem, val)
        self._record(tok, reads, writes)
        return tok

    def dma(self, e, ch, fn, reads=(), writes=(), n=1):
        if not self.enabled:
            return None
        if ch not in self.dma_ch:
            self.dma_ch[ch] = [self._newsem(f"d_{ch}"), 0]
        sem, cnt = self.dma_ch[ch]
        if cnt > 0:
            self._wait(e, ("dma", ("dma", ch, id(sem)), sem, cnt))
        if cnt + 16 * n > DMA_SEM_LIMIT:
            self.retired.append((ch, sem, cnt))
            sem, cnt = self._newsem(f"d_{ch}_{len(self.retired)}"), 0
            self.dma_ch[ch] = [sem, cnt]
        self._deps(e, reads, writes)

        def run():
            insts = fn()
            if not isinstance(insts, (list, tuple)):
                insts = [insts]
            assert len(insts) == n, (ch, len(insts), n)
            for i in insts:
                i.then_inc(sem, 16)
        self.prog[e].append(run)
        cnt += 16 * n
        self.dma_ch[ch][1] = cnt
        tok = ("dma", ("dma", ch, id(sem)), sem, cnt)
        self._record(tok, reads, writes)
        return tok

    def wait_all(self, e):
        for ch, (sem, cnt) in list(self.dma_ch.items()) + [(c, (s_, n_)) for c, s_, n_ in self.retired]:
            if cnt:
                self._wait(e, ("dma", ("dma", ch, id(sem)), sem, cnt))
        for e2 in ENGS:
            n = self.count[e2]
            if n:
                idx, val = (n - 1) // SEM_CHUNK, (n - 1) % SEM_CHUNK + 1
                self._wait(e, ("eng", (e2, idx), self.sems[e2][idx], val))

    def barrier(self):
        for e in ENGS:
            self.wait_all(e)

    def finalize(self):
        self.barrier()
        with self.nc.Block() as block:
            reg = {"pe": block.tensor, "act": block.scalar, "dve": block.vector, "pool": block.gpsimd,
                   "sp": block.sync}
            for e in ENGS:
                def body(_eng, _l=self.prog[e]):
                    for t in _l:
                        t()
                reg[e](body)


def build_nc():
    nc = bass.Bass("TRN2", target_bir_lowering=False)

    def din(name, shape, dt=F32):
        return nc.dram_tensor(name, list(shape), dt, kind="ExternalInput")

    def dout(name, shape):
        return nc.dram_tensor(name, list(shape), F32, kind="ExternalOutput")

    xT = din("xT", [D, TPC + 128])
    xtok = din("xtok", [TPC, D])
    xpvT = din("xpvT", [D, 7 * TPC])
    xsT = din("xsT", [D, 128])
    xstok = din("xstok", [128, D])
    ckT = din("ckT", [NSEQ, 128, 128])
    ck = din("ck", [NSEQ, 128, 128])
    cv = din("cv", [NSEQ, 128, 128])
    st0 = din("st0", [NSEQ, 4, 128, 256])
    table = din("table", [32, 8])
    w_in = din("w_in", [D, N_IN])
    w_up = din("w_up", [16, 512])
    b_gk = din("b_gk", [512])
    sink = din("sink", [8])
    gnw_d = din("gnw", [256])
    w_pa = din("w_pa", [512, D])
    w_pg = din("w_pg", [D, D])
    w_o = din("w_o", [D, D])
    ln_g = din("ln_g", [D])
    ln_b = din("ln_b", [D])
    cstb_d = din("cstb", [128, NCB], BF16)
    ohr_d = din("ohr", [32, 384])
    negr_d = din("negr", [8, 384])
    mj_d = din("mj", [128, 8])
    hneg_d = din("hneg", [128, 1])

    y_o = dout("y", [TPC, D])
    ys_o = dout("ys", [128, D])
    nkp_o = dout("nkp", [128, 128])
    nvp_o = dout("nvp", [128, 128])
    nsp_o = dout("nsp", [4, 128, 256])
    nks_o = dout("nks", [NSEQ, 128, 128])
    nvs_o = dout("nvs", [NSEQ, 128, 128])
    nss_o = dout("nss", [NSEQ, 4, 128, 256])

    rs_scr = nc.dram_tensor("rs_scr", [8, 384], F32)
    att_scr = nc.dram_tensor("att_scr", [NTILE + 1, 128, 512], BF16, kind="ExternalOutput")
    gla_scr = nc.dram_tensor("gla_scr", [NTILE, 128, 1024], BF16, kind="ExternalOutput")
    gla_scr_s = nc.dram_tensor("gla_scr_s", [1, 128, 1024], BF16, kind="ExternalOutput")

    with ExitStack() as st:
        S = Sched(nc, st)

        def sbt(stack, name, shape, dt):
            return stack.enter_context(nc.sbuf_tensor("sb_" + name, list(shape), dt))

        def pstride(t):
            return t[:].ap[0][0]

        NB = 7
        banks = [st.enter_context(nc.psum_tensor(f"bk{i}", [128, 512], F32)) for i in range(NB)]
        bankT = st.enter_context(nc.psum_tensor("bkT", [128, 1024], BF16))
        bctr = [0]

        reserved = set()

        def nextbank():
            while True:
                i = bctr[0] % NB
                bctr[0] += 1
                if f"bk{i}" not in reserved:
                    return banks[i], f"bk{i}"

        cstb = sbt(st, "cstb", [128, NCB], BF16)
        onesb = sbt(st, "onesb", [128, 128], BF16)
        zerob = sbt(st, "zerob", [128, 128], BF16)
        c32 = sbt(st, "c32", [128, 4], F32)
        S.dma("sp", "cst", lambda: nc.sync.dma_start(out=cstb[:], in_=cstb_d.ap()), writes=["cstb"])
        S.op("pool", lambda: nc.gpsimd.memset(onesb[:], 1.0), writes=["onesb"])
        S.op("pool", lambda: nc.gpsimd.memset(zerob[:], 0.0), writes=["zerob"])
        S.op("pool", lambda: nc.gpsimd.memset(c32[:, 0:1], 1.0), writes=["c32a"])
        S.op("pool", lambda: nc.gpsimd.memset(c32[:, 1:2], EPS), writes=["c32b"])
        ident = cstb[:, IDENT:IDENT + 128]

        def proj_fm(dst_bank, col0, W, wkey, wcol, xb, xkey, ntok, nchunks, bkey, msize=128):
            def fn():
                last = None
                for c in range(nchunks):
                    for dc in range(8):
                        last = nc.tensor.matmul(
                            dst_bank[0:msize, col0 + c * ntok: col0 + (c + 1) * ntok],
                            lhsT=W[:, dc, wcol + c * 128: wcol + c * 128 + msize],
                            rhs=xb[:, dc, 0:ntok], start=(dc == 0), stop=(dc == 7))
                return last
            S.op("pe", fn, reads=[wkey, xkey], writes=[bkey])

        with ExitStack() as s1:
            W1 = sbt(s1, "W1", [128, 8, W1_COLS], BF16)
            stg = sbt(s1, "stg", [128, 2, 1024], F32)
            xstg = sbt(s1, "xstg", [128, 2, 1024], F32)
            wupb = sbt(s1, "wupb", [16, 512], BF16)
            negb = sbt(s1, "negb", [128, 4], F32)
            gnw = sbt(s1, "gnw", [128, 2], F32)
            esk = sbt(s1, "esk", [128, 4], F32)
            mj = sbt(s1, "mj", [128, 8], F32)
            biasP = sbt(s1, "biasP", [128, 2, 512], BF16)
            biasQ = sbt(s1, "biasQ", [128, 2, 512], BF16)
            biasC = sbt(s1, "biasC", [128, 2, 512], BF16)
            biasN = sbt(s1, "biasN", [128, 2, 512], BF16)
            biasP0 = sbt(s1, "biasP0", [128, 2, 512], BF16)
            hneg = sbt(s1, "hneg", [128, 1], F32)
            s0 = ExitStack()
            hank = sbt(s0, "hank", [128, 2, 128], F32)
            tabl = sbt(s0, "tabl", [32, 8], F32)
            ohr = sbt(s0, "ohr", [32, 384], F32)
            negr = sbt(s0, "negr", [8, 384], F32)
            rr = sbt(s0, "rr", [8, 384], F32)

            wupf = sbt(s0, "wupf", [16, 512], F32)
            S.dma("sp", "p0", lambda: nc.sync.dma_start(out=wupf[:], in_=w_up.ap()), writes=["wupf"])
            S.op("dve", lambda: nc.vector.tensor_copy(out=wupb[:], in_=wupf[:]), reads=["wupf"], writes=["wupb"])

            def ld_small():
                with nc.allow_non_contiguous_dma(reason="tiny parameter vectors"):
                    a = nc.sync.dma_start(out=negb[:], in_=b_gk.ap().rearrange("(h k) -> k h", k=128))
                    b = nc.sync.dma_start(out=gnw[:], in_=gnw_d.ap().rearrange("(c v) -> v c", v=128))
                c = nc.sync.dma_start(out=esk[0:64, :], in_=bass.AP(sink, 0, [[0, 64], [1, 4]]))
                d = nc.sync.dma_start(out=esk[64:128, :], in_=bass.AP(sink, 4, [[0, 64], [1, 4]]))
                e = nc.sync.dma_start(out=mj[:], in_=mj_d.ap())
                f = nc.sync.dma_start(out=tabl[:], in_=table.ap())
                g = nc.sync.dma_start(out=ohr[:], in_=ohr_d.ap())
                h = nc.sync.dma_start(out=negr[:], in_=negr_d.ap())
                i = nc.sync.dma_start(out=hneg[:], in_=hneg_d.ap())
                return [a, b, c, d, e, f, g, h, i]
            S.dma("sp", "p1", ld_small, writes=["negb", "gnw", "esk", "mj", "tabl", "ohr", "negr", "hneg"], n=9)
            S.op("dve", lambda: nc.vector.tensor_scalar(out=negb[:], in0=negb[:], scalar1=-1.0, scalar2=None,
                                                        op0=ALU.mult), reads=["negb"], writes=["negb"])
            S.op("act", lambda: nc.scalar.activation(out=esk[:], in_=esk[:], func=AF.Exp), reads=["esk"],
                 writes=["esk"])

            t5b, t5k = nextbank()
            S.op("pe", lambda: nc.tensor.matmul(t5b[0:8, 0:384], lhsT=tabl[:, :], rhs=ohr[:, :], start=True,
                                                stop=True), reads=["tabl", "ohr"], writes=[t5k])
            S.op("dve", lambda: nc.vector.tensor_tensor(out=rr[:], in0=t5b[0:8, 0:384], in1=negr[:], op=ALU.add),
                 reads=[t5k, "negr"], writes=["rr"])
            S.dma("sp", "rs", lambda: nc.sync.dma_start(out=rs_scr.ap(), in_=rr[:]), reads=["rr"], writes=["rs_scr"])
            hps = pstride(hank)
            for hd in range(8):
                h, g = hd // 4, hd % 4
                for half, (base, dst) in enumerate(((0, biasP), (128, biasQ))):
                    slot = (hd * 2 + half) % 2
                    S.dma("sp", f"hk{slot}",
                          lambda hd=hd, base=base, slot=slot: nc.sync.dma_start(
                              out=hank[:, slot, :], in_=bass.AP(rs_scr, hd * 384 + base, [[1, 128], [1, 128]])),
                          reads=["rs_scr"], writes=[f"hank{slot}"])
                    S.op("dve",
                         lambda dst=dst, h=h, g=g, slot=slot: nc.vector.tensor_copy(
                             out=dst[:, h, g * 128:(g + 1) * 128],
                             in_=bass.AP(hank, slot * 128 + 127, [[hps, 128], [-1, 128]])),
                         reads=[f"hank{slot}"], writes=["bias" + ("P" if half == 0 else "Q")])
            bps = pstride(biasP)
            cps = pstride(cstb)
            for h in range(2):
                S.op("dve", lambda h=h: nc.vector.tensor_copy(
                    out=biasC[:, h, :].rearrange("p (b g t) -> p b g t", b=16, g=4),
                    in_=bass.AP(biasP, h * 512, [[bps, 128], [0, 16], [128, 4], [1, 8]])),
                    reads=["biasP"], writes=["biasC"])
                S.op("dve", lambda h=h: nc.vector.tensor_tensor(
                    out=biasN[:, h, :].rearrange("p (b g t) -> p b g t", b=16, g=4),
                    in0=bass.AP(biasQ, h * 512, [[bps, 128], [8, 16], [128, 4], [1, 8]]),
                    in1=bass.AP(cstb, BLKNEG, [[cps, 128], [8, 16], [0, 4], [1, 8]]), op=ALU.add),
                    reads=["biasQ", "cstb"], writes=["biasN"])
            S.op("dve", lambda: nc.vector.tensor_scalar(out=biasP0[:], in0=biasP[:], scalar1=hneg[:, 0:1],
                                                        scalar2=None, op0=ALU.add),
                 reads=["biasP", "hneg"], writes=["biasP0"])
            S.barrier()
            s0.close()

            S.stage(2)
            w_view = w_in.ap().rearrange("(dc p) n -> p dc n", p=128)
            wblocks = [("GK", C_GK, 512), ("GV", C_GV, 1024), ("GLR", C_GLR, 16), ("Q", C_Q, 512),
                       ("KV", C_K, 256), ("GATT", C_GATT, 512), ("GQ", C_GQ, 512), ("GGLA", C_GGLA, 1024)]
            wl = [0]
            for name, c0, cw in wblocks:
                for dc in range(8):
                    slot = wl[0] % 2
                    wl[0] += 1
                    S.dma("sp", f"w{slot}",
                          lambda slot=slot, dc=dc, c0=c0, cw=cw: nc.sync.dma_start(
                              out=stg[:, slot, 0:cw], in_=w_view[:, dc, c0:c0 + cw]),
                          writes=[f"stg{slot}"])
                    ceng = ("pool", "dve", "act")[wl[0] % 3]

                    def fcw(slot=slot, dc=dc, c0=c0, cw=cw, ceng=ceng):
                        o = W1[:, dc, c0:c0 + cw]
                        i = stg[:, slot, 0:cw]
                        if ceng == "pool":
                            return nc.gpsimd.tensor_copy(out=o, in_=i)
                        if ceng == "dve":
                            return nc.vector.tensor_copy(out=o, in_=i)
                        return nc.scalar.copy(out=o, in_=i)
                    S.op(ceng, fcw, reads=[f"stg{slot}"], writes=[f"W_{name}"])

            nLl = sbt(s1, "nLl", [128, 64], F32)
            edl = sbt(s1, "edl", [128, 64], F32)
            S32 = sbt(s1, "S32", [128, 1028], F32)
            Sb = sbt(s1, "Sb", [128, 1024], BF16)

            S.stage(3)
            with ExitStack() as sA:
                xA = sbt(sA, "xA", [128, 2, 8, 512], BF16)
                ltA = sbt(sA, "ltA", [128, 2048], F32)
                LtA = sbt(sA, "LtA", [128, 2048], F32)
                kendTA = sbt(sA, "kendTA", [128, 2048], BF16)
                kendA = sbt(sA, "kendA", [128, 4, 512], BF16)
                gvA = sbt(sA, "gvA", [128, 4, 1024], BF16)
                glrbA = sbt(sA, "glrbA", [16, 512], BF16)
                xpv_view = xpvT.ap().rearrange("(dc p) t -> p dc t", p=128)
                NSTEP = 7 * TPC // 512
                xl = [0]
                ops1 = pstride(onesb)
                nps_ = pstride(nLl)

                def load_xA(sidx, par):
                    for dcp in range(4):
                        slot = xl[0] % 2
                        xl[0] += 1
                        S.dma("sp", f"x{slot}", lambda slot=slot, dcp=dcp: nc.sync.dma_start(
                            out=xstg[:, slot, :].rearrange("p (a t) -> p a t", a=2),
                            in_=xpv_view[:, dcp * 2:dcp * 2 + 2, sidx * 512:(sidx + 1) * 512]),
                            writes=[f"xstg{slot}"])
                        S.op("pool", lambda slot=slot, dcp=dcp: nc.gpsimd.tensor_copy(
                            out=xA[:, par, dcp * 2:dcp * 2 + 2, :],
                            in_=xstg[:, slot, :].rearrange("p (a t) -> p a t", a=2)),
                            reads=[f"xstg{slot}"], writes=[f"xA{par}"])

                def phaseA_step(sidx, par):
                    xb = xA[:, par]
                    xkey = f"xA{par}"
                    bk, bkk = nextbank()

                    def fglr():
                        last = None
                        for dc in range(8):
                            last = nc.tensor.matmul(bk[0:16, :], lhsT=W1[:, dc, C_GLR:C_GLR + 16], rhs=xb[:, dc, :],
                                                    start=(dc == 0), stop=(dc == 7))
                        return last
                    S.op("pe", fglr, reads=["W_GLR", xkey], writes=[bkk])
                    S.op("dve", lambda: nc.vector.tensor_copy(out=glrbA[:], in_=bk[0:16, :]), reads=[bkk],
                         writes=["glrbA"])
                    for h in range(4):
                        bkh, bkhk = nextbank()
                        S.op("pe", lambda bkh=bkh, h=h: nc.tensor.matmul(
                            bkh[:, :], lhsT=wupb[:, h * 128:(h + 1) * 128], rhs=glrbA[:, :], start=True, stop=True),
                            reads=["wupb", "glrbA"], writes=[bkhk])
                        S.op("act", lambda bkh=bkh, h=h: nc.scalar.activation(
                            out=ltA[:, h * 512:(h + 1) * 512], in_=bkh[:, :], func=AF.Exp, bias=negb[:, h:h + 1],
                            scale=-1.0), reads=[bkhk, "negb"], writes=["ltA"])
                    S.op("act", lambda: nc.scalar.activation(out=ltA[:], in_=ltA[:], func=AF.Ln, bias=c32[:, 0:1],
                                                             scale=1.0), reads=["ltA", "c32a"], writes=["ltA"])

                    for i in range(4):
                        for nb in range(2):
                            bkv, bkvk = nextbank()

                            def fgv(bkv=bkv, nb=nb, i=i):
                                last = None
                                for dc in range(8):
                                    last = nc.tensor.matmul(bkv[:, :], lhsT=xb[:, dc, i * 128:(i + 1) * 128],
                                                            rhs=W1[:, dc, C_GV + nb * 512: C_GV + (nb + 1) * 512],
                                                            start=(dc == 0), stop=(dc == 7))
                                return last
                            S.op("pe", fgv, reads=["W_GV", xkey], writes=[bkvk])
                            S.op("act", lambda bkv=bkv, nb=nb, i=i: nc.scalar.copy(
                                out=gvA[:, i, nb * 512:(nb + 1) * 512], in_=bkv[:, :]), reads=[bkvk], writes=["gvA"])
                    def fscan():
                        last = None
                        for h in range(4):
                            last = nc.vector.tensor_tensor_scan(
                                out=LtA[:, h * 512:(h + 1) * 512], data0=bass.AP(onesb, 0, [[ops1, 128], [0, 512]]),
                                data1=ltA[:, h * 512:(h + 1) * 512], initial=0.0, op0=ALU.mult, op1=ALU.add)
                        return last
                    S.op("dve", fscan, reads=["ltA", "onesb"], writes=["LtA"])
                    lpsA = pstride(LtA)
                    S.op("dve", lambda: nc.vector.tensor_scalar(
                        out=nLl[:, 0:4], in0=bass.AP(LtA, 511, [[lpsA, 128], [512, 4]]), scalar1=-1.0 / 16.0,
                        scalar2=None, op0=ALU.mult), reads=["LtA"], writes=["nLl"])
                    S.op("dve", lambda: nc.vector.scalar_tensor_tensor(
                        out=ltA[:].rearrange("p (g c) -> p g c", g=4), in0=LtA[:].rearrange("p (g c) -> p g c", g=4),
                        scalar=1.0 / 16.0, in1=bass.AP(nLl, 0, [[nps_, 128], [1, 4], [0, 512]]),
                        op0=ALU.mult, op1=ALU.add), reads=["LtA", "nLl"], writes=["ltA"])
                    S.op("act", lambda: nc.scalar.activation(out=ltA[:], in_=ltA[:], func=AF.Exp), reads=["ltA"],
                         writes=["ltA"])
                    S.op("act", lambda: nc.scalar.activation(out=edl[:, 0:4], in_=nLl[:, 0:4], func=AF.Exp),
                         reads=["nLl"], writes=["edl"])
                    for h in range(4):
                        bkh, bkhk = nextbank()

                        def fgk(bkh=bkh, h=h):
                            last = None
                            for dc in range(8):
                                last = nc.tensor.matmul(bkh[:, :], lhsT=W1[:, dc, C_GK + h * 128: C_GK + (h + 1) * 128],
                                                        rhs=xb[:, dc, :], start=(dc == 0), stop=(dc == 7))
                            return last
                        S.op("pe", fgk, reads=["W_GK", xkey], writes=[bkhk])
                        S.op("dve", lambda bkh=bkh, h=h: nc.vector.tensor_tensor(
                            out=kendTA[:, h * 512:(h + 1) * 512], in0=bkh[:, :], in1=ltA[:, h * 512:(h + 1) * 512],
                            op=ALU.mult), reads=[bkhk, "ltA"], writes=["kendTA"])
                    for pr in range(2):
                        def ftr(pr=pr):
                            last = None
                            for ii in range(2):
                                i = pr * 2 + ii
                                for h in range(4):
                                    last = nc.tensor.transpose(
                                        bankT[:, ii * 512 + h * 128: ii * 512 + (h + 1) * 128],
                                        kendTA[:, h * 512 + i * 128: h * 512 + (i + 1) * 128], ident)
                            return last
                        S.op("pe", ftr, reads=["kendTA", "cstb"], writes=["bkT"])
                        S.op("act", lambda pr=pr: nc.scalar.copy(
                            out=kendA[:, pr * 2:pr * 2 + 2, :].rearrange("p a c -> p (a c)"), in_=bankT[:, 0:1024]),
                            reads=["bkT"], writes=["kendA"])
                    b2 = [nextbank(), nextbank()]

                    def fds():
                        last = None
                        for h in range(4):
                            bk_ = b2[h // 2][0]
                            for i in range(4):
                                last = nc.tensor.matmul(bk_[:, (h % 2) * 256:(h % 2) * 256 + 256],
                                                        lhsT=kendA[:, i, h * 128:(h + 1) * 128],
                                                        rhs=gvA[:, i, h * 256:(h + 1) * 256], start=(i == 0),
                                                        stop=(i == 3))
                        return last
                    S.op("pe", fds, reads=["kendA", "gvA"], writes=[b2[0][1], b2[1][1]])

                    def fu():
                        last = None
                        for h in range(4):
                            bk_ = b2[h // 2][0]
                            last = nc.vector.scalar_tensor_tensor(
                                out=S32[:, h * 256:(h + 1) * 256], in0=S32[:, h * 256:(h + 1) * 256],
                                scalar=edl[:, h:h + 1], in1=bk_[:, (h % 2) * 256:(h % 2) * 256 + 256],
                                op0=ALU.mult, op1=ALU.add)
                        return last
                    S.op("dve", fu, reads=["S32", "edl", b2[0][1], b2[1][1]], writes=["S32"])

                S.op("pool", lambda: nc.gpsimd.memset(S32[:], 0.0), writes=["S32"])
                load_xA(0, 0)
                for sidx in range(NSTEP):
                    par = sidx % 2
                    if sidx + 1 < NSTEP:
                        load_xA(sidx + 1, 1 - par)
                    phaseA_step(sidx, par)
                S.op("pool", lambda: nc.gpsimd.tensor_copy(out=Sb[:], in_=S32[:, 0:1024]), reads=["S32"],
                     writes=["Sb"])
                S.barrier()

            xTb = sbt(s1, "xTb", [128, 2, 8, 128], BF16)
            qT = sbt(s1, "qT", [128, 2, 2, 512], BF16)
            gattS = sbt(s1, "gattS", [128, 2, 512], BF16)
            Kr = sbt(s1, "Kr", [128, 2, 128], BF16)
            Vr = sbt(s1, "Vr", [128, 2, 128], BF16)
            PT = sbt(s1, "PT", [128, 4, 512], BF16)
            lnd = sbt(s1, "lnd", [128, 512], F32)
            actatt = sbt(s1, "actatt", [128, 2, 512], BF16)
            glrb = sbt(s1, "glrb", [16, 128], BF16)
            lt = sbt(s1, "lt", [128, 512], F32)
            Lt = sbt(s1, "Lt", [128, 512], F32)
            ekd = sbt(s1, "ekd", [128, 512], F32)
            eb = sbt(s1, "eb", [128, 512], F32)
            enb = sbt(s1, "enb", [128, 512], F32)
            qt = sbt(s1, "qt", [128, 2, 512], BF16)
            kt = sbt(s1, "kt", [128, 2, 512], BF16)
            kendT = sbt(s1, "kendT", [128, 512], BF16)
            kend = sbt(s1, "kend", [128, 2, 512], BF16)
            gv = sbt(s1, "gv", [128, 2, 1024], BF16)
            gglaS = sbt(s1, "gglaS", [128, 2, 1024], BF16)
            ATm = sbt(s1, "ATm", [128, 512], BF16)
            osq = sbt(s1, "osq", [128, 1024], BF16)
            rstd = sbt(s1, "rstd", [128, 512], F32)
            rg2 = sbt(s1, "rg2", [128, 1024], BF16)
            actgla = sbt(s1, "actgla", [128, 2, 1024], BF16)
            kvtok = sbt(s1, "kvtok", [128, 256], F32)

            S.op("pool", lambda: nc.gpsimd.memset(qT[:].rearrange("p a h c -> p (a h c)"), 0.0),
                 writes=["qT0", "qT1"])

            def load_x(src_view, par, key):
                S.dma("sp", f"x{par}", lambda: nc.sync.dma_start(
                    out=xstg[:, par, :].rearrange("p (dc t) -> p dc t", dc=8), in_=src_view),
                    writes=[f"xstg{par}"])
                S.op("pool", lambda: nc.gpsimd.tensor_copy(
                    out=xTb[:, par].rearrange("p dc t -> p (dc t)"), in_=xstg[:, par, :]),
                    reads=[f"xstg{par}"], writes=[key])

            xT_view = xT.ap().rearrange("(dc p) t -> p dc t", p=128)
            xsT_view = xsT.ap().rearrange("(dc p) t -> p dc t", p=128)

            def decay_prep(par, xkey, sample, need_q):
                xb = xTb[:, par]
                bk, bkk = nextbank()
                proj_fm(bk, 0, W1, "W_GLR", C_GLR, xb, xkey, 128, 1, bkk, msize=16)
                S.op("dve", lambda: nc.vector.tensor_copy(out=glrb[:], in_=bk[0:16, 0:128]), reads=[bkk],
                     writes=["glrb"])
                bk2, bkk2 = nextbank()

                def fn():
                    last = None
                    for h in range(4):
                        last = nc.tensor.matmul(bk2[:, h * 128:(h + 1) * 128], lhsT=wupb[:, h * 128:(h + 1) * 128],
                                                rhs=glrb[:, :], start=True, stop=True)
                    return last
                S.op("pe", fn, reads=["wupb", "glrb"], writes=[bkk2])

                def fe():
                    last = None
                    for h in range(4):
                        last = nc.scalar.activation(out=lt[:, h * 128:(h + 1) * 128], in_=bk2[:, h * 128:(h + 1) * 128],
                                                    func=AF.Exp, bias=negb[:, h:h + 1], scale=-1.0)
                    return last
                S.op("act", fe, reads=[bkk2, "negb"], writes=["lt"])
                S.op("act", lambda: nc.scalar.activation(out=lt[:], in_=lt[:], func=AF.Ln, bias=c32[:, 0:1],
                                                         scale=1.0), reads=["lt", "c32a"], writes=["lt"])
                rst = RST_S if sample else RST_P
                S.op("dve", lambda: nc.vector.tensor_tensor_scan(
                    out=Lt[:], data0=cstb[:, rst:rst + 512], data1=lt[:], initial=0.0, op0=ALU.mult, op1=ALU.add),
                    reads=["lt", "cstb"], writes=["Lt"])
                ng = 64 if sample else 4
                cl = 512 // ng
                lps = pstride(Lt)
                S.op("dve", lambda: nc.vector.tensor_scalar(
                    out=nLl[:, 0:ng], in0=bass.AP(Lt, cl - 1, [[lps, 128], [cl, ng]]), scalar1=-1.0 / 16.0,
                    scalar2=None, op0=ALU.mult), reads=["Lt"], writes=["nLl"])
                nps = pstride(nLl)
                S.op("dve", lambda: nc.vector.scalar_tensor_tensor(
                    out=ekd[:].rearrange("p (g c) -> p g c", g=ng), in0=Lt[:].rearrange("p (g c) -> p g c", g=ng),
                    scalar=1.0 / 16.0, in1=bass.AP(nLl, 0, [[nps, 128], [1, ng], [0, cl]]),
                    op0=ALU.mult, op1=ALU.add), reads=["Lt", "nLl"], writes=["ekd"])
                S.op("act", lambda: nc.scalar.activation(out=ekd[:], in_=ekd[:], func=AF.Exp), reads=["ekd"],
                     writes=["ekd"])
                S.op("act", lambda: nc.scalar.activation(out=edl[:, 0:ng], in_=nLl[:, 0:ng], func=AF.Exp),
                     reads=["nLl"], writes=["edl"])
                if need_q:
                    S.op("act", lambda: nc.scalar.activation(out=eb[:], in_=Lt[:], func=AF.Exp, scale=-1.0 / 16.0),
                         reads=["Lt"], writes=["eb"])
                    S.op("act", lambda: nc.scalar.activation(out=enb[:], in_=Lt[:], func=AF.Exp, scale=1.0 / 16.0),
                         reads=["Lt"], writes=["enb"])

            def gk_gv(par, xkey, need_q, defer_tr=False):
                xb = xTb[:, par]
                bk, bkk = nextbank()
                proj_fm(bk, 0, W1, "W_GK", C_GK, xb, xkey, 128, 4, bkk)
                S.op("dve", lambda: nc.vector.tensor_tensor(out=kendT[:], in0=bk[:, :], in1=ekd[:], op=ALU.mult),
                     reads=[bkk, "ekd"], writes=["kendT"])
                if need_q:
                    S.op("dve", lambda: nc.vector.tensor_tensor(out=kt[:, par, :], in0=bk[:, :], in1=enb[:],
                                                                op=ALU.mult),
                         reads=[bkk, "enb"], writes=[f"kt{par}"])

                def ftr():
                    last = None
                    for h in range(4):
                        last = nc.tensor.transpose(bankT[:, h * 128:(h + 1) * 128], kendT[:, h * 128:(h + 1) * 128],
                                                   ident)
                    return last

                def do_tr():
                    S.op("pe", ftr, reads=["kendT", "cstb"], writes=["bkT"])
                    S.op("act", lambda: nc.scalar.copy(out=kend[:, par, :], in_=bankT[:, 0:512]), reads=["bkT"],
                         writes=[f"kend{par}"])
                if not defer_tr:
                    do_tr()
                for nb in range(2):
                    bkv, bkvk = nextbank()

                    def fgv(bkv=bkv, nb=nb):
                        last = None
                        for dc in range(8):
                            last = nc.tensor.matmul(bkv[:, :], lhsT=xb[:, dc, :],
                                                    rhs=W1[:, dc, C_GV + nb * 512: C_GV + (nb + 1) * 512],
                                                    start=(dc == 0), stop=(dc == 7))
                        return last
                    S.op("pe", fgv, reads=["W_GV", xkey], writes=[bkvk])
                    S.op("act", lambda bkv=bkv, nb=nb: nc.scalar.copy(out=gv[:, par, nb * 512:(nb + 1) * 512],
                                                                      in_=bkv[:, :]),
                         reads=[bkvk], writes=[f"gv{par}"])
                return do_tr if defer_tr else None

            def state_update(par, st32, stkey, kend_ap, kendkey, edl_col0, edl_step, out32=None, outkey=None):
                if out32 is None:
                    out32, outkey = st32, stkey
                b2 = [nextbank(), nextbank()]

                def fn():
                    last = None
                    for h in range(4):
                        bk_ = b2[h // 2][0]
                        last = nc.tensor.matmul(bk_[:, (h % 2) * 256:(h % 2) * 256 + 256],
                                                lhsT=kend_ap[:, h * 128:(h + 1) * 128],
                                                rhs=gv[:, par, h * 256:(h + 1) * 256], start=True, stop=True)
                    return last
                S.op("pe", fn, reads=[kendkey, f"gv{par}"], writes=[b2[0][1], b2[1][1]])

                def fu():
                    last = None
                    for h in range(4):
                        bk_ = b2[h // 2][0]
                        c = edl_col0 + h * edl_step
                        last = nc.vector.scalar_tensor_tensor(
                            out=out32[:, h * 256:(h + 1) * 256], in0=st32[:, h * 256:(h + 1) * 256],
                            scalar=edl[:, c:c + 1], in1=bk_[:, (h % 2) * 256:(h % 2) * 256 + 256],
                            op0=ALU.mult, op1=ALU.add)
                    return last
                S.op("dve", fu, reads=[stkey, "edl", b2[0][1], b2[1][1]], writes=[outkey])

            def attn_proj(par, xkey, ntv_key):
                xb = xTb[:, par]
                bk, bkk = nextbank()
                proj_fm(bk, 0, W1, "W_Q", C_Q, xb, xkey, 128, 4, bkk)
                def fq():
                    nc.scalar.activation(out=qT[0:64, par, 0, :], in_=bk[0:64, :], func=AF.Copy, scale=0.125)
                    return nc.scalar.activation(out=qT[64:128, par, 1, :], in_=bk[64:128, :], func=AF.Copy, scale=0.125)
                S.op("act", fq, reads=[bkk], writes=[f"qT{par}"])
                bk2, bkk2 = nextbank()
                proj_fm(bk2, 0, W1, "W_GATT", C_GATT, xb, xkey, 128, 4, bkk2)
                S.op("act", lambda: nc.scalar.activation(out=gattS[:, par, :], in_=bk2[:, :], func=AF.Silu),
                     reads=[bkk2], writes=[f"gattS{par}"])

            def attn_finish(par, bo, bok, bd, bdk, scr_idx, perm=False):
                def fl():
                    last = None
                    for g in range(4):
                        src = bd[:, g * 128:(g + 1) * 128]
                        dstv = lnd[:, g * 128:(g + 1) * 128]
                        if perm:
                            src = bd[:, :].rearrange("p (b g t) -> p g b t", b=16, g=4)[:, g, :, :]
                            dstv = dstv.rearrange("p (b t) -> p b t", b=16)
                        last = nc.scalar.activation(out=dstv, in_=src, func=AF.Ln, bias=esk[:, g:g + 1], scale=1.0)
                    return last
                S.op("act", fl, reads=[bdk, "esk"], writes=["lnd"])
                S.op("act", lambda: nc.scalar.activation(out=lnd[:], in_=lnd[:], func=AF.Exp, scale=-1.0),
                     reads=["lnd"], writes=["lnd"])
                S.op("pool", lambda: nc.gpsimd.tensor_tensor(out=lnd[:], in0=lnd[:], in1=gattS[:, par, :],
                                                             op=ALU.mult),
                     reads=["lnd", f"gattS{par}"], writes=["lnd"])
                def fm():
                    if not perm:
                        return nc.vector.tensor_tensor(out=actatt[:, par, :], in0=bo[:, :], in1=lnd[:], op=ALU.mult)
                    last = None
                    for g in range(4):
                        last = nc.vector.tensor_tensor(
                            out=actatt[:, par, g * 128:(g + 1) * 128].rearrange("p (b t) -> p b t", b=16),
                            in0=bo[:, :].rearrange("p (b g t) -> p g b t", b=16, g=4)[:, g, :, :],
                            in1=lnd[:, g * 128:(g + 1) * 128].rearrange("p (b t) -> p b t", b=16), op=ALU.mult)
                    return last
                S.op("dve", fm, reads=[bok, "lnd"], writes=[f"actatt{par}"])
                if not os.environ.get("KSKIP_SA"):
                  S.dma("sp", f"sa{par}", lambda: nc.sync.dma_start(out=att_scr.ap()[scr_idx], in_=actatt[:, par, :]),
                      reads=[f"actatt{par}"], writes=[f"att_scr{scr_idx}"])

            def gla_proj(par, xkey):
                xb = xTb[:, par]
                bk, bkk = nextbank()
                proj_fm(bk, 0, W1, "W_GQ", C_GQ, xb, xkey, 128, 4, bkk)
                S.op("dve", lambda: nc.vector.scalar_tensor_tensor(out=qt[:, par, :], in0=bk[:, :], scalar=HK_SCALE,
                                                                   in1=eb[:], op0=ALU.mult, op1=ALU.mult),
                     reads=[bkk, "eb"], writes=[f"qt{par}"])
                for half in range(2):
                    bk2, bkk2 = nextbank()
                    proj_fm(bk2, 0, W1, "W_GGLA", C_GGLA + half * 512, xb, xkey, 128, 4, bkk2)
                    S.op("act", lambda bk2=bk2, half=half: nc.scalar.activation(
                        out=gglaS[:, par, half * 512:(half + 1) * 512], in_=bk2[:, :], func=AF.Silu),
                        reads=[bkk2], writes=[f"gglaS{par}"])

                def fw():
                    last = None
                    v = gglaS[:, par, :].rearrange("p (h c t) -> p h c t", h=4, c=2)
                    for c in range(2):
                        last = nc.gpsimd.tensor_scalar(out=v[:, :, c, :], in0=v[:, :, c, :], scalar1=gnw[:, c:c + 1],
                                                       scalar2=None, op0=ALU.mult)
                    return last
                S.op("pool", fw, reads=[f"gglaS{par}", "gnw"], writes=[f"gglaS{par}"])

            def gla_AT(par, cm):
                bk, bkk = nextbank()

                def fn():
                    last = None
                    for h in range(4):
                        last = nc.tensor.matmul(bk[:, h * 128:(h + 1) * 128], lhsT=kt[:, par, h * 128:(h + 1) * 128],
                                                rhs=qt[:, par, h * 128:(h + 1) * 128], start=True, stop=True)
                    return last
                S.op("pe", fn, reads=[f"kt{par}", f"qt{par}"], writes=[bkk])
                S.op("dve", lambda: nc.vector.tensor_tensor(out=ATm[:], in0=bk[:, :], in1=cstb[:, cm:cm + 512],
                                                            op=ALU.mult), reads=[bkk, "cstb"], writes=["ATm"])

            def gla_finish(par, bo2, scr_idx):
                for i in range(2):
                    S.op("act", lambda i=i: nc.scalar.activation(out=osq[:, i * 512:(i + 1) * 512], in_=bo2[i][0][:, :],
                                                                 func=AF.Square),
                         reads=[bo2[i][1]], writes=["osq"])
                if scr_idx == NTILE: S.stage(6.1)
                bs, bsk = nextbank()
                ops_ = pstride(osq)

                def fs():
                    last = None
                    for c in range(2):
                        last = nc.tensor.matmul(bs[:, :], lhsT=onesb[:, :],
                                                rhs=bass.AP(osq, c * 128, [[ops_, 128], [256, 4], [1, 128]]),
                                                start=(c == 0), stop=(c == 1))
                    return last
                S.op("pe", fs, reads=["osq", "onesb"], writes=[bsk])
                if scr_idx == NTILE: S.stage(6.2)
                S.op("act", lambda: nc.scalar.activation(out=rstd[:], in_=bs[:, :], func=AF.Ln, bias=c32[:, 1:2],
                                                         scale=1.0 / 256.0), reads=[bsk, "c32b"], writes=["rstd"])
                S.op("act", lambda: nc.scalar.activation(out=rstd[:], in_=rstd[:], func=AF.Exp, scale=-0.5),
                     reads=["rstd"], writes=["rstd"])

                if scr_idx == NTILE: S.stage(6.3)

                def fr():
                    last = None
                    gvw = gglaS[:, par, :].rearrange("p (h c t) -> p h c t", h=4, c=2)
                    rv = rg2[:].rearrange("p (h c t) -> p h c t", h=4, c=2)
                    for c in range(2):
                        last = nc.gpsimd.tensor_tensor(out=rv[:, :, c, :], in0=gvw[:, :, c, :],
                                                       in1=rstd[:].rearrange("p (h t) -> p h t", h=4), op=ALU.mult)
                    return last
                S.op("pool", fr, reads=[f"gglaS{par}", "rstd"], writes=["rg2"])
                if scr_idx == NTILE: S.stage(6.4)
                for i in range(2):
                    S.op("dve", lambda i=i: nc.vector.tensor_tensor(
                        out=actgla[:, par, i * 512:(i + 1) * 512], in0=bo2[i][0][:, :],
                        in1=rg2[:, i * 512:(i + 1) * 512], op=ALU.mult),
                        reads=[bo2[i][1], "rg2"], writes=[f"actgla{par}"])
                if scr_idx == NTILE: S.stage(6.5)
                S.dma(("pool" if os.environ.get("KPOOLQ") else "sp"), (f"sa{par}" if os.environ.get("KCH") else f"sg{par}"), lambda: [
                    (nc.gpsimd if os.environ.get("KPOOLQ") else nc.sync).dma_start(out=(gla_scr_s.ap()[0] if scr_idx == NTILE else gla_scr.ap()[scr_idx])[:, i * 512:(i + 1) * 512],
                                      in_=(gv if os.environ.get("KSRC") else actgla)[:, par, i * 512:(i + 1) * 512]) for i in range(2)],
                      reads=([] if os.environ.get("KNODEP") else [f"actgla{par}"]), writes=[f"gla_scr{scr_idx}"], n=2)

            S.stage(5)
            with ExitStack() as s2:
                KcT = sbt(s2, "KcT", [128, 16, 128], BF16)
                Vc = sbt(s2, "Vc", [128, 16, 128], BF16)
                S0f = sbt(s2, "S0f", [128, 2, 1028], F32)
                S0b = sbt(s2, "S0b", [128, 2, 1024], BF16)
                kendm = sbt(s2, "kendm", [128, 2, 512], BF16)
                Snew = sbt(s2, "Snew", [128, 2, 1024], F32)

                s_par = 0
                load_x(xsT_view, s_par, "xTb0")
                s_xkey = "xTb0"
                s_xb = xTb[:, s_par]
                if os.environ.get("KSKIP_CACHE"):
                    S.enabled = False
                S.dma("sp", "w0", lambda: nc.sync.dma_start(
                    out=stg[:].rearrange("p a (b j) -> p (a b) j", j=128), in_=ckT.ap().rearrange("b p j -> p b j")),
                    writes=["stg0", "stg1"])
                S.op("pool", lambda: nc.gpsimd.tensor_copy(out=KcT[:].rearrange("p b j -> p (b j)"),
                                                           in_=stg[:].rearrange("p a c -> p (a c)")),
                     reads=["stg0", "stg1"], writes=["KcT"])
                S.dma("sp", "w0", lambda: nc.sync.dma_start(
                    out=stg[:].rearrange("p a (b j) -> p (a b) j", j=128), in_=cv.ap().rearrange("b j c -> j b c")),
                    writes=["stg0", "stg1"])
                S.op("pool", lambda: nc.gpsimd.tensor_copy(out=Vc[:].rearrange("p b j -> p (b j)"),
                                                           in_=stg[:].rearrange("p a c -> p (a c)")),
                     reads=["stg0", "stg1"], writes=["Vc"])
                if os.environ.get("KSKIP_CACHE"):
                    S.enabled = True
                if not os.environ.get("KSKIP_CP"):
                  S.dma("sp", "cpk", lambda: nc.sync.dma_start(out=nks_o.ap()[:, 0:120, :], in_=ck.ap()[:, 8:128, :]),
                      writes=["nks_a"])
                if not os.environ.get("KSKIP_CP"):
                  S.dma("sp", "cpv", lambda: nc.sync.dma_start(out=nvs_o.ap()[:, 0:120, :], in_=cv.ap()[:, 8:128, :]),
                      writes=["nvs_a"])

                S.stage(5.2)
                attn_proj(s_par, s_xkey, None)
                s_bk, s_bkk = nextbank()
                proj_fm(s_bk, 0, W1, "W_KV", C_K, s_xb, s_xkey, 128, 1, s_bkk)
                S.op("act", lambda: nc.scalar.copy(out=Kr[:, 0, :], in_=s_bk[:, 0:128]), reads=[s_bkk], writes=["Kr0"])
                s_bk2, s_bkk2 = nextbank()

                def fkv():
                    last = None
                    for dc in range(8):
                        last = nc.tensor.matmul(s_bk2[:, 0:256], lhsT=s_xb[:, dc, :], rhs=W1[:, dc, C_K:C_K + 256],
                                                start=(dc == 0), stop=(dc == 7))
                    return last
                S.op("pe", fkv, reads=["W_KV", s_xkey], writes=[s_bkk2])
                S.op("act", lambda: nc.scalar.copy(out=kvtok[:], in_=s_bk2[:, 0:256]), reads=[s_bkk2], writes=["kvtok"])
                S.op("dve", lambda: nc.vector.tensor_copy(out=Vr[:, 0, :], in_=s_bk2[:, 128:256]), reads=[s_bkk2],
                     writes=["Vr0"])

                def st_newkv():
                    l = []
                    for b in range(NSEQ):
                        l.append(nc.sync.dma_start(out=nks_o.ap()[b, 120:128, :], in_=kvtok[b * 8:(b + 1) * 8, 0:128]))
                        l.append(nc.sync.dma_start(out=nvs_o.ap()[b, 120:128, :], in_=kvtok[b * 8:(b + 1) * 8, 128:256]))
                    return l
                S.dma("sp", "nkv", st_newkv, reads=["kvtok"], writes=["nks_b"], n=2 * NSEQ)

                S.stage(5.3)
                scb = []
                for h in range(2):
                    hp = slice(h * 64, (h + 1) * 64)
                    bkn, bknk = nextbank()

                    def fsn(bkn=bkn, h=h, hp=hp):
                        nc.tensor.matmul(bkn[:, :], lhsT=ident, rhs=biasN[:, h, :], start=True, stop=False)
                        return nc.tensor.matmul(bkn[:, :], lhsT=Kr[:, 0, :],
                                                rhs=qT[:, s_par, h, :].rearrange("p (g b t) -> p b g t", g=4, b=16),
                                                start=False, stop=True)
                    S.op("pe", fsn, reads=["cstb", "biasN", "Kr0", f"qT{s_par}"], writes=[bknk])
                    S.op("act", lambda bkn=bkn, h=h: nc.scalar.activation(out=PT[:, h * 2 + 1, :], in_=bkn[:, :],
                                                                          func=AF.Exp),
                         reads=[bknk], writes=[f"PT{h * 2 + 1}"])
                    bkc, bkck = nextbank()

                    def fsc(bkc=bkc, h=h, hp=hp):
                        last = nc.tensor.matmul(bkc[:, :], lhsT=ident, rhs=biasC[:, h, :], start=True, stop=False)
                        qv = qT[:, s_par, h, :].rearrange("p (g b t) -> p g b t", g=4, b=16)
                        for b in range(NSEQ):
                            last = nc.tensor.matmul(bkc[:, b * 32:(b + 1) * 32], lhsT=KcT[:, b, :], rhs=qv[:, :, b, :],
                                                    start=False, stop=(b == NSEQ - 1))
                        return last
                    S.op("pe", fsc, reads=["cstb", "biasC", "KcT", f"qT{s_par}"], writes=[bkck])
                    S.op("act", lambda bkc=bkc, h=h: nc.scalar.activation(out=PT[:, h * 2, :], in_=bkc[:, :],
                                                                          func=AF.Exp),
                         reads=[bkck], writes=[f"PT{h * 2}"])
                s_bo, s_bok = nextbank()
                s_bd, s_bdk = nextbank()

                def fpv_s(dst, use_v):
                    last = None
                    for h in range(2):
                        hp = slice(h * 64, (h + 1) * 64)
                        lw = Vr[:, 0, hp] if use_v else onesb[:, 0:64]
                        last = nc.tensor.matmul(dst[hp, :], lhsT=lw, rhs=PT[:, h * 2 + 1, :], start=True, stop=False)
                        for b in range(NSEQ):
                            lw = Vc[:, b, hp] if use_v else onesb[:, 0:64]
                            last = nc.tensor.matmul(dst[hp, b * 32:(b + 1) * 32], lhsT=lw,
                                                    rhs=PT[:, h * 2, b * 32:(b + 1) * 32], start=False,
                                                    stop=(b == NSEQ - 1))
                    return last
                S.op("pe", lambda: fpv_s(s_bo, True), reads=["Vr0", "Vc", "PT0", "PT1", "PT2", "PT3"], writes=[s_bok])
                S.op("pe", lambda: fpv_s(s_bd, False), reads=["onesb", "PT0", "PT1", "PT2", "PT3"], writes=[s_bdk])
                attn_finish(s_par, s_bo, s_bok, s_bd, s_bdk, NTILE, perm=True)

                S.stage(5.4)
                decay_prep(s_par, s_xkey, True, True)
                gk_gv(s_par, s_xkey, True)
                gla_proj(s_par, s_xkey)
                gla_AT(s_par, CM_S)
                S.stage(5.5)
                s_bo2 = [nextbank(), nextbank()]
                reserved.update((s_bo2[0][1], s_bo2[1][1]))

                def fzero():
                    last = None
                    for i in range(2):
                        last = nc.tensor.matmul(s_bo2[i][0][:, :], lhsT=zerob[:, :], rhs=cstb[:, CM_P:CM_P + 512],
                                                start=True, stop=False)
                    for h in range(4):
                        for c in range(2):
                            blk = (h * 2 + c) % 4
                            last = nc.tensor.matmul(s_bo2[h // 2][0][:, blk * 128:(blk + 1) * 128],
                                                    lhsT=gv[:, s_par, h * 256 + c * 128: h * 256 + (c + 1) * 128],
                                                    rhs=ATm[:, h * 128:(h + 1) * 128], start=False, stop=False)
                    return last
                S.op("pe", fzero, reads=["zerob", "cstb", f"gv{s_par}", "ATm"], writes=[s_bo2[0][1], s_bo2[1][1]])
                sps = pstride(cstb)
                for b in range(NSEQ):
                    sl = b % 2
                    S.dma("sp", f"s0{sl}", lambda b=b, sl=sl: nc.sync.dma_start(
                        out=S0f[:, sl, 0:1024].rearrange("p (h v) -> p h v", h=4),
                        in_=st0.ap()[b].rearrange("h k v -> k h v")), writes=[f"S0f{sl}"])
                    S.op("act", lambda sl=sl: nc.scalar.copy(out=S0b[:, sl, :], in_=S0f[:, sl, 0:1024]),
                         reads=[f"S0f{sl}"], writes=[f"S0b{sl}"])

                    def fin(b=b, sl=sl):
                        last = None
                        for h in range(4):
                            for c in range(2):
                                blk = (h * 2 + c) % 4
                                last = nc.tensor.matmul(
                                    s_bo2[h // 2][0][:, blk * 128 + b * 8: blk * 128 + (b + 1) * 8],
                                    lhsT=S0b[:, sl, h * 256 + c * 128: h * 256 + (c + 1) * 128],
                                    rhs=qt[:, s_par, h * 128 + b * 8: h * 128 + (b + 1) * 8], start=False,
                                    stop=(b == NSEQ - 1 and h % 2 == 1 and c == 1))
                        return last
                    S.op("pe", fin, reads=[f"S0b{sl}", f"qt{s_par}"], writes=[s_bo2[0][1], s_bo2[1][1]])
                    S.op("dve", lambda b=b, sl=sl: nc.vector.tensor_scalar(
                        out=kendm[:, sl, :], in0=kend[:, s_par, :], scalar1=cstb[:, SEQM + b:SEQM + b + 1], scalar2=None,
                        op0=ALU.mult), reads=[f"kend{s_par}", "cstb"], writes=[f"kendm{sl}"])
                    state_update(s_par, S0f[:, sl, :], f"S0f{sl}", kendm[:, sl, :], f"kendm{sl}", b, 16,
                                 out32=Snew[:, sl, :], outkey=f"Snew{sl}")
                    S.dma("sp", f"so{sl}", lambda b=b, sl=sl: nc.sync.dma_start(
                        out=nss_o.ap()[b].rearrange("h k v -> k h v"),
                        in_=Snew[:, sl, :].rearrange("p (h v) -> p h v", h=4)),
                        reads=[f"Snew{sl}"], writes=[f"nss{b}"])
                S.stage(5.6)
                reserved.clear()
                gla_finish(s_par, s_bo2, NTILE)

                S.barrier()

            S.stage(7)
            load_x(xT_view[:, :, 0:128], 1, "xTb1")

            def kv_proj(par, xkey, slot, last_tile):
                xb = xTb[:, par]
                bk, bkk = nextbank()
                proj_fm(bk, 0, W1, "W_KV", C_K, xb, xkey, 128, 1, bkk)
                S.op("act", lambda: nc.scalar.copy(out=Kr[:, slot, :], in_=bk[:, 0:128]), reads=[bkk],
                     writes=[f"Kr{slot}"])
                bk2, bkk2 = nextbank()

                def fkv():
                    last = None
                    for dc in range(8):
                        last = nc.tensor.matmul(bk2[:, 0:256], lhsT=xb[:, dc, :], rhs=W1[:, dc, C_K:C_K + 256],
                                                start=(dc == 0), stop=(dc == 7))
                    return last
                S.op("pe", fkv, reads=["W_KV", xkey], writes=[bkk2])
                S.op("dve", lambda: nc.vector.tensor_copy(out=Vr[:, slot, :], in_=bk2[:, 128:256]), reads=[bkk2],
                     writes=[f"Vr{slot}"])
                if last_tile and not os.environ.get("KSKIP_NKVP"):
                    S.op("dve", lambda: nc.vector.tensor_copy(out=kvtok[:], in_=bk2[:, 0:256]), reads=[bkk2],
                         writes=["kvtok"])
                    def st_pkv():
                        l = []
                        for b in range(16):
                            l.append(nc.sync.dma_start(out=nkp_o.ap()[b * 8:(b + 1) * 8, :],
                                                       in_=kvtok[b * 8:(b + 1) * 8, 0:128]))
                            l.append(nc.sync.dma_start(out=nvp_o.ap()[b * 8:(b + 1) * 8, :],
                                                       in_=kvtok[b * 8:(b + 1) * 8, 128:256]))
                        return l
                    if not os.environ.get("KSKIP_NKVP2"):
                        S.dma("sp", "nkvp", st_pkv, reads=["kvtok"], writes=["nkp"], n=32)

            kv_proj(1, "xTb1", 0, False)
            load_x(xT_view[:, :, 128:256], 0, "xTb0")
            for j in range(NTILE):
                par = j % 2
                xkey = f"xTb{par}"
                sp_, sc_ = j % 2, (j + 1) % 2
                if j + 1 < NTILE:
                    load_x(xT_view[:, :, 128 + (j + 1) * 128: 256 + (j + 1) * 128], 1 - par, f"xTb{1 - par}")
                if j == 0: S.stage(7.1)
                if j >= 1: S.stage(7.9 + j * 0.002)
                attn_proj(par, xkey, None)
                kv_proj(par, xkey, sc_, j == NTILE - 1)
                if j == 0: S.stage(7.2)
                for h in range(2):
                    hp = slice(h * 64, (h + 1) * 64)
                    for half, (slot, bt) in enumerate(((sp_, biasP0 if j == 0 else biasP), (sc_, biasQ))):
                        bks, bksk = nextbank()

                        def fsc(bks=bks, h=h, hp=hp, slot=slot, bt=bt, par=par):
                            nc.tensor.matmul(bks[:, :], lhsT=ident, rhs=bt[:, h, :], start=True, stop=False)
                            return nc.tensor.matmul(bks[:, :], lhsT=Kr[:, slot, :], rhs=qT[:, par, h, :], start=False,
                                                    stop=True)
                        S.op("pe", fsc, reads=["cstb", "biasP", "biasP0", "biasQ", f"Kr{slot}", f"qT{par}"], writes=[bksk])
                        S.op("act", lambda bks=bks, h=h, half=half: nc.scalar.activation(
                            out=PT[:, h * 2 + half, :], in_=bks[:, :], func=AF.Exp),
                            reads=[bksk], writes=[f"PT{h * 2 + half}"])
                if j == 0: S.stage(7.3)
                bo, bok = nextbank()
                bd, bdk = nextbank()

                def fpv(dst, use_v, sp_=sp_, sc_=sc_):
                    last = None
                    for h in range(2):
                        hp = slice(h * 64, (h + 1) * 64)
                        for half, slot in enumerate((sp_, sc_)):
                            lw = Vr[:, slot, hp] if use_v else onesb[:, 0:64]
                            last = nc.tensor.matmul(dst[hp, :], lhsT=lw, rhs=PT[:, h * 2 + half, :],
                                                    start=(half == 0), stop=(half == 1))
                    return last
                S.op("pe", lambda bo=bo, fpv=fpv: fpv(bo, True),
                     reads=[f"Vr{sp_}", f"Vr{sc_}", "PT0", "PT1", "PT2", "PT3"], writes=[bok])
                S.op("pe", lambda bd=bd, fpv=fpv: fpv(bd, False), reads=["onesb", "PT0", "PT1", "PT2", "PT3"],
                     writes=[bdk])
                if j == 0: S.stage(7.4)
                attn_finish(par, bo, bok, bd, bdk, j)

                if j == 0: S.stage(7.5)
                decay_prep(par, xkey, False, True)
                tr_later = gk_gv(par, xkey, True, defer_tr=True)
                gla_proj(par, xkey)
                tr_later()
                gla_AT(par, CM_P)
                if j == 0: S.stage(7.6)
                bo2 = [nextbank(), nextbank()]

                def fo(bo2=bo2, par=par):
                    last = None
                    for h in range(4):
                        for c in range(2):
                            blk = (h * 2 + c) % 4
                            dst = bo2[h // 2][0][:, blk * 128:(blk + 1) * 128]
                            nc.tensor.matmul(dst, lhsT=Sb[:, h * 256 + c * 128: h * 256 + (c + 1) * 128],
                                             rhs=qt[:, par, h * 128:(h + 1) * 128], start=True, stop=False)
                            last = nc.tensor.matmul(dst, lhsT=gv[:, par, h * 256 + c * 128: h * 256 + (c + 1) * 128],
                                                    rhs=ATm[:, h * 128:(h + 1) * 128], start=False, stop=True)
                    return last
                S.op("pe", fo, reads=["Sb", f"qt{par}", f"gv{par}", "ATm"], writes=[bo2[0][1], bo2[1][1]])
                if j == 0: S.stage(7.7)
                state_update(par, S32, "S32", kend[:, par, :], f"kend{par}", 0, 1)
                S.op("act", lambda: nc.scalar.copy(out=Sb[:], in_=S32[:, 0:1024]), reads=["S32"],
                     writes=["Sb"])
                if j == 0: S.stage(7.8)
                gla_finish(par, bo2, j)
            S.stage(7.95)
            S.dma("sp", "nsp", lambda: nc.sync.dma_start(
                out=nsp_o.ap().rearrange("h k v -> k h v"), in_=S32[:, 0:1024].rearrange("p (h v) -> p h v", h=4)),
                reads=["S32"], writes=["nsp"])
            S.barrier()

        S.stage(8)
        with ExitStack() as s3:
            Wr = sbt(s3, "Wr", [128, 8, 2048], BF16)
            Wpa = sbt(s3, "Wpa", [128, 4, 1024], BF16)
            Wpg = sbt(s3, "Wpg", [128, 8, 1024], BF16)
            Wo = sbt(s3, "Wo", [128, 8, 1024], BF16)
            stg2 = sbt(s3, "stg2", [128, 4, 1024], F32)
            lng = sbt(s3, "lng", [128, 1024], F32)
            lnb = sbt(s3, "lnb", [128, 1024], F32)
            x2 = sbt(s3, "x2", [128, 2, 8, 512], BF16)
            att2 = sbt(s3, "att2", [128, 2, 4, 512], BF16)
            gla2 = sbt(s3, "gla2", [128, 2, 4, 1024], BF16)
            sa = sbt(s3, "sa", [128, 2, 512], F32)
            sg = sbt(s3, "sg", [128, 2, 512], F32)
            merged = sbt(s3, "merged", [128, 2, 8, 512], BF16)
            xt2 = sbt(s3, "xt2", [128, 2, 1024], F32)
            bnst = sbt(s3, "bnst", [128, 2, 6], F32)
            bnag = sbt(s3, "bnag", [128, 8], F32)

            S.dma("sp", "lnp", lambda: [nc.sync.dma_start(out=lng[:], in_=bass.AP(ln_g, 0, [[0, 128], [1, 1024]])),
                                        nc.sync.dma_start(out=lnb[:], in_=bass.AP(ln_b, 0, [[0, 128], [1, 1024]]))],
                  writes=["lng", "lnb"], n=2)
            wl2 = [0]

            def load_w(src_view, ndc, ncols, dst, dkey, col0=0):
                for dc in range(ndc):
                    for cb in range(0, ncols, 1024):
                        cw = min(1024, ncols - cb)
                        slot = wl2[0] % 4
                        wl2[0] += 1
                        S.dma("sp", f"v{slot}", lambda slot=slot, dc=dc, cb=cb, cw=cw: nc.sync.dma_start(
                            out=stg2[:, slot, 0:cw], in_=src_view[:, dc, col0 + cb: col0 + cb + cw]),
                            writes=[f"stg2{slot}"])
                        eng = ("pool", "dve", "act")[wl2[0] % 3]

                        def fc(slot=slot, dc=dc, cb=cb, cw=cw, eng=eng):
                            o = dst[:, dc, cb:cb + cw]
                            i = stg2[:, slot, 0:cw]
                            if eng == "pool":
                                return nc.gpsimd.tensor_copy(out=o, in_=i)
                            if eng == "dve":
                                return nc.vector.tensor_copy(out=o, in_=i)
                            return nc.scalar.copy(out=o, in_=i)
                        S.op(eng, fc, reads=[f"stg2{slot}"], writes=[dkey])
            w_view = w_in.ap().rearrange("(dc p) n -> p dc n", p=128)
            load_w(w_view, 8, 2048, Wr, "Wr", col0=C_RATT)
            load_w(w_pa.ap().rearrange("(g p) n -> p g n", p=128), 4, 1024, Wpa, "Wpa")
            load_w(w_pg.ap().rearrange("(c p) n -> p c n", p=128), 8, 1024, Wpg, "Wpg")
            load_w(w_o.ap().rearrange("(c p) n -> p c n", p=128), 8, 1024, Wo, "Wo")

            xT_view = xT.ap().rearrange("(dc p) t -> p dc t", p=128)
            xsT_view = xsT.ap().rearrange("(dc p) t -> p dc t", p=128)
            groups = [(s, 4) for s in range(4)] + [(4, 1)]

            def load_group(gi):
                s, nt = groups[gi]
                par = gi % 2
                T = nt * 128
                for dcp in range(4):
                    slot = wl2[0] % 4
                    wl2[0] += 1
                    src = (xT_view[:, dcp * 2:dcp * 2 + 2, 128 + s * 512: 128 + s * 512 + T] if nt == 4
                           else xsT_view[:, dcp * 2:dcp * 2 + 2, :])
                    S.dma("sp", f"v{slot}", lambda slot=slot, src=src, T=T: nc.sync.dma_start(
                        out=stg2[:, slot, 0:2 * T].rearrange("p (a t) -> p a t", a=2), in_=src),
                        writes=[f"stg2{slot}"])
                    S.op("pool", lambda slot=slot, dcp=dcp, T=T, par=par: nc.gpsimd.tensor_copy(
                        out=x2[:, par, dcp * 2:dcp * 2 + 2, 0:T],
                        in_=stg2[:, slot, 0:2 * T].rearrange("p (a t) -> p a t", a=2)),
                        reads=[f"stg2{slot}"], writes=[f"x2{par}"])
                t0 = s * 4
                S.dma("sp", f"la{par}", lambda: [
                    nc.sync.dma_start(out=att2[:, par, 0:nt, :], in_=att_scr.ap()[t0:t0 + nt].rearrange("n p c -> p n c")),
                    nc.sync.dma_start(out=gla2[:, par, 0:nt, :], in_=(gla_scr_s.ap() if t0 == NTILE else gla_scr.ap()[t0:t0 + nt]).rearrange("n p c -> p n c"))],
                    reads=[f"att_scr{t0 + i}" for i in range(nt)] + [f"gla_scr{t0 + i}" for i in range(nt)],
                    writes=[f"att2{par}", f"gla2{par}"], n=2)

            a2s = pstride(att2)
            g2s = pstride(gla2)
            load_group(0)
            xtl = [0]
            for gi, (s, nt) in enumerate(groups):
                par = gi % 2
                T = nt * 128
                if gi + 1 < len(groups):
                    load_group(gi + 1)
                for dc in range(8):
                    dsl = slice(dc * 128, (dc + 1) * 128)
                    bha, bhak = nextbank()

                    def fha(bha=bha, dsl=dsl, par=par, nt=nt, T=T):
                        last = None
                        for g in range(4):
                            rhs = bass.AP(att2, par * 2048 + g * 128, [[a2s, 128], [512, nt], [1, 128]])
                            last = nc.tensor.matmul(bha[:, 0:T],
                                                    lhsT=Wpa[:, g, dsl], rhs=rhs, start=(g == 0), stop=(g == 3))
                        return last
                    S.op("pe", fha, reads=["Wpa", f"att2{par}"], writes=[bhak])
                    bhg, bhgk = nextbank()

                    def fhg(bhg=bhg, dsl=dsl, par=par, nt=nt, T=T):
                        last = None
                        for c in range(8):
                            rhs = bass.AP(gla2, par * 4096 + c * 128, [[g2s, 128], [1024, nt], [1, 128]])
                            last = nc.tensor.matmul(bhg[:, 0:T],
                                                    lhsT=Wpg[:, c, dsl], rhs=rhs, start=(c == 0), stop=(c == 7))
                        return last
                    S.op("pe", fhg, reads=["Wpg", f"gla2{par}"], writes=[bhgk])
                    gates = []
                    for which, dstt in ((0, sa), (1, sg)):
                        bkr, bkrk = nextbank()

                        def fr(bkr=bkr, which=which, dc=dc, par=par, T=T):
                            last = None
                            for d2 in range(8):
                                last = nc.tensor.matmul(
                                    bkr[:, 0:T], lhsT=Wr[:, d2, which * 1024 + dc * 128: which * 1024 + (dc + 1) * 128],
                                    rhs=x2[:, par, d2, 0:T], start=(d2 == 0), stop=(d2 == 7))
                            return last
                        S.op("pe", fr, reads=["Wr", f"x2{par}"], writes=[bkrk])
                        dpar = dc % 2
                        S.op("act", lambda bkr=bkr, dstt=dstt, dpar=dpar, T=T: nc.scalar.activation(
                            out=dstt[:, dpar, 0:T], in_=bkr[:, 0:T], func=AF.Sigmoid),
                            reads=[bkrk], writes=[f"{'sa' if which == 0 else 'sg'}{dpar}"])
                    dpar = dc % 2
                    S.op("dve", lambda bha=bha, dpar=dpar, T=T: nc.vector.tensor_tensor(
                        out=sa[:, dpar, 0:T], in0=bha[:, 0:T], in1=sa[:, dpar, 0:T], op=ALU.mult),
                        reads=[bhak, f"sa{dpar}"], writes=[f"sa{dpar}"])
                    S.op("dve", lambda bhg=bhg, dpar=dpar, T=T: nc.vector.tensor_tensor(
                        out=sg[:, dpar, 0:T], in0=bhg[:, 0:T], in1=sg[:, dpar, 0:T], op=ALU.mult),
                        reads=[bhgk, f"sg{dpar}"], writes=[f"sg{dpar}"])
                    S.op("pool", lambda dpar=dpar, dc=dc, par=par, T=T: nc.gpsimd.tensor_tensor(
                        out=merged[:, par, dc, 0:T], in0=sa[:, dpar, 0:T], in1=sg[:, dpar, 0:T], op=ALU.add),
                        reads=[f"sa{dpar}", f"sg{dpar}"], writes=[f"merged{par}"])
                for i in range(nt):
                    xp = xtl[0] % 2
                    xtl[0] += 1
                    src = xtok.ap()[(s * 4 + i) * 128:(s * 4 + i + 1) * 128, :] if nt == 4 else xstok.ap()
                    dsto = y_o.ap()[(s * 4 + i) * 128:(s * 4 + i + 1) * 128, :] if nt == 4 else ys_o.ap()
                    S.dma("sp", f"xt{xp}", lambda xp=xp, src=src: nc.sync.dma_start(out=xt2[:, xp, :], in_=src),
                          writes=[f"xt2{xp}"])
                    by = [nextbank(), nextbank()]

                    def fy(by=by, par=par, i=i):
                        last = None
                        for nb in range(2):
                            for dc in range(8):
                                last = nc.tensor.matmul(by[nb][0][:, :], lhsT=merged[:, par, dc, i * 128:(i + 1) * 128],
                                                        rhs=Wo[:, dc, nb * 512:(nb + 1) * 512], start=(dc == 0),
                                                        stop=(dc == 7))
                        return last
                    S.op("pe", fy, reads=["Wo", f"merged{par}"], writes=[by[0][1], by[1][1]])
                    for nb in range(2):
                        S.op("dve", lambda xp=xp, nb=nb, by=by: nc.vector.scalar_tensor_tensor(
                            out=xt2[:, xp, nb * 512:(nb + 1) * 512], in0=xt2[:, xp, nb * 512:(nb + 1) * 512],
                            scalar=DN_ALPHA, in1=by[nb][0][:, :], op0=ALU.mult, op1=ALU.add),
                            reads=[f"xt2{xp}", by[nb][1]], writes=[f"xt2{xp}"])

                    def fbn(xp=xp):
                        nc.vector.bn_stats(out=bnst[:, 0, :], in_=xt2[:, xp, 0:512])
                        return nc.vector.bn_stats(out=bnst[:, 1, :], in_=xt2[:, xp, 512:1024])
                    S.op("dve", fbn, reads=[f"xt2{xp}"], writes=["bnst"])
                    S.op("dve", lambda: nc.vector.bn_aggr(out=bnag[:, 0:2], in_=bnst[:].rearrange("p a b -> p (a b)")),
                         reads=["bnst"], writes=["bnag"])
                    S.op("act", lambda: nc.scalar.activation(out=bnag[:, 2:3], in_=bnag[:, 1:2], func=AF.Ln,
                                                             bias=c32[:, 1:2], scale=1.0),
                         reads=["bnag", "c32b"], writes=["bnag"])
                    S.op("act", lambda: nc.scalar.activation(out=bnag[:, 2:3], in_=bnag[:, 2:3], func=AF.Exp,
                                                             scale=-0.5), reads=["bnag"], writes=["bnag"])
                    S.op("dve", lambda: nc.vector.scalar_tensor_tensor(
                        out=bnag[:, 3:4], in0=bnag[:, 0:1], scalar=-1.0, in1=bnag[:, 2:3], op0=ALU.mult,
                        op1=ALU.mult), reads=["bnag"], writes=["bnag"])
                    S.op("act", lambda xp=xp: nc.scalar.activation(out=xt2[:, xp, :], in_=xt2[:, xp, :],
                                                                   func=AF.Identity, bias=bnag[:, 3:4],
                                                                   scale=bnag[:, 2:3]),
                         reads=[f"xt2{xp}", "bnag"], writes=[f"xt2{xp}"])
                    S.op("pool", lambda xp=xp: nc.gpsimd.tensor_tensor(out=xt2[:, xp, :], in0=xt2[:, xp, :],
                                                                       in1=lng[:], op=ALU.mult),
                         reads=[f"xt2{xp}", "lng"], writes=[f"xt2{xp}"])
                    S.op("pool", lambda xp=xp: nc.gpsimd.tensor_tensor(out=xt2[:, xp, :], in0=xt2[:, xp, :],
                                                                       in1=lnb[:], op=ALU.add),
                         reads=[f"xt2{xp}", "lnb"], writes=[f"xt2{xp}"])
                    S.dma("sp", f"yo{xp}", lambda xp=xp, dsto=dsto: nc.sync.dma_start(out=dsto, in_=xt2[:, xp, :]),
                          reads=[f"xt2{xp}"], writes=[f"yout{xp}"])
            S.barrier()
        S.finalize()
    return nc


def _t5_bucket(dist):
    n = np.maximum(dist, 0)
    nf = np.maximum(n, 1).astype(np.float32)
    large = 16 + (np.log(nf / np.float32(16)) / np.float32(math.log(128 / 16)) * np.float32(16)).astype(np.int32)
    large = np.minimum(large, 31)
    return np.where(n < 16, n, large)


def _constants():
    cb = np.zeros((128, NCB), np.float32)
    p = np.arange(128)[:, None]
    t = np.arange(128)[None, :]
    cmp_ = (p <= t).astype(np.float32)
    cb[:, CM_P:CM_P + 512] = np.tile(cmp_, (1, 4))
    same = (p // 8 == t // 8)
    cms = (same & (p <= t)).astype(np.float32)
    cb[:, CM_S:CM_S + 512] = np.tile(cms, (1, 4))
    cb[:, BLKNEG:BLKNEG + 128] = np.where(same, 0.0, NEG)
    cb[:, SEQM:SEQM + 16] = (p // 8 == np.arange(16)[None, :]).astype(np.float32)
    c512 = np.arange(512)[None, :]
    cb[:, RST_P:RST_P + 512] = np.broadcast_to((c512 % 128 != 0).astype(np.float32), (128, 512))
    cb[:, RST_S:RST_S + 512] = np.broadcast_to((c512 % 8 != 0).astype(np.float32), (128, 512))
    cb[:, IDENT:IDENT + 128] = np.eye(128, dtype=np.float32)
    cb = cb.astype(ml_dtypes.bfloat16)
    n = np.arange(384)
    dist = 255 - n
    valid = (dist >= 0) & (dist <= 128) & (n <= 382)
    bucket = _t5_bucket(dist)
    ohr = np.zeros((32, 384), np.float32)
    ohr[bucket[valid], n[valid]] = 1.0
    negr = np.broadcast_to(np.where(valid, 0.0, NEG).astype(np.float32), (8, 384)).copy()
    return cb, ohr, negr


_NC_CACHE = {}


def make_in_maps(x_prompt, x_sample, cache_k, cache_v, state_gla, rel_bias_table, w_in, w_gk_up, b_gk, attn_sink,
                 gla_norm_w, w_pa, w_pg, w_o, ln_g, ln_b):
    f = lambda a: np.ascontiguousarray(np.asarray(a, dtype=np.float32))
    x_prompt, x_sample, cache_k, cache_v, state_gla = map(f, (x_prompt, x_sample, cache_k, cache_v, state_gla))
    w_in0 = f(w_in)[0]
    pq = np.array([(h * 4 + g) * 64 + d for g in range(4) for h in range(2) for d in range(64)])
    perm = np.arange(N_IN)
    perm[C_Q:C_Q + 512] = C_Q + pq
    perm[C_GATT:C_GATT + 512] = C_GATT + pq
    w_in_p = np.ascontiguousarray(w_in0[:, perm])
    w_pa_p = np.ascontiguousarray(f(w_pa)[0][pq, :])
    cb, ohr, negr = _constants()

    xp = x_prompt[0]
    xpT = np.ascontiguousarray(xp.T)
    in_maps = []
    for c in range(NCORES):
        xT_c = np.zeros((D, TPC + 128), np.float32)
        xT_c[:, 128:] = xpT[:, c * TPC:(c + 1) * TPC]
        if c > 0:
            xT_c[:, 0:128] = xpT[:, c * TPC - 128:c * TPC]
        xs_c = x_sample[c * NSEQ:(c + 1) * NSEQ].reshape(128, D)
        ck_c = cache_k[0, c * NSEQ:(c + 1) * NSEQ].reshape(NSEQ, 128, 128)
        cv_c = cache_v[0, c * NSEQ:(c + 1) * NSEQ].reshape(NSEQ, 128, 128)
        mj = np.zeros((128, 8), np.float32)
        xpv = np.zeros((D, 7 * TPC), np.float32)
        if c > 0:
            xpv[:, (7 - c) * TPC:] = xpT[:, 0:c * TPC]
        negr_c = negr.copy()
        hneg = np.full((128, 1), NEG if c == 0 else 0.0, np.float32)
        if c == 0:
            pass
        in_maps.append({
            "xT": xT_c, "xpvT": xpv, "xtok": np.ascontiguousarray(xp[c * TPC:(c + 1) * TPC]),
            "xsT": np.ascontiguousarray(xs_c.T), "xstok": np.ascontiguousarray(xs_c),
            "ckT": np.ascontiguousarray(ck_c.transpose(0, 2, 1)), "ck": np.ascontiguousarray(ck_c),
            "cv": np.ascontiguousarray(cv_c), "st0": np.ascontiguousarray(state_gla[0, c * NSEQ:(c + 1) * NSEQ]),
            "table": f(rel_bias_table), "w_in": w_in_p, "w_up": f(w_gk_up)[0], "b_gk": f(b_gk)[0],
            "sink": f(attn_sink)[0], "gnw": f(gla_norm_w)[0], "w_pa": w_pa_p, "w_pg": f(w_pg)[0], "w_o": f(w_o)[0],
            "ln_g": f(ln_g)[0], "ln_b": f(ln_b)[0], "cstb": cb, "ohr": ohr, "negr": negr_c, "mj": mj, "hneg": hneg,
        })
    return in_maps


def kernel(**inputs):
    in_maps = make_in_maps(**inputs)
    if "nc" not in _NC_CACHE:
        _NC_CACHE["nc"] = build_nc()
    res = run_bass_kernel_spmd(_NC_CACHE["nc"], in_maps, core_ids=list(range(NCORES)))
    R = res.results
    y_prompt = np.concatenate([R[c]["y"] for c in range(NCORES)], 0).reshape(1, 16384, D)
    y_sample = np.concatenate([R[c]["ys"] for c in range(NCORES)], 0).reshape(128, 8, D)
    nkp = R[NCORES - 1]["nkp"].reshape(1, 1, 128, 2, 64)
    nvp = R[NCORES - 1]["nvp"].reshape(1, 1, 128, 2, 64)
    nsp = R[NCORES - 1]["nsp"].reshape(1, 1, 4, 128, 256)
    nks = np.concatenate([R[c]["nks"] for c in range(NCORES)], 0).reshape(1, 128, 128, 2, 64)
    nvs = np.concatenate([R[c]["nvs"] for c in range(NCORES)], 0).reshape(1, 128, 128, 2, 64)
    nss = np.concatenate([R[c]["nss"] for c in range(NCORES)], 0).reshape(1, 128, 4, 128, 256)
    return tuple(np.asarray(a, dtype=np.float32) for a in (y_prompt, y_sample, nkp, nvp, nsp, nks, nvs, nss))
```

```python
import math
import os
from contextlib import ExitStack

import numpy as np
import ml_dtypes

import concourse.bass as bass
import concourse.mybir as mybir
from concourse.bass_utils import run_bass_kernel_spmd

F32 = mybir.dt.float32
BF16 = mybir.dt.bfloat16
AF = mybir.ActivationFunctionType
ALU = mybir.AluOpType

NCORES = 8
D = 1024
TPC = 2048
NTILE = TPC // 128
NSEQ = 16
N_IN = 6416
HK_SCALE = 128 ** -0.5
DN_ALPHA = 2.0 ** 0.25
EPS = 1e-5
NEG = -1e30

C_Q, C_K, C_V, C_GATT, C_GQ, C_GK, C_GV, C_GGLA, C_GLR, C_RATT, C_RGLA = (
    0, 512, 640, 768, 1280, 1792, 2304, 3328, 4352, 4368, 5392)
W1_COLS = 4368

CM_P, CM_S, BLKNEG, SEQM, RST_P, RST_S, IDENT = 0, 512, 1024, 1152, 1168, 1680, 2192
NCB = 2320

ENGS = ("pe", "act", "dve", "pool", "sp")
SEM_CHUNK = 500
DMA_SEM_LIMIT = 512


class Sched:
    def __init__(self, nc, stack):
        self.nc = nc
        self.eng = {"pe": nc.tensor, "act": nc.scalar, "dve": nc.vector, "pool": nc.gpsimd, "sp": nc.sync}
        self.stack = stack
        self.sems = {e: [] for e in ENGS}
        self.count = {e: 0 for e in ENGS}
        self.known = {e: {} for e in ENGS}
        self.last_w = {}
        self.readers = {}
        self.dma_ch = {}
        self.prog = {e: [] for e in ENGS}
        self.retired = []
        self.enabled = True
        self.stop_at = float(os.environ.get("KSTOP", "99"))

    def stage(self, n):
        self.enabled = n <= self.stop_at

    def _newsem(self, name):
        return self.stack.enter_context(self.nc.semaphore(name))

    def _eng_sem(self, e, idx):
        while len(self.sems[e]) <= idx:
            self.sems[e].append(self._newsem(f"s_{e}_{len(self.sems[e])}"))
        return self.sems[e][idx]

    def _wait(self, e, tok):
        _, key, sem, val = tok
        if self.known[e].get(key, 0) >= val:
            return
        self.known[e][key] = val
        eng = self.eng[e]
        self.prog[e].append(lambda: eng.wait_ge(sem, val))

    def _deps(self, e, reads, writes):
        toks = []
        for b in reads:
            t = self.last_w.get(b)
            if t is not None:
                toks.append(t)
        for b in writes:
            t = self.last_w.get(b)
            if t is not None:
                toks.append(t)
            toks.extend(self.readers.get(b, ()))
        for t in toks:
            self._wait(e, t)

    def _record(self, tok, reads, writes):
        for b in reads:
            self.readers.setdefault(b, []).append(tok)
        for b in writes:
            self.last_w[b] = tok
            self.readers[b] = []

    def op(self, e, fn, reads=(), writes=()):
        if not self.enabled:
            return None
        self._deps(e, reads, writes)
        n = self.count[e]
        idx, val = n // SEM_CHUNK, n % SEM_CHUNK + 1
        sem = self._eng_sem(e, idx)
        self.prog[e].append(lambda: fn().then_inc(sem, 1))
        self.count[e] = n + 1
        tok = ("eng", (e, idx), sem, val)
        self._record(tok, reads, writes)
        return tok

    def dma(self, e, ch, fn, reads=(), writes=(), n=1):
        if not self.enabled:
            return None
        if ch not in self.dma_ch:
            self.dma_ch[ch] = [self._newsem(f"d_{ch}"), 0]
        sem, cnt = self.dma_ch[ch]
        if cnt > 0:
            self._wait(e, ("dma", ("dma", ch, id(sem)), sem, cnt))
        if cnt + 16 * n > DMA_SEM_LIMIT:
            self.retired.append((ch, sem, cnt))
            sem, cnt = self._newsem(f"d_{ch}_{len(self.retired)}"), 0
            self.dma_ch[ch] = [sem, cnt]
        self._deps(e, reads, writes)

        def run():
            insts = fn()
            if not isinstance(insts, (list, tuple)):
                insts = [insts]
            assert len(insts) == n, (ch, len(insts), n)
            for i in insts:
                i.then_inc(sem, 16)
        self.prog[e].append(run)
        cnt += 16 * n
        self.dma_ch[ch][1] = cnt
        tok = ("dma", ("dma", ch, id(sem)), sem, cnt)
        self._record(tok, reads, writes)
        return tok

    def wait_all(self, e):
        for ch, (sem, cnt) in list(self.dma_ch.items()) + [(c, (s_, n_)) for c, s_, n_ in self.retired]:
            if cnt:
                self._wait(e, ("dma", ("dma", ch, id(sem)), sem, cnt))
        for e2 in ENGS:
            n = self.count[e2]
            if n:
                idx, val = (n - 1) // SEM_CHUNK, (n - 1) % SEM_CHUNK + 1
                self._wait(e, ("eng", (e2, idx), self.sems[e2][idx], val))

    def barrier(self):
        for e in ENGS:
            self.wait_all(e)

    def finalize(self):
        self.barrier()
        with self.nc.Block() as block:
            reg = {"pe": block.tensor, "act": block.scalar, "dve": block.vector, "pool": block.gpsimd,
                   "sp": block.sync}
            for e in ENGS:
                def body(_eng, _l=self.prog[e]):
                    for t in _l:
                        t()
                reg[e](body)


def build_nc():
    nc = bass.Bass("TRN2", target_bir_lowering=False)

    def din(name, shape, dt=F32):
        return nc.dram_tensor(name, list(shape), dt, kind="ExternalInput")

    def dout(name, shape):
        return nc.dram_tensor(name, list(shape), F32, kind="ExternalOutput")

    xT = din("xT", [D, TPC + 128])
    xtok = din("xtok", [TPC, D])
    xpvT = din("xpvT", [D, 7 * TPC])
    xsT = din("xsT", [D, 128])
    xstok = din("xstok", [128, D])
    ckT = din("ckT", [NSEQ, 128, 128])
    ck = din("ck", [NSEQ, 128, 128])
    cv = din("cv", [NSEQ, 128, 128])
    st0 = din("st0", [NSEQ, 4, 128, 256])
    table = din("table", [32, 8])
    w_in = din("w_in", [D, N_IN])
    w_up = din("w_up", [16, 512])
    b_gk = din("b_gk", [512])
    sink = din("sink", [8])
    gnw_d = din("gnw", [256])
    w_pa = din("w_pa", [512, D])
    w_pg = din("w_pg", [D, D])
    w_o = din("w_o", [D, D])
    ln_g = din("ln_g", [D])
    ln_b = din("ln_b", [D])
    cstb_d = din("cstb", [128, NCB], BF16)
    ohr_d = din("ohr", [32, 384])
    negr_d = din("negr", [8, 384])
    mj_d = din("mj", [128, 8])
    hneg_d = din("hneg", [128, 1])

    y_o = dout("y", [TPC, D])
    ys_o = dout("ys", [128, D])
    nkp_o = dout("nkp", [128, 128])
    nvp_o = dout("nvp", [128, 128])
    nsp_o = dout("nsp", [4, 128, 256])
    nks_o = dout("nks", [NSEQ, 128, 128])
    nvs_o = dout("nvs", [NSEQ, 128, 128])
    nss_o = dout("nss", [NSEQ, 4, 128, 256])

    rs_scr = nc.dram_tensor("rs_scr", [8, 384], F32)
    att_scr = nc.dram_tensor("att_scr", [NTILE + 1, 128, 512], BF16, kind="ExternalOutput")
    gla_scr = nc.dram_tensor("gla_scr", [NTILE, 128, 1024], BF16, kind="ExternalOutput")
    gla_scr_s = nc.dram_tensor("gla_scr_s", [1, 128, 1024], BF16, kind="ExternalOutput")

    with ExitStack() as st:
        S = Sched(nc, st)

        def sbt(stack, name, shape, dt):
            return stack.enter_context(nc.sbuf_tensor("sb_" + name, list(shape), dt))

        def pstride(t):
            return t[:].ap[0][0]

        NB = 7
        banks = [st.enter_context(nc.psum_tensor(f"bk{i}", [128, 512], F32)) for i in range(NB)]
        bankT = st.enter_context(nc.psum_tensor("bkT", [128, 1024], BF16))
        bctr = [0]

        reserved = set()

        def nextbank():
            while True:
                i = bctr[0] % NB
                bctr[0] += 1
                if f"bk{i}" not in reserved:
                    return banks[i], f"bk{i}"

        cstb = sbt(st, "cstb", [128, NCB], BF16)
        onesb = sbt(st, "onesb", [128, 128], BF16)
        zerob = sbt(st, "zerob", [128, 128], BF16)
        c32 = sbt(st, "c32", [128, 4], F32)
        S.dma("sp", "cst", lambda: nc.sync.dma_start(out=cstb[:], in_=cstb_d.ap()), writes=["cstb"])
        S.op("pool", lambda: nc.gpsimd.memset(onesb[:], 1.0), writes=["onesb"])
        S.op("pool", lambda: nc.gpsimd.memset(zerob[:], 0.0), writes=["zerob"])
        S.op("pool", lambda: nc.gpsimd.memset(c32[:, 0:1], 1.0), writes=["c32a"])
        S.op("pool", lambda: nc.gpsimd.memset(c32[:, 1:2], EPS), writes=["c32b"])
        ident = cstb[:, IDENT:IDENT + 128]

        def proj_fm(dst_bank, col0, W, wkey, wcol, xb, xkey, ntok, nchunks, bkey, msize=128):
            def fn():
                last = None
                for c in range(nchunks):
                    for dc in range(8):
                        last = nc.tensor.matmul(
                            dst_bank[0:msize, col0 + c * ntok: col0 + (c + 1) * ntok],
                            lhsT=W[:, dc, wcol + c * 128: wcol + c * 128 + msize],
                            rhs=xb[:, dc, 0:ntok], start=(dc == 0), stop=(dc == 7))
                return last
            S.op("pe", fn, reads=[wkey, xkey], writes=[bkey])

        with ExitStack() as s1:
            W1 = sbt(s1, "W1", [128, 8, W1_COLS], BF16)
            stg = sbt(s1, "stg", [128, 2, 1024], F32)
            xstg = sbt(s1, "xstg", [128, 2, 1024], F32)
            wupb = sbt(s1, "wupb", [16, 512], BF16)
            negb = sbt(s1, "negb", [128, 4], F32)
            gnw = sbt(s1, "gnw", [128, 2], F32)
            esk = sbt(s1, "esk", [128, 4], F32)
            mj = sbt(s1, "mj", [128, 8], F32)
            biasP = sbt(s1, "biasP", [128, 2, 512], BF16)
            biasQ = sbt(s1, "biasQ", [128, 2, 512], BF16)
            biasC = sbt(s1, "biasC", [128, 2, 512], BF16)
            biasN = sbt(s1, "biasN", [128, 2, 512], BF16)
            biasP0 = sbt(s1, "biasP0", [128, 2, 512], BF16)
            hneg = sbt(s1, "hneg", [128, 1], F32)
            s0 = ExitStack()
            hank = sbt(s0, "hank", [128, 2, 128], F32)
            tabl = sbt(s0, "tabl", [32, 8], F32)
            ohr = sbt(s0, "ohr", [32, 384], F32)
            negr = sbt(s0, "negr", [8, 384], F32)
            rr = sbt(s0, "rr", [8, 384], F32)

            wupf = sbt(s0, "wupf", [16, 512], F32)
            S.dma("sp", "p0", lambda: nc.sync.dma_start(out=wupf[:], in_=w_up.ap()), writes=["wupf"])
            S.op("dve", lambda: nc.vector.tensor_copy(out=wupb[:], in_=wupf[:]), reads=["wupf"], writes=["wupb"])

            def ld_small():
                with nc.allow_non_contiguous_dma(reason="tiny parameter vectors"):
                    a = nc.sync.dma_start(out=negb[:], in_=b_gk.ap().rearrange("(h k) -> k h", k=128))
                    b = nc.sync.dma_start(out=gnw[:], in_=gnw_d.ap().rearrange("(c v) -> v c", v=128))
                c = nc.sync.dma_start(out=esk[0:64, :], in_=bass.AP(sink, 0, [[0, 64], [1, 4]]))
                d = nc.sync.dma_start(out=esk[64:128, :], in_=bass.AP(sink, 4, [[0, 64], [1, 4]]))
                e = nc.sync.dma_start(out=mj[:], in_=mj_d.ap())
                f = nc.sync.dma_start(out=tabl[:], in_=table.ap())
                g = nc.sync.dma_start(out=ohr[:], in_=ohr_d.ap())
                h = nc.sync.dma_start(out=negr[:], in_=negr_d.ap())
                i = nc.sync.dma_start(out=hneg[:], in_=hneg_d.ap())
                return [a, b, c, d, e, f, g, h, i]
            S.dma("sp", "p1", ld_small, writes=["negb", "gnw", "esk", "mj", "tabl", "ohr", "negr", "hneg"], n=9)
            S.op("dve", lambda: nc.vector.tensor_scalar(out=negb[:], in0=negb[:], scalar1=-1.0, scalar2=None,
                                                        op0=ALU.mult), reads=["negb"], writes=["negb"])
            S.op("act", lambda: nc.scalar.activation(out=esk[:], in_=esk[:], func=AF.Exp), reads=["esk"],
                 writes=["esk"])

            t5b, t5k = nextbank()
            S.op("pe", lambda: nc.tensor.matmul(t5b[0:8, 0:384], lhsT=tabl[:, :], rhs=ohr[:, :], start=True,
                                                stop=True), reads=["tabl", "ohr"], writes=[t5k])
            S.op("dve", lambda: nc.vector.tensor_tensor(out=rr[:], in0=t5b[0:8, 0:384], in1=negr[:], op=ALU.add),
                 reads=[t5k, "negr"], writes=["rr"])
            S.dma("sp", "rs", lambda: nc.sync.dma_start(out=rs_scr.ap(), in_=rr[:]), reads=["rr"], writes=["rs_scr"])
            hps = pstride(hank)
            for hd in range(8):
                h, g = hd // 4, hd % 4
                for half, (base, dst) in enumerate(((0, biasP), (128, biasQ))):
                    slot = (hd * 2 + half) % 2
                    S.dma("sp", f"hk{slot}",
                          lambda hd=hd, base=base, slot=slot: nc.sync.dma_start(
                              out=hank[:, slot, :], in_=bass.AP(rs_scr, hd * 384 + base, [[1, 128], [1, 128]])),
                          reads=["rs_scr"], writes=[f"hank{slot}"])
                    S.op("dve",
                         lambda dst=dst, h=h, g=g, slot=slot: nc.vector.tensor_copy(
                             out=dst[:, h, g * 128:(g + 1) * 128],
                             in_=bass.AP(hank, slot * 128 + 127, [[hps, 128], [-1, 128]])),
                         reads=[f"hank{slot}"], writes=["bias" + ("P" if half == 0 else "Q")])
            bps = pstride(biasP)
            cps = pstride(cstb)
            for h in range(2):
                S.op("dve", lambda h=h: nc.vector.tensor_copy(
                    out=biasC[:, h, :].rearrange("p (b g t) -> p b g t", b=16, g=4),
                    in_=bass.AP(biasP, h * 512, [[bps, 128], [0, 16], [128, 4], [1, 8]])),
                    reads=["biasP"], writes=["biasC"])
                S.op("dve", lambda h=h: nc.vector.tensor_tensor(
                    out=biasN[:, h, :].rearrange("p (b g t) -> p b g t", b=16, g=4),
                    in0=bass.AP(biasQ, h * 512, [[bps, 128], [8, 16], [128, 4], [1, 8]]),
                    in1=bass.AP(cstb, BLKNEG, [[cps, 128], [8, 16], [0, 4], [1, 8]]), op=ALU.add),
                    reads=["biasQ", "cstb"], writes=["biasN"])
            S.op("dve", lambda: nc.vector.tensor_scalar(out=biasP0[:], in0=biasP[:], scalar1=hneg[:, 0:1],
                                                        scalar2=None, op0=ALU.add),
                 reads=["biasP", "hneg"], writes=["biasP0"])
            S.barrier()
            s0.close()

            S.stage(2)
            w_view = w_in.ap().rearrange("(dc p) n -> p dc n", p=128)
            wblocks = [("GK", C_GK, 512), ("GV", C_GV, 1024), ("GLR", C_GLR, 16), ("Q", C_Q, 512),
                       ("KV", C_K, 256), ("GATT", C_GATT, 512), ("GQ", C_GQ, 512), ("GGLA", C_GGLA, 1024)]
            wl = [0]
            for name, c0, cw in wblocks:
                for dc in range(8):
                    slot = wl[0] % 2
                    wl[0] += 1
                    S.dma("sp", f"w{slot}",
                          lambda slot=slot, dc=dc, c0=c0, cw=cw: nc.sync.dma_start(
                              out=stg[:, slot, 0:cw], in_=w_view[:, dc, c0:c0 + cw]),
                          writes=[f"stg{slot}"])
                    ceng = ("pool", "dve", "act")[wl[0] % 3]

                    def fcw(slot=slot, dc=dc, c0=c0, cw=cw, ceng=ceng):
                        o = W1[:, dc, c0:c0 + cw]
                        i = stg[:, slot, 0:cw]
                        if ceng == "pool":
                            return nc.gpsimd.tensor_copy(out=o, in_=i)
                        if ceng == "dve":
                            return nc.vector.tensor_copy(out=o, in_=i)
                        return nc.scalar.copy(out=o, in_=i)
                    S.op(ceng, fcw, reads=[f"stg{slot}"], writes=[f"W_{name}"])

            nLl = sbt(s1, "nLl", [128, 64], F32)
            edl = sbt(s1, "edl", [128, 64], F32)
            S32 = sbt(s1, "S32", [128, 1028], F32)
            Sb = sbt(s1, "Sb", [128, 1024], BF16)

            S.stage(3)
            with ExitStack() as sA:
                xA = sbt(sA, "xA", [128, 2, 8, 512], BF16)
                ltA = sbt(sA, "ltA", [128, 2048], F32)
                LtA = sbt(sA, "LtA", [128, 2048], F32)
                kendTA = sbt(sA, "kendTA", [128, 2048], BF16)
                kendA = sbt(sA, "kendA", [128, 4, 512], BF16)
                gvA = sbt(sA, "gvA", [128, 4, 1024], BF16)
                glrbA = sbt(sA, "glrbA", [16, 512], BF16)
                xpv_view = xpvT.ap().rearrange("(dc p) t -> p dc t", p=128)
                NSTEP = 7 * TPC // 512
                xl = [0]
                ops1 = pstride(onesb)
                nps_ = pstride(nLl)

                def load_xA(sidx, par):
                    for dcp in range(4):
                        slot = xl[0] % 2
                        xl[0] += 1
                        S.dma("sp", f"x{slot}", lambda slot=slot, dcp=dcp: nc.sync.dma_start(
                            out=xstg[:, slot, :].rearrange("p (a t) -> p a t", a=2),
                            in_=xpv_view[:, dcp * 2:dcp * 2 + 2, sidx * 512:(sidx + 1) * 512]),
                            writes=[f"xstg{slot}"])
                        S.op("pool", lambda slot=slot, dcp=dcp: nc.gpsimd.tensor_copy(
                            out=xA[:, par, dcp * 2:dcp * 2 + 2, :],
                            in_=xstg[:, slot, :].rearrange("p (a t) -> p a t", a=2)),
                            reads=[f"xstg{slot}"], writes=[f"xA{par}"])

                def phaseA_step(sidx, par):
                    xb = xA[:, par]
                    xkey = f"xA{par}"
                    bk, bkk = nextbank()

                    def fglr():
                        last = None
                        for dc in range(8):
                            last = nc.tensor.matmul(bk[0:16, :], lhsT=W1[:, dc, C_GLR:C_GLR + 16], rhs=xb[:, dc, :],
                                                    start=(dc == 0), stop=(dc == 7))
                        return last
                    S.op("pe", fglr, reads=["W_GLR", xkey], writes=[bkk])
                    S.op("dve", lambda: nc.vector.tensor_copy(out=glrbA[:], in_=bk[0:16, :]), reads=[bkk],
                         writes=["glrbA"])
                    for h in range(4):
                        bkh, bkhk = nextbank()
                        S.op("pe", lambda bkh=bkh, h=h: nc.tensor.matmul(
                            bkh[:, :], lhsT=wupb[:, h * 128:(h + 1) * 128], rhs=glrbA[:, :], start=True, stop=True),
                            reads=["wupb", "glrbA"], writes=[bkhk])
                        S.op("act", lambda bkh=bkh, h=h: nc.scalar.activation(
                            out=ltA[:, h * 512:(h + 1) * 512], in_=bkh[:, :], func=AF.Exp, bias=negb[:, h:h + 1],
                            scale=-1.0), reads=[bkhk, "negb"], writes=["ltA"])
                    S.op("act", lambda: nc.scalar.activation(out=ltA[:], in_=ltA[:], func=AF.Ln, bias=c32[:, 0:1],
                                                             scale=1.0), reads=["ltA", "c32a"], writes=["ltA"])

                    for i in range(4):
                        for nb in range(2):
                            bkv, bkvk = nextbank()

                            def fgv(bkv=bkv, nb=nb, i=i):
                                last = None
                                for dc in range(8):
                                    last = nc.tensor.matmul(bkv[:, :], lhsT=xb[:, dc, i * 128:(i + 1) * 128],
                                                            rhs=W1[:, dc, C_GV + nb * 512: C_GV + (nb + 1) * 512],
                                                            start=(dc == 0), stop=(dc == 7))
                                return last
                            S.op("pe", fgv, reads=["W_GV", xkey], writes=[bkvk])
                            S.op("act", lambda bkv=bkv, nb=nb, i=i: nc.scalar.copy(
                                out=gvA[:, i, nb * 512:(nb + 1) * 512], in_=bkv[:, :]), reads=[bkvk], writes=["gvA"])
                    def fscan():
                        last = None
                        for h in range(4):
                            last = nc.vector.tensor_tensor_scan(
                                out=LtA[:, h * 512:(h + 1) * 512], data0=bass.AP(onesb, 0, [[ops1, 128], [0, 512]]),
                                data1=ltA[:, h * 512:(h + 1) * 512], initial=0.0, op0=ALU.mult, op1=ALU.add)
                        return last
                    S.op("dve", fscan, reads=["ltA", "onesb"], writes=["LtA"])
                    lpsA = pstride(LtA)
                    S.op("dve", lambda: nc.vector.tensor_scalar(
                        out=nLl[:, 0:4], in0=bass.AP(LtA, 511, [[lpsA, 128], [512, 4]]), scalar1=-1.0 / 16.0,
                        scalar2=None, op0=ALU.mult), reads=["LtA"], writes=["nLl"])
                    S.op("dve", lambda: nc.vector.scalar_tensor_tensor(
                        out=ltA[:].rearrange("p (g c) -> p g c", g=4), in0=LtA[:].rearrange("p (g c) -> p g c", g=4),
                        scalar=1.0 / 16.0, in1=bass.AP(nLl, 0, [[nps_, 128], [1, 4], [0, 512]]),
                        op0=ALU.mult, op1=ALU.add), reads=["LtA", "nLl"], writes=["ltA"])
                    S.op("act", lambda: nc.scalar.activation(out=ltA[:], in_=ltA[:], func=AF.Exp), reads=["ltA"],
                         writes=["ltA"])
                    S.op("act", lambda: nc.scalar.activation(out=edl[:, 0:4], in_=nLl[:, 0:4], func=AF.Exp),
                         reads=["nLl"], writes=["edl"])
                    for h in range(4):
                        bkh, bkhk = nextbank()

                        def fgk(bkh=bkh, h=h):
                            last = None
                            for dc in range(8):
                                last = nc.tensor.matmul(bkh[:, :], lhsT=W1[:, dc, C_GK + h * 128: C_GK + (h + 1) * 128],
                                                        rhs=xb[:, dc, :], start=(dc == 0), stop=(dc == 7))
                            return last
                        S.op("pe", fgk, reads=["W_GK", xkey], writes=[bkhk])
                        S.op("dve", lambda bkh=bkh, h=h: nc.vector.tensor_tensor(
                            out=kendTA[:, h * 512:(h + 1) * 512], in0=bkh[:, :], in1=ltA[:, h * 512:(h + 1) * 512],
                            op=ALU.mult), reads=[bkhk, "ltA"], writes=["kendTA"])
                    for pr in range(2):
                        def ftr(pr=pr):
                            last = None
                            for ii in range(2):
                                i = pr * 2 + ii
                                for h in range(4):
                                    last = nc.tensor.transpose(
                                        bankT[:, ii * 512 + h * 128: ii * 512 + (h + 1) * 128],
                                        kendTA[:, h * 512 + i * 128: h * 512 + (i + 1) * 128], ident)
                            return last
                        S.op("pe", ftr, reads=["kendTA", "cstb"], writes=["bkT"])
                        S.op("act", lambda pr=pr: nc.scalar.copy(
                            out=kendA[:, pr * 2:pr * 2 + 2, :].rearrange("p a c -> p (a c)"), in_=bankT[:, 0:1024]),
                            reads=["bkT"], writes=["kendA"])
                    b2 = [nextbank(), nextbank()]

                    def fds():
                        last = None
                        for h in range(4):
                            bk_ = b2[h // 2][0]
                            for i in range(4):
                                last = nc.tensor.matmul(bk_[:, (h % 2) * 256:(h % 2) * 256 + 256],
                                                        lhsT=kendA[:, i, h * 128:(h + 1) * 128],
                                                        rhs=gvA[:, i, h * 256:(h + 1) * 256], start=(i == 0),
                                                        stop=(i == 3))
                        return last
                    S.op("pe", fds, reads=["kendA", "gvA"], writes=[b2[0][1], b2[1][1]])

                    def fu():
                        last = None
                        for h in range(4):
                            bk_ = b2[h // 2][0]
                            last = nc.vector.scalar_tensor_tensor(
                                out=S32[:, h * 256:(h + 1) * 256], in0=S32[:, h * 256:(h + 1) * 256],
                                scalar=edl[:, h:h + 1], in1=bk_[:, (h % 2) * 256:(h % 2) * 256 + 256],
                                op0=ALU.mult, op1=ALU.add)
                        return last
                    S.op("dve", fu, reads=["S32", "edl", b2[0][1], b2[1][1]], writes=["S32"])

                S.op("pool", lambda: nc.gpsimd.memset(S32[:], 0.0), writes=["S32"])
                load_xA(0, 0)
                for sidx in range(NSTEP):
                    par = sidx % 2
                    if sidx + 1 < NSTEP:
                        load_xA(sidx + 1, 1 - par)
                    phaseA_step(sidx, par)
                S.op("pool", lambda: nc.gpsimd.tensor_copy(out=Sb[:], in_=S32[:, 0:1024]), reads=["S32"],
                     writes=["Sb"])
                S.barrier()

            xTb = sbt(s1, "xTb", [128, 2, 8, 128], BF16)
            qT = sbt(s1, "qT", [128, 2, 2, 512], BF16)
            gattS = sbt(s1, "gattS", [128, 2, 512], BF16)
            Kr = sbt(s1, "Kr", [128, 2, 128], BF16)
            Vr = sbt(s1, "Vr", [128, 2, 128], BF16)
            PT = sbt(s1, "PT", [128, 4, 512], BF16)
            lnd = sbt(s1, "lnd", [128, 512], F32)
            actatt = sbt(s1, "actatt", [128, 2, 512], BF16)
            glrb = sbt(s1, "glrb", [16, 128], BF16)
            lt = sbt(s1, "lt", [128, 512], F32)
            Lt = sbt(s1, "Lt", [128, 512], F32)
            ekd = sbt(s1, "ekd", [128, 512], F32)
            eb = sbt(s1, "eb", [128, 512], F32)
            enb = sbt(s1, "enb", [128, 512], F32)
            qt = sbt(s1, "qt", [128, 2, 512], BF16)
            kt = sbt(s1, "kt", [128, 2, 512], BF16)
            kendT = sbt(s1, "kendT", [128, 512], BF16)
            kend = sbt(s1, "kend", [128, 2, 512], BF16)
            gv = sbt(s1, "gv", [128, 2, 1024], BF16)
            gglaS = sbt(s1, "gglaS", [128, 2, 1024], BF16)
            ATm = sbt(s1, "ATm", [128, 512], BF16)
            osq = sbt(s1, "osq", [128, 1024], BF16)
            rstd = sbt(s1, "rstd", [128, 512], F32)
            rg2 = sbt(s1, "rg2", [128, 1024], BF16)
            actgla = sbt(s1, "actgla", [128, 2, 1024], BF16)
            kvtok = sbt(s1, "kvtok", [128, 256], F32)

            S.op("pool", lambda: nc.gpsimd.memset(qT[:].rearrange("p a h c -> p (a h c)"), 0.0),
                 writes=["qT0", "qT1"])

            def load_x(src_view, par, key):
                S.dma("sp", f"x{par}", lambda: nc.sync.dma_start(
                    out=xstg[:, par, :].rearrange("p (dc t) -> p dc t", dc=8), in_=src_view),
                    writes=[f"xstg{par}"])
                S.op("pool", lambda: nc.gpsimd.tensor_copy(
                    out=xTb[:, par].rearrange("p dc t -> p (dc t)"), in_=xstg[:, par, :]),
                    reads=[f"xstg{par}"], writes=[key])

            xT_view = xT.ap().rearrange("(dc p) t -> p dc t", p=128)
            xsT_view = xsT.ap().rearrange("(dc p) t -> p dc t", p=128)

            def decay_prep(par, xkey, sample, need_q):
                xb = xTb[:, par]
                bk, bkk = nextbank()
                proj_fm(bk, 0, W1, "W_GLR", C_GLR, xb, xkey, 128, 1, bkk, msize=16)
                S.op("dve", lambda: nc.vector.tensor_copy(out=glrb[:], in_=bk[0:16, 0:128]), reads=[bkk],
                     writes=["glrb"])
                bk2, bkk2 = nextbank()

                def fn():
                    last = None
                    for h in range(4):
                        last = nc.tensor.matmul(bk2[:, h * 128:(h + 1) * 128], lhsT=wupb[:, h * 128:(h + 1) * 128],
                                                rhs=glrb[:, :], start=True, stop=True)
                    return last
                S.op("pe", fn, reads=["wupb", "glrb"], writes=[bkk2])

                def fe():
                    last = None
                    for h in range(4):
                        last = nc.scalar.activation(out=lt[:, h * 128:(h + 1) * 128], in_=bk2[:, h * 128:(h + 1) * 128],
                                                    func=AF.Exp, bias=negb[:, h:h + 1], scale=-1.0)
                    return last
                S.op("act", fe, reads=[bkk2, "negb"], writes=["lt"])
                S.op("act", lambda: nc.scalar.activation(out=lt[:], in_=lt[:], func=AF.Ln, bias=c32[:, 0:1],
                                                         scale=1.0), reads=["lt", "c32a"], writes=["lt"])
                rst = RST_S if sample else RST_P
                S.op("dve", lambda: nc.vector.tensor_tensor_scan(
                    out=Lt[:], data0=cstb[:, rst:rst + 512], data1=lt[:], initial=0.0, op0=ALU.mult, op1=ALU.add),
                    reads=["lt", "cstb"], writes=["Lt"])
                ng = 64 if sample else 4
                cl = 512 // ng
                lps = pstride(Lt)
                S.op("dve", lambda: nc.vector.tensor_scalar(
                    out=nLl[:, 0:ng], in0=bass.AP(Lt, cl - 1, [[lps, 128], [cl, ng]]), scalar1=-1.0 / 16.0,
                    scalar2=None, op0=ALU.mult), reads=["Lt"], writes=["nLl"])
                nps = pstride(nLl)
                S.op("dve", lambda: nc.vector.scalar_tensor_tensor(
                    out=ekd[:].rearrange("p (g c) -> p g c", g=ng), in0=Lt[:].rearrange("p (g c) -> p g c", g=ng),
                    scalar=1.0 / 16.0, in1=bass.AP(nLl, 0, [[nps, 128], [1, ng], [0, cl]]),
                    op0=ALU.mult, op1=ALU.add), reads=["Lt", "nLl"], writes=["ekd"])
                S.op("act", lambda: nc.scalar.activation(out=ekd[:], in_=ekd[:], func=AF.Exp), reads=["ekd"],
                     writes=["ekd"])
                S.op("act", lambda: nc.scalar.activation(out=edl[:, 0:ng], in_=nLl[:, 0:ng], func=AF.Exp),
                     reads=["nLl"], writes=["edl"])
                if need_q:
                    S.op("act", lambda: nc.scalar.activation(out=eb[:], in_=Lt[:], func=AF.Exp, scale=-1.0 / 16.0),
                         reads=["Lt"], writes=["eb"])
                    S.op("act", lambda: nc.scalar.activation(out=enb[:], in_=Lt[:], func=AF.Exp, scale=1.0 / 16.0),
                         reads=["Lt"], writes=["enb"])

            def gk_gv(par, xkey, need_q, defer_tr=False):
                xb = xTb[:, par]
                bk, bkk = nextbank()
                proj_fm(bk, 0, W1, "W_GK", C_GK, xb, xkey, 128, 4, bkk)
                S.op("dve", lambda: nc.vector.tensor_tensor(out=kendT[:], in0=bk[:, :], in1=ekd[:], op=ALU.mult),
                     reads=[bkk, "ekd"], writes=["kendT"])
                if need_q:
                    S.op("dve", lambda: nc.vector.tensor_tensor(out=kt[:, par, :], in0=bk[:, :], in1=enb[:],
                                                                op=ALU.mult),
                         reads=[bkk, "enb"], writes=[f"kt{par}"])

                def ftr():
                    last = None
                    for h in range(4):
                        last = nc.tensor.transpose(bankT[:, h * 128:(h + 1) * 128], kendT[:, h * 128:(h + 1) * 128],
                                                   ident)
                    return last

                def do_tr():
                    S.op("pe", ftr, reads=["kendT", "cstb"], writes=["bkT"])
                    S.op("act", lambda: nc.scalar.copy(out=kend[:, par, :], in_=bankT[:, 0:512]), reads=["bkT"],
                         writes=[f"kend{par}"])
                if not defer_tr:
                    do_tr()
                for nb in range(2):
                    bkv, bkvk = nextbank()

                    def fgv(bkv=bkv, nb=nb):
                        last = None
                        for dc in range(8):
                            last = nc.tensor.matmul(bkv[:, :], lhsT=xb[:, dc, :],
                                                    rhs=W1[:, dc, C_GV + nb * 512: C_GV + (nb + 1) * 512],
                                                    start=(dc == 0), stop=(dc == 7))
                        return last
                    S.op("pe", fgv, reads=["W_GV", xkey], writes=[bkvk])
                    S.op("act", lambda bkv=bkv, nb=nb: nc.scalar.copy(out=gv[:, par, nb * 512:(nb + 1) * 512],
                                                                      in_=bkv[:, :]),
                         reads=[bkvk], writes=[f"gv{par}"])
                return do_tr if defer_tr else None

            def state_update(par, st32, stkey, kend_ap, kendkey, edl_col0, edl_step, out32=None, outkey=None):
                if out32 is None:
                    out32, outkey = st32, stkey
                b2 = [nextbank(), nextbank()]

                def fn():
                    last = None
                    for h in range(4):
                        bk_ = b2[h // 2][0]
                        last = nc.tensor.matmul(bk_[:, (h % 2) * 256:(h % 2) * 256 + 256],
                                                lhsT=kend_ap[:, h * 128:(h + 1) * 128],
                                                rhs=gv[:, par, h * 256:(h + 1) * 256], start=True, stop=True)
                    return last
                S.op("pe", fn, reads=[kendkey, f"gv{par}"], writes=[b2[0][1], b2[1][1]])

                def fu():
                    last = None
                    for h in range(4):
                        bk_ = b2[h // 2][0]
                        c = edl_col0 + h * edl_step
                        last = nc.vector.scalar_tensor_tensor(
                            out=out32[:, h * 256:(h + 1) * 256], in0=st32[:, h * 256:(h + 1) * 256],
                            scalar=edl[:, c:c + 1], in1=bk_[:, (h % 2) * 256:(h % 2) * 256 + 256],
                            op0=ALU.mult, op1=ALU.add)
                    return last
                S.op("dve", fu, reads=[stkey, "edl", b2[0][1], b2[1][1]], writes=[outkey])

            def attn_proj(par, xkey, ntv_key):
                xb = xTb[:, par]
                bk, bkk = nextbank()
                proj_fm(bk, 0, W1, "W_Q", C_Q, xb, xkey, 128, 4, bkk)
                def fq():
                    nc.scalar.activation(out=qT[0:64, par, 0, :], in_=bk[0:64, :], func=AF.Copy, scale=0.125)
                    return nc.scalar.activation(out=qT[64:128, par, 1, :], in_=bk[64:128, :], func=AF.Copy, scale=0.125)
                S.op("act", fq, reads=[bkk], writes=[f"qT{par}"])
                bk2, bkk2 = nextbank()
                proj_fm(bk2, 0, W1, "W_GATT", C_GATT, xb, xkey, 128, 4, bkk2)
                S.op("act", lambda: nc.scalar.activation(out=gattS[:, par, :], in_=bk2[:, :], func=AF.Silu),
                     reads=[bkk2], writes=[f"gattS{par}"])

            def attn_finish(par, bo, bok, bd, bdk, scr_idx, perm=False):
                def fl():
                    last = None
                    for g in range(4):
                        src = bd[:, g * 128:(g + 1) * 128]
                        dstv = lnd[:, g * 128:(g + 1) * 128]
                        if perm:
                            src = bd[:, :].rearrange("p (b g t) -> p g b t", b=16, g=4)[:, g, :, :]
                            dstv = dstv.rearrange("p (b t) -> p b t", b=16)
                        last = nc.scalar.activation(out=dstv, in_=src, func=AF.Ln, bias=esk[:, g:g + 1], scale=1.0)
                    return last
                S.op("act", fl, reads=[bdk, "esk"], writes=["lnd"])
                S.op("act", lambda: nc.scalar.activation(out=lnd[:], in_=lnd[:], func=AF.Exp, scale=-1.0),
                     reads=["lnd"], writes=["lnd"])
                S.op("pool", lambda: nc.gpsimd.tensor_tensor(out=lnd[:], in0=lnd[:], in1=gattS[:, par, :],
                                                             op=ALU.mult),
                     reads=["lnd", f"gattS{par}"], writes=["lnd"])
                def fm():
                    if not perm:
                        return nc.vector.tensor_tensor(out=actatt[:, par, :], in0=bo[:, :], in1=lnd[:], op=ALU.mult)
                    last = None
                    for g in range(4):
                        last = nc.vector.tensor_tensor(
                            out=actatt[:, par, g * 128:(g + 1) * 128].rearrange("p (b t) -> p b t", b=16),
                            in0=bo[:, :].rearrange("p (b g t) -> p g b t", b=16, g=4)[:, g, :, :],
                            in1=lnd[:, g * 128:(g + 1) * 128].rearrange("p (b t) -> p b t", b=16), op=ALU.mult)
                    return last
                S.op("dve", fm, reads=[bok, "lnd"], writes=[f"actatt{par}"])
                if not os.environ.get("KSKIP_SA"):
                  S.dma("sp", f"sa{par}", lambda: nc.sync.dma_start(out=att_scr.ap()[scr_idx], in_=actatt[:, par, :]),
                      reads=[f"actatt{par}"], writes=[f"att_scr{scr_idx}"])

            def gla_proj(par, xkey):
                xb = xTb[:, par]
                bk, bkk = nextbank()
                proj_fm(bk, 0, W1, "W_GQ", C_GQ, xb, xkey, 128, 4, bkk)
                S.op("dve", lambda: nc.vector.scalar_tensor_tensor(out=qt[:, par, :], in0=bk[:, :], scalar=HK_SCALE,
                                                                   in1=eb[:], op0=ALU.mult, op1=ALU.mult),
                     reads=[bkk, "eb"], writes=[f"qt{par}"])
                for half in range(2):
                    bk2, bkk2 = nextbank()
                    proj_fm(bk2, 0, W1, "W_GGLA", C_GGLA + half * 512, xb, xkey, 128, 4, bkk2)
                    S.op("act", lambda bk2=bk2, half=half: nc.scalar.activation(
                        out=gglaS[:, par, half * 512:(half + 1) * 512], in_=bk2[:, :], func=AF.Silu),
                        reads=[bkk2], writes=[f"gglaS{par}"])

                def fw():
                    last = None
                    v = gglaS[:, par, :].rearrange("p (h c t) -> p h c t", h=4, c=2)
                    for c in range(2):
                        last = nc.gpsimd.tensor_scalar(out=v[:, :, c, :], in0=v[:, :, c, :], scalar1=gnw[:, c:c + 1],
                                                       scalar2=None, op0=ALU.mult)
                    return last
                S.op("pool", fw, reads=[f"gglaS{par}", "gnw"], writes=[f"gglaS{par}"])

            def gla_AT(par, cm):
                bk, bkk = nextbank()

                def fn():
                    last = None
                    for h in range(4):
                        last = nc.tensor.matmul(bk[:, h * 128:(h + 1) * 128], lhsT=kt[:, par, h * 128:(h + 1) * 128],
                                                rhs=qt[:, par, h * 128:(h + 1) * 128], start=True, stop=True)
                    return last
                S.op("pe", fn, reads=[f"kt{par}", f"qt{par}"], writes=[bkk])
                S.op("dve", lambda: nc.vector.tensor_tensor(out=ATm[:], in0=bk[:, :], in1=cstb[:, cm:cm + 512],
                                                            op=ALU.mult), reads=[bkk, "cstb"], writes=["ATm"])

            def gla_finish(par, bo2, scr_idx):
                for i in range(2):
                    S.op("act", lambda i=i: nc.scalar.activation(out=osq[:, i * 512:(i + 1) * 512], in_=bo2[i][0][:, :],
                                                                 func=AF.Square),
                         reads=[bo2[i][1]], writes=["osq"])
                if scr_idx == NTILE: S.stage(6.1)
                bs, bsk = nextbank()
                ops_ = pstride(osq)

                def fs():
                    last = None
                    for c in range(2):
                        last = nc.tensor.matmul(bs[:, :], lhsT=onesb[:, :],
                                                rhs=bass.AP(osq, c * 128, [[ops_, 128], [256, 4], [1, 128]]),
                                                start=(c == 0), stop=(c == 1))
                    return last
                S.op("pe", fs, reads=["osq", "onesb"], writes=[bsk])
                if scr_idx == NTILE: S.stage(6.2)
                S.op("act", lambda: nc.scalar.activation(out=rstd[:], in_=bs[:, :], func=AF.Ln, bias=c32[:, 1:2],
                                                         scale=1.0 / 256.0), reads=[bsk, "c32b"], writes=["rstd"])
                S.op("act", lambda: nc.scalar.activation(out=rstd[:], in_=rstd[:], func=AF.Exp, scale=-0.5),
                     reads=["rstd"], writes=["rstd"])

                if scr_idx == NTILE: S.stage(6.3)

                def fr():
                    last = None
                    gvw = gglaS[:, par, :].rearrange("p (h c t) -> p h c t", h=4, c=2)
                    rv = rg2[:].rearrange("p (h c t) -> p h c t", h=4, c=2)
                    for c in range(2):
                        last = nc.gpsimd.tensor_tensor(out=rv[:, :, c, :], in0=gvw[:, :, c, :],
                                                       in1=rstd[:].rearrange("p (h t) -> p h t", h=4), op=ALU.mult)
                    return last
                S.op("pool", fr, reads=[f"gglaS{par}", "rstd"], writes=["rg2"])
                if scr_idx == NTILE: S.stage(6.4)
                for i in range(2):
                    S.op("dve", lambda i=i: nc.vector.tensor_tensor(
                        out=actgla[:, par, i * 512:(i + 1) * 512], in0=bo2[i][0][:, :],
                        in1=rg2[:, i * 512:(i + 1) * 512], op=ALU.mult),
                        reads=[bo2[i][1], "rg2"], writes=[f"actgla{par}"])
                if scr_idx == NTILE: S.stage(6.5)
                S.dma(("pool" if os.environ.get("KPOOLQ") else "sp"), (f"sa{par}" if os.environ.get("KCH") else f"sg{par}"), lambda: [
                    (nc.gpsimd if os.environ.get("KPOOLQ") else nc.sync).dma_start(out=(gla_scr_s.ap()[0] if scr_idx == NTILE else gla_scr.ap()[scr_idx])[:, i * 512:(i + 1) * 512],
                                      in_=(gv if os.environ.get("KSRC") else actgla)[:, par, i * 512:(i + 1) * 512]) for i in range(2)],
                      reads=([] if os.environ.get("KNODEP") else [f"actgla{par}"]), writes=[f"gla_scr{scr_idx}"], n=2)

            S.stage(5)
            with ExitStack() as s2:
                KcT = sbt(s2, "KcT", [128, 16, 128], BF16)
                Vc = sbt(s2, "Vc", [128, 16, 128], BF16)
                S0f = sbt(s2, "S0f", [128, 2, 1028], F32)
                S0b = sbt(s2, "S0b", [128, 2, 1024], BF16)
                kendm = sbt(s2, "kendm", [128, 2, 512], BF16)
                Snew = sbt(s2, "Snew", [128, 2, 1024], F32)

                s_par = 0
                load_x(xsT_view, s_par, "xTb0")
                s_xkey = "xTb0"
                s_xb = xTb[:, s_par]
                if os.environ.get("KSKIP_CACHE"):
                    S.enabled = False
                S.dma("sp", "w0", lambda: nc.sync.dma_start(
                    out=stg[:].rearrange("p a (b j) -> p (a b) j", j=128), in_=ckT.ap().rearrange("b p j -> p b j")),
                    writes=["stg0", "stg1"])
                S.op("pool", lambda: nc.gpsimd.tensor_copy(out=KcT[:].rearrange("p b j -> p (b j)"),
                                                           in_=stg[:].rearrange("p a c -> p (a c)")),
                     reads=["stg0", "stg1"], writes=["KcT"])
                S.dma("sp", "w0", lambda: nc.sync.dma_start(
                    out=stg[:].rearrange("p a (b j) -> p (a b) j", j=128), in_=cv.ap().rearrange("b j c -> j b c")),
                    writes=["stg0", "stg1"])
                S.op("pool", lambda: nc.gpsimd.tensor_copy(out=Vc[:].rearrange("p b j -> p (b j)"),
                                                           in_=stg[:].rearrange("p a c -> p (a c)")),
                     reads=["stg0", "stg1"], writes=["Vc"])
                if os.environ.get("KSKIP_CACHE"):
                    S.enabled = True
                if not os.environ.get("KSKIP_CP"):
                  S.dma("sp", "cpk", lambda: nc.sync.dma_start(out=nks_o.ap()[:, 0:120, :], in_=ck.ap()[:, 8:128, :]),
                      writes=["nks_a"])
                if not os.environ.get("KSKIP_CP"):
                  S.dma("sp", "cpv", lambda: nc.sync.dma_start(out=nvs_o.ap()[:, 0:120, :], in_=cv.ap()[:, 8:128, :]),
                      writes=["nvs_a"])

                S.stage(5.2)
                attn_proj(s_par, s_xkey, None)
                s_bk, s_bkk = nextbank()
                proj_fm(s_bk, 0, W1, "W_KV", C_K, s_xb, s_xkey, 128, 1, s_bkk)
                S.op("act", lambda: nc.scalar.copy(out=Kr[:, 0, :], in_=s_bk[:, 0:128]), reads=[s_bkk], writes=["Kr0"])
                s_bk2, s_bkk2 = nextbank()

                def fkv():
                    last = None
                    for dc in range(8):
                        last = nc.tensor.matmul(s_bk2[:, 0:256], lhsT=s_xb[:, dc, :], rhs=W1[:, dc, C_K:C_K + 256],
                                                start=(dc == 0), stop=(dc == 7))
                    return last
                S.op("pe", fkv, reads=["W_KV", s_xkey], writes=[s_bkk2])
                S.op("act", lambda: nc.scalar.copy(out=kvtok[:], in_=s_bk2[:, 0:256]), reads=[s_bkk2], writes=["kvtok"])
                S.op("dve", lambda: nc.vector.tensor_copy(out=Vr[:, 0, :], in_=s_bk2[:, 128:256]), reads=[s_bkk2],
                     writes=["Vr0"])

                def st_newkv():
                    l = []
                    for b in range(NSEQ):
                        l.append(nc.sync.dma_start(out=nks_o.ap()[b, 120:128, :], in_=kvtok[b * 8:(b + 1) * 8, 0:128]))
                        l.append(nc.sync.dma_start(out=nvs_o.ap()[b, 120:128, :], in_=kvtok[b * 8:(b + 1) * 8, 128:256]))
                    return l
                S.dma("sp", "nkv", st_newkv, reads=["kvtok"], writes=["nks_b"], n=2 * NSEQ)

                S.stage(5.3)
                scb = []
                for h in range(2):
                    hp = slice(h * 64, (h + 1) * 64)
                    bkn, bknk = nextbank()

                    def fsn(bkn=bkn, h=h, hp=hp):
                        nc.tensor.matmul(bkn[:, :], lhsT=ident, rhs=biasN[:, h, :], start=True, stop=False)
                        return nc.tensor.matmul(bkn[:, :], lhsT=Kr[:, 0, :],
                                                rhs=qT[:, s_par, h, :].rearrange("p (g b t) -> p b g t", g=4, b=16),
                                                start=False, stop=True)
                    S.op("pe", fsn, reads=["cstb", "biasN", "Kr0", f"qT{s_par}"], writes=[bknk])
                    S.op("act", lambda bkn=bkn, h=h: nc.scalar.activation(out=PT[:, h * 2 + 1, :], in_=bkn[:, :],
                                                                          func=AF.Exp),
                         reads=[bknk], writes=[f"PT{h * 2 + 1}"])
                    bkc, bkck = nextbank()

                    def fsc(bkc=bkc, h=h, hp=hp):
                        last = nc.tensor.matmul(bkc[:, :], lhsT=ident, rhs=biasC[:, h, :], start=True, stop=False)
                        qv = qT[:, s_par, h, :].rearrange("p (g b t) -> p g b t", g=4, b=16)
                        for b in range(NSEQ):
                            last = nc.tensor.matmul(bkc[:, b * 32:(b + 1) * 32], lhsT=KcT[:, b, :], rhs=qv[:, :, b, :],
                                                    start=False, stop=(b == NSEQ - 1))
                        return last
                    S.op("pe", fsc, reads=["cstb", "biasC", "KcT", f"qT{s_par}"], writes=[bkck])
                    S.op("act", lambda bkc=bkc, h=h: nc.scalar.activation(out=PT[:, h * 2, :], in_=bkc[:, :],
                                                                          func=AF.Exp),
                         reads=[bkck], writes=[f"PT{h * 2}"])
                s_bo, s_bok = nextbank()
                s_bd, s_bdk = nextbank()

                def fpv_s(dst, use_v):
                    last = None
                    for h in range(2):
                        hp = slice(h * 64, (h + 1) * 64)
                        lw = Vr[:, 0, hp] if use_v else onesb[:, 0:64]
                        last = nc.tensor.matmul(dst[hp, :], lhsT=lw, rhs=PT[:, h * 2 + 1, :], start=True, stop=False)
                        for b in range(NSEQ):
                            lw = Vc[:, b, hp] if use_v else onesb[:, 0:64]
                            last = nc.tensor.matmul(dst[hp, b * 32:(b + 1) * 32], lhsT=lw,
                                                    rhs=PT[:, h * 2, b * 32:(b + 1) * 32], start=False,
                                                    stop=(b == NSEQ - 1))
                    return last
                S.op("pe", lambda: fpv_s(s_bo, True), reads=["Vr0", "Vc", "PT0", "PT1", "PT2", "PT3"], writes=[s_bok])
                S.op("pe", lambda: fpv_s(s_bd, False), reads=["onesb", "PT0", "PT1", "PT2", "PT3"], writes=[s_bdk])
                attn_finish(s_par, s_bo, s_bok, s_bd, s_bdk, NTILE, perm=True)

                S.stage(5.4)
                decay_prep(s_par, s_xkey, True, True)
                gk_gv(s_par, s_xkey, True)
                gla_proj(s_par, s_xkey)
                gla_AT(s_par, CM_S)
                S.stage(5.5)
                s_bo2 = [nextbank(), nextbank()]
                reserved.update((s_bo2[0][1], s_bo2[1][1]))

                def fzero():
                    last = None
                    for i in range(2):
                        last = nc.tensor.matmul(s_bo2[i][0][:, :], lhsT=zerob[:, :], rhs=cstb[:, CM_P:CM_P + 512],
                                                start=True, stop=False)
                    for h in range(4):
                        for c in range(2):
                            blk = (h * 2 + c) % 4
                            last = nc.tensor.matmul(s_bo2[h // 2][0][:, blk * 128:(blk + 1) * 128],
                                                    lhsT=gv[:, s_par, h * 256 + c * 128: h * 256 + (c + 1) * 128],
                                                    rhs=ATm[:, h * 128:(h + 1) * 128], start=False, stop=False)
                    return last
                S.op("pe", fzero, reads=["zerob", "cstb", f"gv{s_par}", "ATm"], writes=[s_bo2[0][1], s_bo2[1][1]])
                sps = pstride(cstb)
                def ld_s0(b):
                    sl = b % 2
                    S.dma("sp", f"s0{sl}", lambda b=b, sl=sl: nc.sync.dma_start(
                        out=S0f[:, sl, 0:1024].rearrange("p (h v) -> p h v", h=4),
                        in_=st0.ap()[b].rearrange("h k v -> k h v")), writes=[f"S0f{sl}"])
                ld_s0(0)
                for b in range(NSEQ):
                    sl = b % 2
                    if b >= 1 and b + 1 < NSEQ:
                        pass
                    S.op("act", lambda sl=sl: nc.scalar.copy(out=S0b[:, sl, :], in_=S0f[:, sl, 0:1024]),
                         reads=[f"S0f{sl}"], writes=[f"S0b{sl}"])
                    if b + 1 < NSEQ:
                        ld_s0(b + 1)

                    def fin(b=b, sl=sl):
                        last = None
                        for h in range(4):
                            for c in range(2):
                                blk = (h * 2 + c) % 4
                                last = nc.tensor.matmul(
                                    s_bo2[h // 2][0][:, blk * 128 + b * 8: blk * 128 + (b + 1) * 8],
                                    lhsT=S0b[:, sl, h * 256 + c * 128: h * 256 + (c + 1) * 128],
                                    rhs=qt[:, s_par, h * 128 + b * 8: h * 128 + (b + 1) * 8], start=False,
                                    stop=(b == NSEQ - 1 and h % 2 == 1 and c == 1))
                        return last
                    S.op("pe", fin, reads=[f"S0b{sl}", f"qt{s_par}"], writes=[s_bo2[0][1], s_bo2[1][1]])
                    S.op("dve", lambda b=b, sl=sl: nc.vector.tensor_scalar(
                        out=kendm[:, sl, :], in0=kend[:, s_par, :], scalar1=cstb[:, SEQM + b:SEQM + b + 1], scalar2=None,
                        op0=ALU.mult), reads=[f"kend{s_par}", "cstb"], writes=[f"kendm{sl}"])
                    state_update(s_par, S0f[:, sl, :], f"S0f{sl}", kendm[:, sl, :], f"kendm{sl}", b, 16,
                                 out32=Snew[:, sl, :], outkey=f"Snew{sl}")
                    S.dma("sp", f"so{sl}", lambda b=b, sl=sl: nc.sync.dma_start(
                        out=nss_o.ap()[b].rearrange("h k v -> k h v"),
                        in_=Snew[:, sl, :].rearrange("p (h v) -> p h v", h=4)),
                        reads=[f"Snew{sl}"], writes=[f"nss{b}"])
                S.stage(5.6)
                reserved.clear()
                gla_finish(s_par, s_bo2, NTILE)

                S.barrier()

            S.stage(7)
            load_x(xT_view[:, :, 0:128], 1, "xTb1")

            def kv_proj(par, xkey, slot, last_tile):
                xb = xTb[:, par]
                bk, bkk = nextbank()
                proj_fm(bk, 0, W1, "W_KV", C_K, xb, xkey, 128, 1, bkk)
                S.op("act", lambda: nc.scalar.copy(out=Kr[:, slot, :], in_=bk[:, 0:128]), reads=[bkk],
                     writes=[f"Kr{slot}"])
                bk2, bkk2 = nextbank()

                def fkv():
                    last = None
                    for dc in range(8):
                        last = nc.tensor.matmul(bk2[:, 0:256], lhsT=xb[:, dc, :], rhs=W1[:, dc, C_K:C_K + 256],
                                                start=(dc == 0), stop=(dc == 7))
                    return last
                S.op("pe", fkv, reads=["W_KV", xkey], writes=[bkk2])
                S.op("dve", lambda: nc.vector.tensor_copy(out=Vr[:, slot, :], in_=bk2[:, 128:256]), reads=[bkk2],
                     writes=[f"Vr{slot}"])
                if last_tile and not os.environ.get("KSKIP_NKVP"):
                    S.op("dve", lambda: nc.vector.tensor_copy(out=kvtok[:], in_=bk2[:, 0:256]), reads=[bkk2],
                         writes=["kvtok"])
                    def st_pkv():
                        l = []
                        for b in range(16):
                            l.append(nc.sync.dma_start(out=nkp_o.ap()[b * 8:(b + 1) * 8, :],
                                                       in_=kvtok[b * 8:(b + 1) * 8, 0:128]))
                            l.append(nc.sync.dma_start(out=nvp_o.ap()[b * 8:(b + 1) * 8, :],
                                                       in_=kvtok[b * 8:(b + 1) * 8, 128:256]))
                        return l
                    if not os.environ.get("KSKIP_NKVP2"):
                        S.dma("sp", "nkvp", st_pkv, reads=["kvtok"], writes=["nkp"], n=32)

            kv_proj(1, "xTb1", 0, False)
            load_x(xT_view[:, :, 128:256], 0, "xTb0")
            for j in range(NTILE):
                par = j % 2
                xkey = f"xTb{par}"
                sp_, sc_ = j % 2, (j + 1) % 2
                if j + 1 < NTILE:
                    load_x(xT_view[:, :, 128 + (j + 1) * 128: 256 + (j + 1) * 128], 1 - par, f"xTb{1 - par}")
                if j == 0: S.stage(7.1)
                if j >= 1: S.stage(7.9 + j * 0.002)
                attn_proj(par, xkey, None)
                kv_proj(par, xkey, sc_, j == NTILE - 1)
                if j == 0: S.stage(7.2)
                for h in range(2):
                    hp = slice(h * 64, (h + 1) * 64)
                    for half, (slot, bt) in enumerate(((sp_, biasP0 if j == 0 else biasP), (sc_, biasQ))):
                        bks, bksk = nextbank()

                        def fsc(bks=bks, h=h, hp=hp, slot=slot, bt=bt, par=par):
                            nc.tensor.matmul(bks[:, :], lhsT=ident, rhs=bt[:, h, :], start=True, stop=False)
                            return nc.tensor.matmul(bks[:, :], lhsT=Kr[:, slot, :], rhs=qT[:, par, h, :], start=False,
                                                    stop=True)
                        S.op("pe", fsc, reads=["cstb", "biasP", "biasP0", "biasQ", f"Kr{slot}", f"qT{par}"], writes=[bksk])
                        S.op("act", lambda bks=bks, h=h, half=half: nc.scalar.activation(
                            out=PT[:, h * 2 + half, :], in_=bks[:, :], func=AF.Exp),
                            reads=[bksk], writes=[f"PT{h * 2 + half}"])
                if j == 0: S.stage(7.3)
                bo, bok = nextbank()
                bd, bdk = nextbank()

                def fpv(dst, use_v, sp_=sp_, sc_=sc_):
                    last = None
                    for h in range(2):
                        hp = slice(h * 64, (h + 1) * 64)
                        for half, slot in enumerate((sp_, sc_)):
                            lw = Vr[:, slot, hp] if use_v else onesb[:, 0:64]
                            last = nc.tensor.matmul(dst[hp, :], lhsT=lw, rhs=PT[:, h * 2 + half, :],
                                                    start=(half == 0), stop=(half == 1))
                    return last
                S.op("pe", lambda bo=bo, fpv=fpv: fpv(bo, True),
                     reads=[f"Vr{sp_}", f"Vr{sc_}", "PT0", "PT1", "PT2", "PT3"], writes=[bok])
                S.op("pe", lambda bd=bd, fpv=fpv: fpv(bd, False), reads=["onesb", "PT0", "PT1", "PT2", "PT3"],
                     writes=[bdk])
                if j == 0: S.stage(7.4)
                attn_finish(par, bo, bok, bd, bdk, j)

                if j == 0: S.stage(7.5)
                decay_prep(par, xkey, False, True)
                tr_later = gk_gv(par, xkey, True, defer_tr=True)
                gla_proj(par, xkey)
                tr_later()
                gla_AT(par, CM_P)
                if j == 0: S.stage(7.6)
                bo2 = [nextbank(), nextbank()]

                def fo(bo2=bo2, par=par):
                    last = None
                    for h in range(4):
                        for c in range(2):
                            blk = (h * 2 + c) % 4
                            dst = bo2[h // 2][0][:, blk * 128:(blk + 1) * 128]
                            nc.tensor.matmul(dst, lhsT=Sb[:, h * 256 + c * 128: h * 256 + (c + 1) * 128],
                                             rhs=qt[:, par, h * 128:(h + 1) * 128], start=True, stop=False)
                            last = nc.tensor.matmul(dst, lhsT=gv[:, par, h * 256 + c * 128: h * 256 + (c + 1) * 128],
                                                    rhs=ATm[:, h * 128:(h + 1) * 128], start=False, stop=True)
                    return last
                S.op("pe", fo, reads=["Sb", f"qt{par}", f"gv{par}", "ATm"], writes=[bo2[0][1], bo2[1][1]])
                if j == 0: S.stage(7.7)
                state_update(par, S32, "S32", kend[:, par, :], f"kend{par}", 0, 1)
                S.op("act", lambda: nc.scalar.copy(out=Sb[:], in_=S32[:, 0:1024]), reads=["S32"],
                     writes=["Sb"])
                if j == 0: S.stage(7.8)
                gla_finish(par, bo2, j)
            S.stage(7.95)
            S.dma("sp", "nsp", lambda: nc.sync.dma_start(
                out=nsp_o.ap().rearrange("h k v -> k h v"), in_=S32[:, 0:1024].rearrange("p (h v) -> p h v", h=4)),
                reads=["S32"], writes=["nsp"])
            S.barrier()

        S.stage(8)
        with ExitStack() as s3:
            Wr = sbt(s3, "Wr", [128, 8, 2048], BF16)
            Wpa = sbt(s3, "Wpa", [128, 4, 1024], BF16)
            Wpg = sbt(s3, "Wpg", [128, 8, 1024], BF16)
            Wo = sbt(s3, "Wo", [128, 8, 1024], BF16)
            stg2 = sbt(s3, "stg2", [128, 4, 1024], F32)
            lng = sbt(s3, "lng", [128, 1024], F32)
            lnb = sbt(s3, "lnb", [128, 1024], F32)
            x2 = sbt(s3, "x2", [128, 2, 8, 512], BF16)
            att2 = sbt(s3, "att2", [128, 2, 4, 512], BF16)
            gla2 = sbt(s3, "gla2", [128, 2, 4, 1024], BF16)
            sa = sbt(s3, "sa", [128, 2, 512], F32)
            sg = sbt(s3, "sg", [128, 2, 512], F32)
            merged = sbt(s3, "merged", [128, 2, 8, 512], BF16)
            xt2 = sbt(s3, "xt2", [128, 2, 1024], F32)
            bnst = sbt(s3, "bnst", [128, 2, 6], F32)
            bnag = sbt(s3, "bnag", [128, 8], F32)

            S.dma("sp", "lnp", lambda: [nc.sync.dma_start(out=lng[:], in_=bass.AP(ln_g, 0, [[0, 128], [1, 1024]])),
                                        nc.sync.dma_start(out=lnb[:], in_=bass.AP(ln_b, 0, [[0, 128], [1, 1024]]))],
                  writes=["lng", "lnb"], n=2)
            wl2 = [0]

            def load_w(src_view, ndc, ncols, dst, dkey, col0=0):
                for dc in range(ndc):
                    for cb in range(0, ncols, 1024):
                        cw = min(1024, ncols - cb)
                        slot = wl2[0] % 4
                        wl2[0] += 1
                        S.dma("sp", f"v{slot}", lambda slot=slot, dc=dc, cb=cb, cw=cw: nc.sync.dma_start(
                            out=stg2[:, slot, 0:cw], in_=src_view[:, dc, col0 + cb: col0 + cb + cw]),
                            writes=[f"stg2{slot}"])
                        eng = ("pool", "dve", "act")[wl2[0] % 3]

                        def fc(slot=slot, dc=dc, cb=cb, cw=cw, eng=eng):
                            o = dst[:, dc, cb:cb + cw]
                            i = stg2[:, slot, 0:cw]
                            if eng == "pool":
                                return nc.gpsimd.tensor_copy(out=o, in_=i)
                            if eng == "dve":
                                return nc.vector.tensor_copy(out=o, in_=i)
                            return nc.scalar.copy(out=o, in_=i)
                        S.op(eng, fc, reads=[f"stg2{slot}"], writes=[dkey])
            w_view = w_in.ap().rearrange("(dc p) n -> p dc n", p=128)
            load_w(w_view, 8, 2048, Wr, "Wr", col0=C_RATT)
            load_w(w_pa.ap().rearrange("(g p) n -> p g n", p=128), 4, 1024, Wpa, "Wpa")
            load_w(w_pg.ap().rearrange("(c p) n -> p c n", p=128), 8, 1024, Wpg, "Wpg")
            load_w(w_o.ap().rearrange("(c p) n -> p c n", p=128), 8, 1024, Wo, "Wo")

            xT_view = xT.ap().rearrange("(dc p) t -> p dc t", p=128)
            xsT_view = xsT.ap().rearrange("(dc p) t -> p dc t", p=128)
            groups = [(s, 4) for s in range(4)] + [(4, 1)]

            def load_group(gi):
                s, nt = groups[gi]
                par = gi % 2
                T = nt * 128
                for dcp in range(4):
                    slot = wl2[0] % 4
                    wl2[0] += 1
                    src = (xT_view[:, dcp * 2:dcp * 2 + 2, 128 + s * 512: 128 + s * 512 + T] if nt == 4
                           else xsT_view[:, dcp * 2:dcp * 2 + 2, :])
                    S.dma("sp", f"v{slot}", lambda slot=slot, src=src, T=T: nc.sync.dma_start(
                        out=stg2[:, slot, 0:2 * T].rearrange("p (a t) -> p a t", a=2), in_=src),
                        writes=[f"stg2{slot}"])
                    S.op("pool", lambda slot=slot, dcp=dcp, T=T, par=par: nc.gpsimd.tensor_copy(
                        out=x2[:, par, dcp * 2:dcp * 2 + 2, 0:T],
                        in_=stg2[:, slot, 0:2 * T].rearrange("p (a t) -> p a t", a=2)),
                        reads=[f"stg2{slot}"], writes=[f"x2{par}"])
                t0 = s * 4
                S.dma("sp", f"la{par}", lambda: [
                    nc.sync.dma_start(out=att2[:, par, 0:nt, :], in_=att_scr.ap()[t0:t0 + nt].rearrange("n p c -> p n c")),
                    nc.sync.dma_start(out=gla2[:, par, 0:nt, :], in_=(gla_scr_s.ap() if t0 == NTILE else gla_scr.ap()[t0:t0 + nt]).rearrange("n p c -> p n c"))],
                    reads=[f"att_scr{t0 + i}" for i in range(nt)] + [f"gla_scr{t0 + i}" for i in range(nt)],
                    writes=[f"att2{par}", f"gla2{par}"], n=2)

            a2s = pstride(att2)
            g2s = pstride(gla2)
            load_group(0)
            xtl = [0]
            for gi, (s, nt) in enumerate(groups):
                par = gi % 2
                T = nt * 128
                if gi + 1 < len(groups):
                    load_group(gi + 1)
                for dc in range(8):
                    dsl = slice(dc * 128, (dc + 1) * 128)
                    bha, bhak = nextbank()

                    def fha(bha=bha, dsl=dsl, par=par, nt=nt, T=T):
                        last = None
                        for g in range(4):
                            rhs = bass.AP(att2, par * 2048 + g * 128, [[a2s, 128], [512, nt], [1, 128]])
                            last = nc.tensor.matmul(bha[:, 0:T],
                                                    lhsT=Wpa[:, g, dsl], rhs=rhs, start=(g == 0), stop=(g == 3))
                        return last
                    S.op("pe", fha, reads=["Wpa", f"att2{par}"], writes=[bhak])
                    bhg, bhgk = nextbank()

                    def fhg(bhg=bhg, dsl=dsl, par=par, nt=nt, T=T):
                        last = None
                        for c in range(8):
                            rhs = bass.AP(gla2, par * 4096 + c * 128, [[g2s, 128], [1024, nt], [1, 128]])
                            last = nc.tensor.matmul(bhg[:, 0:T],
                                                    lhsT=Wpg[:, c, dsl], rhs=rhs, start=(c == 0), stop=(c == 7))
                        return last
                    S.op("pe", fhg, reads=["Wpg", f"gla2{par}"], writes=[bhgk])
                    gates = []
                    for which, dstt in ((0, sa), (1, sg)):
                        bkr, bkrk = nextbank()

                        def fr(bkr=bkr, which=which, dc=dc, par=par, T=T):
                            last = None
                            for d2 in range(8):
                                last = nc.tensor.matmul(
                                    bkr[:, 0:T], lhsT=Wr[:, d2, which * 1024 + dc * 128: which * 1024 + (dc + 1) * 128],
                                    rhs=x2[:, par, d2, 0:T], start=(d2 == 0), stop=(d2 == 7))
                            return last
                        S.op("pe", fr, reads=["Wr", f"x2{par}"], writes=[bkrk])
                        dpar = dc % 2
                        S.op("act", lambda bkr=bkr, dstt=dstt, dpar=dpar, T=T: nc.scalar.activation(
                            out=dstt[:, dpar, 0:T], in_=bkr[:, 0:T], func=AF.Sigmoid),
                            reads=[bkrk], writes=[f"{'sa' if which == 0 else 'sg'}{dpar}"])
                    dpar = dc % 2
                    S.op("dve", lambda bha=bha, dpar=dpar, T=T: nc.vector.tensor_tensor(
                        out=sa[:, dpar, 0:T], in0=bha[:, 0:T], in1=sa[:, dpar, 0:T], op=ALU.mult),
                        reads=[bhak, f"sa{dpar}"], writes=[f"sa{dpar}"])
                    S.op("dve", lambda bhg=bhg, dpar=dpar, T=T: nc.vector.tensor_tensor(
                        out=sg[:, dpar, 0:T], in0=bhg[:, 0:T], in1=sg[:, dpar, 0:T], op=ALU.mult),
                        reads=[bhgk, f"sg{dpar}"], writes=[f"sg{dpar}"])
                    S.op("pool", lambda dpar=dpar, dc=dc, par=par, T=T: nc.gpsimd.tensor_tensor(
                        out=merged[:, par, dc, 0:T], in0=sa[:, dpar, 0:T], in1=sg[:, dpar, 0:T], op=ALU.add),
                        reads=[f"sa{dpar}", f"sg{dpar}"], writes=[f"merged{par}"])
                for i in range(nt):
                    xp = xtl[0] % 2
                    xtl[0] += 1
                    src = xtok.ap()[(s * 4 + i) * 128:(s * 4 + i + 1) * 128, :] if nt == 4 else xstok.ap()
                    dsto = y_o.ap()[(s * 4 + i) * 128:(s * 4 + i + 1) * 128, :] if nt == 4 else ys_o.ap()
                    S.dma("sp", f"xt{xp}", lambda xp=xp, src=src: nc.sync.dma_start(out=xt2[:, xp, :], in_=src),
                          writes=[f"xt2{xp}"])
                    by = [nextbank(), nextbank()]

                    def fy(by=by, par=par, i=i):
                        last = None
                        for nb in range(2):
                            for dc in range(8):
                                last = nc.tensor.matmul(by[nb][0][:, :], lhsT=merged[:, par, dc, i * 128:(i + 1) * 128],
                                                        rhs=Wo[:, dc, nb * 512:(nb + 1) * 512], start=(dc == 0),
                                                        stop=(dc == 7))
                        return last
                    S.op("pe", fy, reads=["Wo", f"merged{par}"], writes=[by[0][1], by[1][1]])
                    for nb in range(2):
                        S.op("dve", lambda xp=xp, nb=nb, by=by: nc.vector.scalar_tensor_tensor(
                            out=xt2[:, xp, nb * 512:(nb + 1) * 512], in0=xt2[:, xp, nb * 512:(nb + 1) * 512],
                            scalar=DN_ALPHA, in1=by[nb][0][:, :], op0=ALU.mult, op1=ALU.add),
                            reads=[f"xt2{xp}", by[nb][1]], writes=[f"xt2{xp}"])

                    def fbn(xp=xp):
                        nc.vector.bn_stats(out=bnst[:, 0, :], in_=xt2[:, xp, 0:512])
                        return nc.vector.bn_stats(out=bnst[:, 1, :], in_=xt2[:, xp, 512:1024])
                    S.op("dve", fbn, reads=[f"xt2{xp}"], writes=["bnst"])
                    S.op("dve", lambda: nc.vector.bn_aggr(out=bnag[:, 0:2], in_=bnst[:].rearrange("p a b -> p (a b)")),
                         reads=["bnst"], writes=["bnag"])
                    S.op("act", lambda: nc.scalar.activation(out=bnag[:, 2:3], in_=bnag[:, 1:2], func=AF.Ln,
                                                             bias=c32[:, 1:2], scale=1.0),
                         reads=["bnag", "c32b"], writes=["bnag"])
                    S.op("act", lambda: nc.scalar.activation(out=bnag[:, 2:3], in_=bnag[:, 2:3], func=AF.Exp,
                                                             scale=-0.5), reads=["bnag"], writes=["bnag"])
                    S.op("dve", lambda: nc.vector.scalar_tensor_tensor(
                        out=bnag[:, 3:4], in0=bnag[:, 0:1], scalar=-1.0, in1=bnag[:, 2:3], op0=ALU.mult,
                        op1=ALU.mult), reads=["bnag"], writes=["bnag"])
                    S.op("act", lambda xp=xp: nc.scalar.activation(out=xt2[:, xp, :], in_=xt2[:, xp, :],
                                                                   func=AF.Identity, bias=bnag[:, 3:4],
                                                                   scale=bnag[:, 2:3]),
                         reads=[f"xt2{xp}", "bnag"], writes=[f"xt2{xp}"])
                    S.op("pool", lambda xp=xp: nc.gpsimd.tensor_tensor(out=xt2[:, xp, :], in0=xt2[:, xp, :],
                                                                       in1=lng[:], op=ALU.mult),
                         reads=[f"xt2{xp}", "lng"], writes=[f"xt2{xp}"])
                    S.op("pool", lambda xp=xp: nc.gpsimd.tensor_tensor(out=xt2[:, xp, :], in0=xt2[:, xp, :],
                                                                       in1=lnb[:], op=ALU.add),
                         reads=[f"xt2{xp}", "lnb"], writes=[f"xt2{xp}"])
                    S.dma("sp", f"yo{xp}", lambda xp=xp, dsto=dsto: nc.sync.dma_start(out=dsto, in_=xt2[:, xp, :]),
                          reads=[f"xt2{xp}"], writes=[f"yout{xp}"])
            S.barrier()
        S.finalize()
    return nc


def _t5_bucket(dist):
    n = np.maximum(dist, 0)
    nf = np.maximum(n, 1).astype(np.float32)
    large = 16 + (np.log(nf / np.float32(16)) / np.float32(math.log(128 / 16)) * np.float32(16)).astype(np.int32)
    large = np.minimum(large, 31)
    return np.where(n < 16, n, large)


def _constants():
    cb = np.zeros((128, NCB), np.float32)
    p = np.arange(128)[:, None]
    t = np.arange(128)[None, :]
    cmp_ = (p <= t).astype(np.float32)
    cb[:, CM_P:CM_P + 512] = np.tile(cmp_, (1, 4))
    same = (p // 8 == t // 8)
    cms = (same & (p <= t)).astype(np.float32)
    cb[:, CM_S:CM_S + 512] = np.tile(cms, (1, 4))
    cb[:, BLKNEG:BLKNEG + 128] = np.where(same, 0.0, NEG)
    cb[:, SEQM:SEQM + 16] = (p // 8 == np.arange(16)[None, :]).astype(np.float32)
    c512 = np.arange(512)[None, :]
    cb[:, RST_P:RST_P + 512] = np.broadcast_to((c512 % 128 != 0).astype(np.float32), (128, 512))
    cb[:, RST_S:RST_S + 512] = np.broadcast_to((c512 % 8 != 0).astype(np.float32), (128, 512))
    cb[:, IDENT:IDENT + 128] = np.eye(128, dtype=np.float32)
    cb = cb.astype(ml_dtypes.bfloat16)
    n = np.arange(384)
    dist = 255 - n
    valid = (dist >= 0) & (dist <= 128) & (n <= 382)
    bucket = _t5_bucket(dist)
    ohr = np.zeros((32, 384), np.float32)
    ohr[bucket[valid], n[valid]] = 1.0
    negr = np.broadcast_to(np.where(valid, 0.0, NEG).astype(np.float32), (8, 384)).copy()
    return cb, ohr, negr


_NC_CACHE = {}


def make_in_maps(x_prompt, x_sample, cache_k, cache_v, state_gla, rel_bias_table, w_in, w_gk_up, b_gk, attn_sink,
                 gla_norm_w, w_pa, w_pg, w_o, ln_g, ln_b):
    f = lambda a: np.ascontiguousarray(np.asarray(a, dtype=np.float32))
    x_prompt, x_sample, cache_k, cache_v, state_gla = map(f, (x_prompt, x_sample, cache_k, cache_v, state_gla))
    w_in0 = f(w_in)[0]
    pq = np.array([(h * 4 + g) * 64 + d for g in range(4) for h in range(2) for d in range(64)])
    perm = np.arange(N_IN)
    perm[C_Q:C_Q + 512] = C_Q + pq
    perm[C_GATT:C_GATT + 512] = C_GATT + pq
    w_in_p = np.ascontiguousarray(w_in0[:, perm])
    w_pa_p = np.ascontiguousarray(f(w_pa)[0][pq, :])
    cb, ohr, negr = _constants()

    xp = x_prompt[0]
    xpT = np.ascontiguousarray(xp.T)
    in_maps = []
    for c in range(NCORES):
        xT_c = np.zeros((D, TPC + 128), np.float32)
        xT_c[:, 128:] = xpT[:, c * TPC:(c + 1) * TPC]
        if c > 0:
            xT_c[:, 0:128] = xpT[:, c * TPC - 128:c * TPC]
        xs_c = x_sample[c * NSEQ:(c + 1) * NSEQ].reshape(128, D)
        ck_c = cache_k[0, c * NSEQ:(c + 1) * NSEQ].reshape(NSEQ, 128, 128)
        cv_c = cache_v[0, c * NSEQ:(c + 1) * NSEQ].reshape(NSEQ, 128, 128)
        mj = np.zeros((128, 8), np.float32)
        xpv = np.zeros((D, 7 * TPC), np.float32)
        if c > 0:
            xpv[:, (7 - c) * TPC:] = xpT[:, 0:c * TPC]
        negr_c = negr.copy()
        hneg = np.full((128, 1), NEG if c == 0 else 0.0, np.float32)
        if c == 0:
            pass
        in_maps.append({
            "xT": xT_c, "xpvT": xpv, "xtok": np.ascontiguousarray(xp[c * TPC:(c + 1) * TPC]),
            "xsT": np.ascontiguousarray(xs_c.T), "xstok": np.ascontiguousarray(xs_c),
            "ckT": np.ascontiguousarray(ck_c.transpose(0, 2, 1)), "ck": np.ascontiguousarray(ck_c),
            "cv": np.ascontiguousarray(cv_c), "st0": np.ascontiguousarray(state_gla[0, c * NSEQ:(c + 1) * NSEQ]),
            "table": f(rel_bias_table), "w_in": w_in_p, "w_up": f(w_gk_up)[0], "b_gk": f(b_gk)[0],
            "sink": f(attn_sink)[0], "gnw": f(gla_norm_w)[0], "w_pa": w_pa_p, "w_pg": f(w_pg)[0], "w_o": f(w_o)[0],
            "ln_g": f(ln_g)[0], "ln_b": f(ln_b)[0], "cstb": cb, "ohr": ohr, "negr": negr_c, "mj": mj, "hneg": hneg,
        })
    return in_maps


def kernel(**inputs):
    in_maps = make_in_maps(**inputs)
    if "nc" not in _NC_CACHE:
        _NC_CACHE["nc"] = build_nc()
    res = run_bass_kernel_spmd(_NC_CACHE["nc"], in_maps, core_ids=list(range(NCORES)))
    R = res.results
    y_prompt = np.concatenate([R[c]["y"] for c in range(NCORES)], 0).reshape(1, 16384, D)
    y_sample = np.concatenate([R[c]["ys"] for c in range(NCORES)], 0).reshape(128, 8, D)
    nkp = R[NCORES - 1]["nkp"].reshape(1, 1, 128, 2, 64)
    nvp = R[NCORES - 1]["nvp"].reshape(1, 1, 128, 2, 64)
    nsp = R[NCORES - 1]["nsp"].reshape(1, 1, 4, 128, 256)
    nks = np.concatenate([R[c]["nks"] for c in range(NCORES)], 0).reshape(1, 128, 128, 2, 64)
    nvs = np.concatenate([R[c]["nvs"] for c in range(NCORES)], 0).reshape(1, 128, 128, 2, 64)
    nss = np.concatenate([R[c]["nss"] for c in range(NCORES)], 0).reshape(1, 128, 4, 128, 256)
    return tuple(np.asarray(a, dtype=np.float32) for a in (y_prompt, y_sample, nkp, nvp, nsp, nks, nvs, nss))
```
